# Optimizing a Trainium2 kernel written in Bass

```python
import jax, jax.numpy as jnp
from jax import lax
import numpy as np

D_MODEL = 1024
BATCH = 8
SEQ = 4096
DEPTH = 2
DEC_BATCH = 4
DEC_SEQ = 8192
PAST_LEN = 128

HEAD_DIM = 64
N_ATTN_HEADS = 8
ATTN_WIDTH = N_ATTN_HEADS * HEAD_DIM
DILATED_PATTERNS = ((128, 1), (512, 4), (2048, 16))
ROT_DIM = HEAD_DIM // 4
ROPE_THETA = 500000.0
N_DN_HEADS = 4
DN_HEAD_DIM = 128
DN_WIDTH = N_DN_HEADS * DN_HEAD_DIM
CHUNK = 64
CONV_K = 5
D_FF = 2816
IN_WIDTH = 3 * ATTN_WIDTH + 4 * DN_WIDTH + 4 * N_DN_HEADS
MIX_WIDTH = ATTN_WIDTH + DN_WIDTH
N_MOD = 9
EPS = 1e-6

kernel_name = "hybrid_dilated_attn_gated_deltanet_encoder"


def _rmsnorm(x, w):
    xf = x.astype(jnp.float32)
    y = xf * lax.rsqrt(jnp.mean(xf * xf, axis=-1, keepdims=True) + EPS)
    return (y * w.astype(jnp.float32)).astype(x.dtype)


def _l2norm(x):
    return x * lax.rsqrt(jnp.sum(x * x, axis=-1, keepdims=True) + EPS)


def _partial_rope(x):
    S = x.shape[1]
    half = ROT_DIM // 2
    inv = ROPE_THETA ** (-jnp.arange(half, dtype=jnp.float32) / half)
    ang = jnp.arange(S, dtype=jnp.float32)[:, None] * inv[None, :]
    cos = jnp.cos(ang)[None, :, None, :]
    sin = jnp.sin(ang)[None, :, None, :]
    xr = x[..., :ROT_DIM].astype(jnp.float32)
    x1, x2 = xr[..., :half], xr[..., half:]
    rot = jnp.concatenate([x1 * cos - x2 * sin, x2 * cos + x1 * sin], axis=-1).astype(x.dtype)
    return jnp.concatenate([rot, x[..., ROT_DIM:]], axis=-1)


def _dilated_window_attn(q, k, v, dilation, radius):
    B, S, H, Dh = q.shape
    L = S // dilation
    G = B * dilation

    def to_sub(t):
        return t.reshape(B, L, dilation, H, Dh).transpose(0, 2, 1, 3, 4).reshape(G, L, H, Dh)

    qs, ks, vs = to_sub(q), to_sub(k), to_sub(v)
    nb = -(-L // radius)
    Lp = nb * radius
    qb = jnp.pad(qs, ((0, 0), (0, Lp - L), (0, 0), (0, 0))).reshape(G, nb, radius, H, Dh)

    def kblocks(t):
        tp = jnp.pad(t, ((0, 0), (radius, Lp - L + radius), (0, 0), (0, 0))).reshape(G, nb + 2, radius, H, Dh)
        return jnp.concatenate([tp[:, :-2], tp[:, 1:-1], tp[:, 2:]], axis=2)

    kb, vb = kblocks(ks), kblocks(vs)
    qpos = np.arange(nb)[:, None] * radius + np.arange(radius)[None, :]
    kpos = np.arange(nb)[:, None] * radius - radius + np.arange(3 * radius)[None, :]
    valid = ((kpos[:, None, :] >= 0) & (kpos[:, None, :] < L)
             & (np.abs(qpos[:, :, None] - kpos[:, None, :]) <= radius))

    scores = jnp.einsum('gnqhd,gnkhd->gnhqk', qb, kb).astype(jnp.float32) * (Dh ** -0.5)
    scores = jnp.where(valid[None, :, None], scores, -1e30)
    m = jnp.max(scores, axis=-1, keepdims=True)
    p = jnp.exp(scores - m)
    denom = jnp.sum(p, axis=-1, keepdims=True)
    o = jnp.einsum('gnhqk,gnkhd->gnhqd', p, vb.astype(jnp.float32)) / denom
    lse = (m + jnp.log(denom))[..., 0]

    o = o.transpose(0, 1, 3, 2, 4).reshape(G, Lp, H, Dh)[:, :L]
    o = o.reshape(B, dilation, L, H, Dh).transpose(0, 2, 1, 3, 4).reshape(B, S, H, Dh)
    lse = lse.transpose(0, 1, 3, 2).reshape(G, Lp, H)[:, :L]
    lse = lse.reshape(B, dilation, L, H).transpose(0, 2, 1, 3).reshape(B, S, H)
    return o, lse


def _short_conv(x, w):
    S = x.shape[1]
    pad = CONV_K // 2
    xp = jnp.pad(x, ((0, 0), (pad, pad), (0, 0)))
    y = xp[:, 0:S] * w[0]
    for j in range(1, CONV_K):
        y = y + xp[:, j:j + S] * w[j]
    return jax.nn.silu(y)


def _gated_delta_chunked(q, k, v, g, beta):
    B, S, H, Dk = q.shape
    Dv = v.shape[-1]
    N = S // CHUNK

    def chunks(t):
        return t.reshape(B, N, CHUNK, H, t.shape[-1]).transpose(1, 0, 3, 2, 4)

    qc, kc, vc = chunks(q), chunks(k), chunks(v)
    gc = jnp.cumsum(g.reshape(B, N, CHUNK, H).transpose(1, 0, 3, 2), axis=-1)
    bc = beta.reshape(B, N, CHUNK, H).transpose(1, 0, 3, 2)
    lower = np.tril(np.ones((CHUNK, CHUNK), dtype=bool))
    strict = np.tril(np.ones((CHUNK, CHUNK), dtype=bool), -1)
    decay = jnp.exp(jnp.where(lower, gc[..., :, None] - gc[..., None, :], -jnp.inf))
    kbeta = kc * bc[..., None]
    A = jnp.where(strict, jnp.einsum('nbhid,nbhjd->nbhij', kbeta, kc) * decay, 0.0)
    eye = jnp.eye(CHUNK, dtype=jnp.float32)
    T = lax.linalg.triangular_solve(eye + A, jnp.broadcast_to(eye, A.shape), left_side=True, lower=True)
    u = T @ (vc * bc[..., None])
    w = T @ (kbeta * jnp.exp(gc)[..., None])
    qk = jnp.einsum('nbhid,nbhjd->nbhij', qc, kc) * decay

    def step(state, xs):
        q_i, k_i, u_i, w_i, qk_i, g_i = xs
        v_new = u_i - w_i @ state
        o = (q_i * jnp.exp(g_i)[..., None]) @ state + qk_i @ v_new
        g_last = g_i[..., -1]
        k_dec = k_i * jnp.exp(g_last[..., None] - g_i)[..., None]
        state = state * jnp.exp(g_last)[..., None, None] + jnp.einsum('bhcd,bhce->bhde', k_dec, v_new)
        return state, o

    state0 = jnp.zeros((B, H, Dk, Dv), jnp.float32)
    _, o = lax.scan(step, state0, (qc, kc, u, w, qk, gc))
    return o.transpose(1, 0, 3, 2, 4).reshape(B, S, H, Dv)


def _mixer(h, w_in, conv_w, a_log, dt_bias, dn_norm, w_out):
    B, S, _ = h.shape
    proj = h @ w_in
    aq, ak, av, dn_qkv, z, a, b = jnp.split(
        proj, [ATTN_WIDTH, 2 * ATTN_WIDTH, 3 * ATTN_WIDTH, 3 * ATTN_WIDTH + 3 * DN_WIDTH,
               3 * ATTN_WIDTH + 4 * DN_WIDTH, 3 * ATTN_WIDTH + 4 * DN_WIDTH + 2 * N_DN_HEADS], axis=-1)

    aq = _partial_rope(aq.reshape(B, S, N_ATTN_HEADS, HEAD_DIM))
    ak = _partial_rope(ak.reshape(B, S, N_ATTN_HEADS, HEAD_DIM))
    av = av.reshape(B, S, N_ATTN_HEADS, HEAD_DIM)
    outs, lses = [], []
    for window, dil in DILATED_PATTERNS:
        o_g, lse_g = _dilated_window_attn(aq, ak, av, dil, window // (2 * dil))
        outs.append(o_g)
        lses.append(lse_g)
    wts = jax.nn.softmax(jnp.stack(lses, axis=0), axis=0)
    attn = jnp.sum(wts[..., None] * jnp.stack(outs, axis=0), axis=0)
    attn = attn.reshape(B, S, ATTN_WIDTH).astype(h.dtype)

    qkv = _short_conv(dn_qkv, conv_w).astype(jnp.float32)
    dq, dk, dv = jnp.split(qkv, 3, axis=-1)
    dq = _l2norm(dq.reshape(B, S, N_DN_HEADS, DN_HEAD_DIM)) * (DN_HEAD_DIM ** -0.5)
    dk = _l2norm(dk.reshape(B, S, N_DN_HEADS, DN_HEAD_DIM))
    dv = dv.reshape(B, S, N_DN_HEADS, DN_HEAD_DIM)
    a = a.astype(jnp.float32).reshape(B, S, 2, N_DN_HEADS)
    g = -jnp.exp(a_log.astype(jnp.float32)) * jax.nn.softplus(a + dt_bias.astype(jnp.float32))
    beta = jax.nn.sigmoid(b.astype(jnp.float32).reshape(B, S, 2, N_DN_HEADS))
    o_f = _gated_delta_chunked(dq, dk, dv, g[:, :, 0], beta[:, :, 0])
    flip = lambda t: jnp.flip(t, axis=1)
    o_b = flip(_gated_delta_chunked(flip(dq), flip(dk), flip(dv), flip(g[:, :, 1]), flip(beta[:, :, 1])))
    o = o_f + o_b
    zf = z.astype(jnp.float32).reshape(B, S, N_DN_HEADS, DN_HEAD_DIM)
    o = o * lax.rsqrt(jnp.mean(o * o, axis=-1, keepdims=True) + EPS) * dn_norm.astype(jnp.float32) * jax.nn.silu(zf)
    dn = o.reshape(B, S, DN_WIDTH).astype(h.dtype)

    return jnp.concatenate([attn, dn], axis=-1) @ w_out


def _swiglu(h, w_gate, w_up, w_down):
    return (jax.nn.silu(h @ w_gate) * (h @ w_up)) @ w_down


def _trunk(x, c, ada_w, ada_b, norm_ffn1, ffn1_w_gate, ffn1_w_up, ffn1_w_down, norm_mix, w_in, conv_w,
           a_log, dt_bias, dn_norm, w_out, norm_ffn2, ffn2_w_gate, ffn2_w_up, ffn2_w_down, norm_final):
    sc = jax.nn.silu(c)
    for l in range(DEPTH):
        mod = (sc @ ada_w[l] + ada_b[l])[:, None, :]
        sh1, sc1, gt1, sh2, sc2, gt2, sh3, sc3, gt3 = jnp.split(mod, N_MOD, axis=-1)
        h = _rmsnorm(x, norm_ffn1[l]) * (1 + sc1) + sh1
        x = x + 0.5 * gt1 * _swiglu(h, ffn1_w_gate[l], ffn1_w_up[l], ffn1_w_down[l])
        h = _rmsnorm(x, norm_mix[l]) * (1 + sc2) + sh2
        x = x + gt2 * _mixer(h, w_in[l], conv_w[l], a_log[l], dt_bias[l], dn_norm[l], w_out[l])
        h = _rmsnorm(x, norm_ffn2[l]) * (1 + sc3) + sh3
        x = x + 0.5 * gt3 * _swiglu(h, ffn2_w_gate[l], ffn2_w_up[l], ffn2_w_down[l])
    return _rmsnorm(x, norm_final)


def setup_inputs(seed: int = 0) -> dict:
    key = jax.random.key(seed)
    ks = jax.random.split(key, 24)
    f32 = jnp.float32
    D, F = D_MODEL, D_FF
    nrm = lambda k, shape, scale: jax.random.normal(k, shape, f32) * scale
    gain = lambda k, shape: 1.0 + 0.02 * jax.random.normal(k, shape, f32)
    dt = jnp.exp(jax.random.uniform(ks[13], (DEPTH, 2, N_DN_HEADS), f32, np.log(1e-3), np.log(1e-1)))
    return {
        "x_prompt": nrm(ks[0], (BATCH, SEQ, D), 1.0),
        "x_sample": nrm(ks[1], (DEC_BATCH, DEC_SEQ, D), 1.0),
        "c_prompt": nrm(ks[2], (BATCH, D), 1.0),
        "c_sample": nrm(ks[3], (DEC_BATCH, D), 1.0),
        "ada_w": nrm(ks[4], (DEPTH, D, N_MOD * D), D ** -0.5),
        "ada_b": nrm(ks[5], (DEPTH, N_MOD * D), 0.02),
        "norm_ffn1": gain(ks[6], (DEPTH, D)),
        "ffn1_w_gate": nrm(ks[7], (DEPTH, D, F), D ** -0.5),
        "ffn1_w_up": nrm(ks[8], (DEPTH, D, F), D ** -0.5),
        "ffn1_w_down": nrm(ks[9], (DEPTH, F, D), F ** -0.5),
        "norm_mix": gain(ks[10], (DEPTH, D)),
        "w_in": nrm(ks[11], (DEPTH, D, IN_WIDTH), D ** -0.5),
        "conv_w": nrm(ks[12], (DEPTH, CONV_K, 3 * DN_WIDTH), CONV_K ** -0.5),
        "a_log": jnp.log(jax.random.uniform(ks[14], (DEPTH, 2, N_DN_HEADS), f32, 1.0, 16.0)),
        "dt_bias": dt + jnp.log(-jnp.expm1(-dt)),
        "dn_norm": gain(ks[15], (DEPTH, DN_HEAD_DIM)),
        "w_out": nrm(ks[16], (DEPTH, MIX_WIDTH, D), MIX_WIDTH ** -0.5),
        "norm_ffn2": gain(ks[17], (DEPTH, D)),
        "ffn2_w_gate": nrm(ks[18], (DEPTH, D, F), D ** -0.5),
        "ffn2_w_up": nrm(ks[19], (DEPTH, D, F), D ** -0.5),
        "ffn2_w_down": nrm(ks[20], (DEPTH, F, D), F ** -0.5),
        "norm_final": gain(ks[21], (D,)),
    }


def reference(x_prompt, x_sample, c_prompt, c_sample, ada_w, ada_b, norm_ffn1, ffn1_w_gate, ffn1_w_up,
              ffn1_w_down, norm_mix, w_in, conv_w, a_log, dt_bias, dn_norm, w_out, norm_ffn2, ffn2_w_gate,
              ffn2_w_up, ffn2_w_down, norm_final):
    y_prompt = _trunk(x_prompt, c_prompt, ada_w, ada_b, norm_ffn1, ffn1_w_gate, ffn1_w_up, ffn1_w_down,
                      norm_mix, w_in, conv_w, a_log, dt_bias, dn_norm, w_out, norm_ffn2, ffn2_w_gate,
                      ffn2_w_up, ffn2_w_down, norm_final)
    y_sample = _trunk(x_sample, c_sample, ada_w, ada_b, norm_ffn1, ffn1_w_gate, ffn1_w_up, ffn1_w_down,
                      norm_mix, w_in, conv_w, a_log, dt_bias, dn_norm, w_out, norm_ffn2, ffn2_w_gate,
                      ffn2_w_up, ffn2_w_down, norm_final)
    return (y_prompt, y_sample)
```

```python
import numpy as np
from contextlib import ExitStack
import concourse.bass as bass
import concourse.mybir as mybir
from concourse.bass_utils import run_bass_kernel_spmd

F32 = mybir.dt.float32
BF16 = mybir.dt.bfloat16
AF = mybir.ActivationFunctionType
ALU = mybir.AluOpType
AX = mybir.AxisListType

D = 1024
FF = 2816
NFC = FF // 128
NH = 8
HD = 64
NDH = 4
DK = 128
INW = 3600
EPS = 1e-6
ROPE_THETA = 500000.0
TT = 512
NEG = -30000.0
F32R = mybir.dt.float32r
DN_R = "none"


class Prog:
    CENG = ("pe", "act", "dve", "pool")

    def __init__(self, nc, es):
        self.nc = nc
        self.es = es
        self.ops = []
        self.last_w = {}
        self.readers = {}
        self.flushed = 0
        self.sems = {}
        self.cnt = {}
        self.seen = {}
        self.nblock = 0

    def _sem(self, key):
        if key not in self.sems:
            self.sems[key] = self.es.enter_context(self.nc.semaphore("s%d" % len(self.sems)))
            self.cnt[key] = 0
        return self.sems[key]

    def add(self, eng, fn, r=(), w=(), slot=None, ndma=1):
        idx = len(self.ops)
        hard, war = set(), set()
        for k in r:
            if k in self.last_w:
                hard.add(self.last_w[k])
        for k in w:
            if k in self.last_w:
                hard.add(self.last_w[k])
            for i in self.readers.get(k, ()):
                war.add(i)
        for k in w:
            self.last_w[k] = idx
            self.readers[k] = []
        for k in r:
            if k not in w:
                self.readers.setdefault(k, []).append(idx)
        deps = set()
        isdma = slot is not None
        for d in hard:
            od = self.ops[d]
            if od["eng"] == eng and eng == "pe" and not isdma:
                continue
            deps.add(d)
        for d in war:
            od = self.ops[d]
            if od["eng"] == eng and eng == "pe" and not isdma and od["slot"] is None:
                continue
            deps.add(d)
        deps.discard(idx)
        deps = {d for d in deps if d >= self.flushed}
        for d in deps:
            self.ops[d]["users"] = True
        self.ops.append(dict(eng=eng, fn=fn, deps=deps, slot=slot, ndma=ndma, users=False, sig=None))
        return idx

    def flush(self):
        nc = self.nc
        ops = self.ops[self.flushed:]
        slot_map = {}
        for op in ops:
            if op["slot"] is not None:
                sk = (op["eng"], op["slot"])
                if sk not in slot_map:
                    slot_map[sk] = sum(1 for k2 in slot_map if k2[0] == op["eng"])
                key = ("dmap", op["eng"], slot_map[sk])
                self._sem(key)
                self.cnt[key] += 16 * op["ndma"]
                op["sig"] = (key, self.cnt[key])
            elif op["users"]:
                key = ("eng", op["eng"])
                self._sem(key)
                self.cnt[key] += 1
                op["sig"] = (key, self.cnt[key])
        engs = {"pe": "tensor", "act": "scalar", "dve": "vector", "pool": "gpsimd", "sp": "sync"}
        with nc.Block() as block:
            for eng, attr in engs.items():
                mine = [op for op in ops if op["eng"] == eng]

                def body(e, mine=mine, eng=eng):
                    seen = self.seen.setdefault(eng, {})
                    for op in mine:
                        need = {}
                        for d in op["deps"]:
                            sg = self.ops[d]["sig"]
                            assert sg is not None
                            if need.get(sg[0], 0) < sg[1]:
                                need[sg[0]] = sg[1]
                        for k, v in need.items():
                            if seen.get(k, 0) < v:
                                e.wait_ge(self.sems[k], v)
                                seen[k] = v
                        res = op["fn"](e)
                        if op["slot"] is not None:
                            assert res is not None and len(res) == op["ndma"], (len(res), op["ndma"])
                            for ins in res:
                                ins.then_inc(self.sems[op["sig"][0]], 16)
                        elif op["sig"] is not None:
                            ins = res[-1] if isinstance(res, (list, tuple)) else res
                            ins.then_inc(self.sems[op["sig"][0]], 1)
                        op["fn"] = None
                    for k, v in self.cnt.items():
                        if seen.get(k, 0) < v:
                            e.wait_ge(self.sems[k], v)
                            seen[k] = v
                getattr(block, attr)(body)
        self.flushed = len(self.ops)


class Builder:
    def __init__(self, seg, depth, debug=None):
        self.SEG = seg
        self.NT = 2 * seg
        self.depth = depth
        self.debug = debug
        self.nc = bass.Bass("TRN2", target_bir_lowering=False)
        try:
            self.nc.allow_low_precision("bf16 matmuls by design")
        except Exception:
            pass
        try:
            self.nc.allow_non_contiguous_dma("small strided param loads")
        except Exception:
            pass
        self.uid = 0

    def dram(self, name, shape, dt, kind="Internal"):
        return self.nc.dram_tensor(name, list(shape), dt, kind=kind).ap()

    def sb(self, es, name, shape, dt):
        self.uid += 1
        return es.enter_context(self.nc.sbuf_tensor("%s_%d" % (name, self.uid), list(shape), dt))

    def ps(self, es, name, shape, dt=F32):
        self.uid += 1
        return es.enter_context(self.nc.psum_tensor("%s_%d" % (name, self.uid), list(shape), dt))

    def build(self):
        nc = self.nc
        NT, SEG, depth = self.NT, self.SEG, self.depth
        L = depth
        inp = {}

        def ext(name, shape, dt=F32):
            inp[name] = self.dram(name, shape, dt, kind="ExternalInput")
            return inp[name]

        x_in = ext("x", [NT, D])
        c_in = ext("c2", [2, D])
        flag_in = ext("flag", [128, 1])
        cos_in = ext("cosT", [128, NT])
        sin_in = ext("sinT", [128, NT])
        cst_in = ext("consts", [128, 16 * 128])
        ada_w = ext("ada_w", [L, D, 9 * D])
        ada_b = ext("ada_b", [L, 9 * D])
        nrm1 = ext("norm_ffn1", [L, D])
        w1g = ext("ffn1_w_gate", [L, D, FF])
        w1u = ext("ffn1_w_up", [L, D, FF])
        w1d = ext("ffn1_w_down", [L, FF, D])
        nrm2 = ext("norm_mix", [L, D])
        w_in = ext("w_in", [L, D, INW])
        conv_w = ext("conv_w", [L, 5, 1536])
        a_log = ext("a_log", [L, 8])
        dt_bias = ext("dt_bias", [L, 8])
        dn_norm = ext("dn_norm", [L, 128])
        w_out = ext("w_out", [L, D, D])
        nrm3 = ext("norm_ffn2", [L, D])
        w2g = ext("ffn2_w_gate", [L, D, FF])
        w2u = ext("ffn2_w_up", [L, D, FF])
        w2d = ext("ffn2_w_down", [L, FF, D])
        nrmf = ext("norm_final", [1, D])
        y_out = self.dram("y", [NT, D], F32, kind="ExternalOutput")
        self.dbg_out = {}

        PADW = SEG + 4
        XT = self.dram("XT", [8, 128, NT], F32)
        X1T = self.dram("X1T", [8, 128, NT], F32)
        QT = self.dram("QT", [4, 128, NT], BF16)
        KT = self.dram("KT", [4, 128, NT], BF16)
        VP = self.dram("VP", [NT, 1024], BF16)
        PDN = self.dram("PDN", [12, 128, 2 * PADW], F32)
        ZS = self.dram("ZS", [NT, 512], F32)
        GB = self.dram("GB", [NT, 16], F32)
        MIXT = self.dram("MIXT", [8, 128, NT], BF16)
        DNQT = self.dram("DNQT", [4, 128, NT], F32)
        DNKT = self.dram("DNKT", [4, 128, NT], F32)
        DNK = self.dram("DNK", [NT, 512], F32)
        DNV = self.dram("DNV", [NT, 512], F32)
        OFB = self.dram("OFB", [2, NT, 512], F32)
        WS = {}
        for l in range(L):
            for f in (1, 2):
                for m in range(NFC):
                    WS[(l, "gu", f, m)] = self.dram("wgu%d_%d_%d" % (l, f, m), [128, 2048], BF16)
                for dc in range(8):
                    WS[(l, "dn", f, dc)] = self.dram("wdn%d_%d_%d" % (l, f, dc), [128, NFC * 128], BF16)
            for j in range(5):
                WS[(l, "inF", j)] = self.dram("winF%d_%d" % (l, j), [128, 4096], BF16)
            for j in range(2):
                WS[(l, "inP", j)] = self.dram("winP%d_%d" % (l, j), [128, 4096], BF16)
            for j in range(2):
                WS[(l, "inT", j)] = self.dram("winT%d_%d" % (l, j), [128, 4096], BF16)
            WS[(l, "inAB")] = self.dram("winAB%d" % l, [128, 128], BF16)
            for j in range(2):
                WS[(l, "out", j)] = self.dram("wout%d_%d" % (l, j), [128, 4096], BF16)

        es0 = ExitStack()
        self.es0 = es0
        P = Prog(nc, es0)
        self.P = P

        cst = self.sb(es0, "cst", [128, 16 * 128], F32)
        cstb = self.sb(es0, "cstb", [128, 16 * 128], BF16)
        flag = self.sb(es0, "flag", [128, 1], F32)
        modT = self.sb(es0, "modT", [128, L, 72, 2], F32)
        coef = self.sb(es0, "coef", [128, L, 9, 8, 2], F32)
        nfin = self.sb(es0, "nfin", [128, 8], F32)
        convw = self.sb(es0, "convw", [128, L, 5, 12], F32)
        gconst = self.sb(es0, "gconst", [128, L, 2, 8], F32)
        dnw = self.sb(es0, "dnw", [128, L, 512], F32)
        ones_b = self.sb(es0, "ones_b", [128, 128], BF16)
        epsD = self.sb(es0, "epsD", [128, 1], F32)

        def C(i):
            return cst[:, i * 128:(i + 1) * 128]

        def CB(i):
            return cstb[:, i * 128:(i + 1) * 128]
        self.C, self.CB = C, CB

        P.add("sp", lambda e: [e.dma_start(out=cst[:], in_=cst_in[:, :])], w=["cst"], slot="c0")
        P.add("sp", lambda e: [e.dma_start(out=flag[:], in_=flag_in[:, :])], w=["flag"], slot="c1")
        P.add("dve", lambda e: e.tensor_copy(cstb[:], cst[:]), r=["cst"], w=["cstb"])
        P.add("dve", lambda e: e.memset(ones_b[:], 1.0), w=["ones_b"])
        self.epsc = self.sb(es0, "epsc", [128, 4], F32)
        P.add("dve", lambda e: e.memset(self.epsc[:, 0:1], float(D * EPS)), w=["epsc0"])
        P.add("dve", lambda e: e.memset(self.epsc[:, 1:2], float(EPS)), w=["epsc1"])
        P.add("dve", lambda e: e.memset(self.epsc[:, 2:3], 1.0), w=["epsc2"])
        P.add("dve", lambda e: e.memset(self.epsc[:, 3:4], 0.0), w=["epsc3"])

        self.prologue_params(inp, modT, coef, nfin, convw, gconst, dnw, flag)
        P.flush()
        self.prologue_weights(inp, WS)
        P.flush()

        st = dict(XT=XT, X1T=X1T, QT=QT, KT=KT, VP=VP, PDN=PDN, ZS=ZS, GB=GB, MIXT=MIXT, DNQT=DNQT,
                  DNKT=DNKT, DNK=DNK, DNV=DNV, OFB=OFB, WS=WS, coef=coef, nfin=nfin, convw=convw,
                  gconst=gconst, dnw=dnw, ones_b=ones_b, flag=flag, x_in=x_in, y_out=y_out,
                  cos_in=cos_in, sin_in=sin_in, PADW=PADW)
        self.st = st
        for l in range(L):
            if self.debug == "P":
                break
            self.phase_A(l)
            P.flush()
            if self.debug == "A":
                break
            self.phase_attn(l)
            P.flush()
            if self.debug == "attn":
                break
            self.phase_dn(l)
            P.flush()
            if self.debug and self.debug.startswith("dn"):
                break
            self.phase_C(l)
            P.flush()
        if self.debug:
            self.dumps()
            P.flush()
        P.add("sp", lambda e: None, w=["Y"], slot=None)
        P.flush()
        es0.close()
        return nc

    def prologue_params(self, inp, modT, coef, nfin, convw, gconst, dnw, flag):
        nc, P, L = self.nc, self.P, self.depth
        es = ExitStack()
        cT = self.sb(es, "cT", [128, 8, 2], F32)
        scT = self.sb(es, "scT", [128, 8, 2], F32)
        adab = self.sb(es, "adab", [128, L, 72], F32)
        nw = self.sb(es, "nw", [128, L, 3, 8], F32)
        aw = [self.sb(es, "aw%d" % i, [128, 8, 512], F32) for i in range(2)]
        mps = self.ps(es, "mps", [128, 72, 2])
        tmpc = self.sb(es, "tmpc", [128, 8, 2], F32)

        c_in = inp["c2"]
        C = self.C
        prm = [self.sb(es, "prm%d" % i, [128, 128], F32) for i in range(3 + L)]
        tps = self.ps(es, "tps", [128, 512])

        def rows(ap2, i):
            return ap2[i:i + 1, :].rearrange("o (c p) -> (o c) p", p=128)

        def load_T(k, tile, srcs, nrows, col0, readers):
            P.add("sp", lambda e: [e.dma_start(out=tile[r0:r0 + n, :], in_=src) for (r0, n, src) in srcs],
                  w=[("prm", k)], slot=("prm", k), ndma=len(srcs))
            P.add("pe", lambda e: e.transpose(tps[:, col0:col0 + nrows], tile[0:nrows, :], C(0)[0:nrows, 0:nrows]),
                  r=[("prm", k), "cst"], w=["tps"])
        load_T(0, prm[0], [(s_ * 8, 8, rows(c_in, s_)) for s_ in range(2)], 16, 0, None)
        P.add("dve", lambda e: e.tensor_copy(cT[:].rearrange("p c s -> p s c"), tps[:, 0:16].rearrange("p (s c) -> p s c", s=2)),
              r=["tps"], w=["cT"])
        P.add("act", lambda e: e.activation(out=scT[:], in_=cT[:], func=AF.Silu), r=["cT"], w=["scT"])
        names = ["norm_ffn1", "norm_mix", "norm_ffn2"]
        srcs = [((i * L + l) * 8, 8, rows(inp[names[i]], l)) for i in range(3) for l in range(L)]
        srcs.append((3 * L * 8, 8, rows(inp["norm_final"], 0)))
        nr = 3 * L * 8 + 8
        load_T(1, prm[1], srcs, nr, 16, None)
        P.add("dve", lambda e: e.tensor_copy(nw[:].rearrange("p l i c -> p i l c"),
                                             tps[:, 16:16 + 3 * L * 8].rearrange("p (i l c) -> p i l c", i=3, l=L)),
              r=["tps"], w=["nw"])
        P.add("dve", lambda e: e.tensor_copy(nfin[:], tps[:, 16 + 3 * L * 8:16 + nr]), r=["tps"], w=["nfin"])
        for l in range(L):
            col0 = 16 + nr + l * 60
            load_T(2 + l, prm[2 + l], [(j * 12, 12, rows(inp["conv_w"][l], j)) for j in range(5)], 60, col0, None)
            P.add("dve", lambda e, l=l, col0=col0: e.tensor_copy(convw[:, l, :, :].rearrange("p j c -> p (j c)"), tps[:, col0:col0 + 60]),
                  r=["tps"], w=[("convw", l)])
        P.add("dve", lambda e: e.memset(tmpc[:], 0.0), r=[("convw", l) for l in range(L)], w=["convw", "tmpc"])
        tps2 = self.ps(es, "tps2", [128, 512])
        prmb = [self.sb(es, "prmb%d" % l, [128, 128], F32) for l in range(L)]
        for l in range(L):
            P.add("sp", lambda e, l=l: [e.dma_start(out=prmb[l][0:72, :], in_=rows(inp["ada_b"], l))], w=[("prmb", l)], slot=("prmb", l))
            P.add("pe", lambda e, l=l: e.transpose(tps2[:, l * 72:(l + 1) * 72], prmb[l][0:72, :], C(0)[0:72, 0:72]),
                  r=[("prmb", l), "cst"], w=["tps2"])
            P.add("dve", lambda e, l=l: e.tensor_copy(adab[:, l, :], tps2[:, l * 72:(l + 1) * 72]), r=["tps2"], w=[("adab", l)])
        P.add("dve", lambda e: e.memset(tmpc[:], 0.0), r=[("adab", l) for l in range(L)] + ["tmpc"], w=["adab", "tmpc"])
        P.add("sp", lambda e: [e.dma_start(out=gconst[:, l, i, :], in_=inp[nm][l:l + 1, :].partition_broadcast(128))
                               for l in range(L) for i, nm in enumerate(("a_log", "dt_bias"))],
              w=["gconst0"], slot="p5", ndma=2 * L)
        P.add("dve", lambda e: e.tensor_scalar_mul(nfin[:], nfin[:], float(np.sqrt(D))), r=["nfin"], w=["nfin"])
        P.add("act", lambda e: e.activation(out=gconst[:, :, 0, :], in_=gconst[:, :, 0, :], func=AF.Exp),
              r=["gconst0"], w=["gconst1"])
        P.add("dve", lambda e: e.tensor_scalar_mul(gconst[:, :, 0, :], gconst[:, :, 0, :], -1.0),
              r=["gconst1"], w=["gconst"])
        for l in range(L):
            P.add("sp", lambda e, l=l: [e.dma_start(out=dnw[:, l, h * 128:(h + 1) * 128],
                                                   in_=inp["dn_norm"][l:l + 1, :].partition_broadcast(128))
                                        for h in range(4)], w=["dnw"], slot="p6", ndma=4)
        for l in range(L):
            for pc in range(18):
                buf = aw[pc % 2]
                bk = ("aw", pc % 2)
                src = inp["ada_w"][l].rearrange("(kc p) n -> p kc n", p=128)[:, :, pc * 512:(pc + 1) * 512]
                P.add("sp", lambda e, buf=buf, src=src: [e.dma_start(out=buf[:], in_=src)], w=[bk], slot=bk)
                for jj in range(4):
                    j = pc * 4 + jj

                    def mm(e, buf=buf, jj=jj, j=j):
                        ins = None
                        for kc in range(8):
                            ins = e.matmul(mps[:, j, :], lhsT=buf[:, kc, jj * 128:(jj + 1) * 128], rhs=scT[:, kc, :],
                                           start=(kc == 0), stop=(kc == 7))
                        return ins
                    P.add("pe", mm, r=[bk, "scT"], w=[("mps", j)])
            P.add("dve", lambda e, l=l: e.tensor_tensor(out=modT[:, l, :, :], in0=mps[:],
                                                        in1=adab[:, l, :].unsqueeze(2).broadcast_to([128, 72, 2]),
                                                        op=ALU.add),
                  r=[("mps", j) for j in range(72)] + ["adab"], w=[("mps", j) for j in range(72)] + [("modT", l)])
        sqD = float(np.sqrt(D))
        for l in range(L):
            for i in range(3):
                sh = modT[:, l, (3 * i) * 8:(3 * i + 1) * 8, :]
                sc = modT[:, l, (3 * i + 1) * 8:(3 * i + 2) * 8, :]
                gt = modT[:, l, (3 * i + 2) * 8:(3 * i + 3) * 8, :]
                gs = 1.0 if i == 1 else 0.5
                wv = nw[:, l, i, :].unsqueeze(2).broadcast_to([128, 8, 2])
                P.add("dve", lambda e, sc=sc: e.tensor_scalar(tmpc[:], sc, 1.0, sqD, op0=ALU.add, op1=ALU.mult),
                      r=[("modT", l)], w=["tmpc"])
                P.add("dve", lambda e, l=l, i=i, wv=wv: e.tensor_tensor(out=coef[:, l, 3 * i, :, :], in0=tmpc[:], in1=wv,
                                                                       op=ALU.mult),
                      r=["tmpc", "nw"], w=[("coefa", l, i)])
                P.add("dve", lambda e, l=l, i=i, sh=sh: e.tensor_copy(coef[:, l, 3 * i + 1, :, :], sh),
                      r=[("modT", l)], w=[("coefb", l, i)])
                P.add("dve", lambda e, l=l, i=i, gt=gt, gs=gs: e.tensor_scalar_mul(coef[:, l, 3 * i + 2, :, :], gt, gs),
                      r=[("modT", l)], w=[("coefg", l, i)])
        P.add("dve", lambda e: e.memset(tmpc[:], 0.0),
              r=[("coefa", l, i) for l in range(L) for i in range(3)] + [("coefb", l, i) for l in range(L) for i in range(3)]
              + [("coefg", l, i) for l in range(L) for i in range(3)] + ["nfin", "convw", "gconst", "dnw", "tmpc"],
              w=["coef", "tmpc"])
        P.flush()
        es.close()

    def prologue_weights(self, inp, WS):
        nc, P, L = self.nc, self.P, self.depth
        es = ExitStack()
        NB = 3
        s32 = [self.sb(es, "s32_%d" % i, [128, 4096], F32) for i in range(NB)]
        s16 = [self.sb(es, "s16_%d" % i, [128, 4096], BF16) for i in range(NB)]
        s16p = [self.sb(es, "s16p_%d" % i, [128, 4096], BF16) for i in range(2)]
        state = dict(i=0, ip=0)
        cast_eng = ["dve", "act", "pool"]

        def piece(dkey, n, srcs, perm_key=None):
            dst = WS[dkey]
            perm_dst = WS[perm_key] if perm_key is not None else None
            i = state["i"]
            state["i"] += 1
            b = i % NB
            k32, k16 = ("s32", b), ("s16", b)

            def ld(e):
                return [e.dma_start(out=dv(s32[b]), in_=sv) for dv, sv in srcs]
            P.add("sp", ld, w=[k32], slot=k32, ndma=len(srcs))
            ce = cast_eng[i % 2]
            if ce == "act":
                P.add("act", lambda e: e.copy(out=s16[b][:, :n], in_=s32[b][:, :n]), r=[k32], w=[k16])
            else:
                P.add(ce, lambda e: e.tensor_copy(s16[b][:, :n], s32[b][:, :n]), r=[k32], w=[k16])
            P.add("pool", lambda e: [e.dma_start(out=dst[:, :n], in_=s16[b][:, :n])], r=[k16], w=[("W", dkey)], slot=k16)
            if perm_dst is not None:
                ip = state["ip"] % 2
                state["ip"] += 1
                kp = ("s16p", ip)
                src4 = s16[b][:].rearrange("p (j k e d) -> p j k e d", j=4, k=8, e=2)
                dst4 = s16p[ip][:].rearrange("p (j k e d) -> p j k e d", j=4, k=8, e=2)

                def pm(e):
                    for j in range(4):
                        e.tensor_copy(dst4[:, j, :, :, 16:64], src4[:, j, :, :, 16:64])
                        e.tensor_copy(dst4[:, j, :, :, 0:8], src4[:, j, :, :, 8:16])
                        ins = e.tensor_copy(dst4[:, j, :, :, 8:16], src4[:, j, :, :, 0:8])
                    return ins
                P.add("dve", pm, r=[k16], w=[kp])
                P.add("pool", lambda e: [e.dma_start(out=perm_dst[:, :], in_=s16p[ip][:])], r=[kp], w=[("W", perm_key)], slot=kp)

        def view(shape_str, lo, hi, **kw):
            return lambda t: t[:, lo:hi].rearrange(shape_str, **kw)

        for l in range(L):
            for f, (wg, wu, wd) in ((1, ("ffn1_w_gate", "ffn1_w_up", "ffn1_w_down")), (2, ("ffn2_w_gate", "ffn2_w_up", "ffn2_w_down"))):
                g3 = inp[wg][l].rearrange("(kc p) n -> p kc n", p=128)
                u3 = inp[wu][l].rearrange("(kc p) n -> p kc n", p=128)
                d3 = inp[wd][l].rearrange("(fc p) n -> p fc n", p=128)
                for m in range(NFC):
                    piece((l, "gu", f, m), 2048,
                          [(view("p (k c) -> p k c", 0, 1024, k=8), g3[:, :, m * 128:(m + 1) * 128]),
                           (view("p (k c) -> p k c", 1024, 2048, k=8), u3[:, :, m * 128:(m + 1) * 128])])
                for dc in range(8):
                    piece((l, "dn", f, dc), NFC * 128,
                          [(view("p (k c) -> p k c", f0 * 128, f1 * 128, k=f1 - f0), d3[:, f0:f1, dc * 128:(dc + 1) * 128])
                           for (f0, f1) in ((0, 8), (8, 16), (16, NFC))])
            i3 = inp["w_in"][l].rearrange("(kc p) n -> p kc n", p=128)
            fcols = [0, 512, 1536, 2048, 2560]
            for j in range(5):
                c0 = fcols[j]
                srcs = [(view("p (k c) -> p k c", jj * 1024, (jj + 1) * 1024, k=8), i3[:, :, c0 + jj * 128:c0 + (jj + 1) * 128])
                        for jj in range(4)]
                piece((l, "inF", j), 4096, srcs, perm_key=((l, "inP", j) if j < 2 else None))
            for j, c0 in enumerate((1024, 3072)):
                piece((l, "inT", j), 4096, [(view("p (k c) -> p k c", 0, 4096, k=8), i3[:, :, c0:c0 + 512])])
            piece((l, "inAB"), 128, [(view("p (k c) -> p k c", 0, 128, k=8), i3[:, :, 3584:3600])])
            o3 = inp["w_out"][l].rearrange("(kc p) n -> p kc n", p=128)
            for j in range(2):
                srcs = [(view("p (k c) -> p k c", jj * 1024, (jj + 1) * 1024, k=8),
                         o3[:, :, (j * 4 + jj) * 128:(j * 4 + jj + 1) * 128]) for jj in range(4)]
                piece((l, "out", j), 4096, srcs)
        P.flush()
        es.close()


    def dumps(self):
        st = self.st
        dbg = self.debug
        if dbg == "P":
            self.dump("coef", st["coef"][:].rearrange("p a b c d -> p (a b c d)"), [])
            self.dump("w_gu0", st["WS"][(0, "gu", 1, 0)], [])
            self.dump("w_inP0", st["WS"][(0, "inP", 0)], [])
            self.dump("w_dn3", st["WS"][(0, "dn", 2, 3)], [])
        if dbg == "A":
            for nm in ("X1T", "QT", "KT", "VP", "PDN", "ZS", "GB"):
                self.dump(nm, st[nm], [])
        if dbg == "attn":
            self.dump("MIXT", st["MIXT"][0:4], [])
        if dbg and dbg.startswith("dn"):
            names = {"dn1": ("DNQT", "DNKT", "DNK", "DNV"), "dn2": ("DNQT", "DNKT", "DNK", "DNV", "OFB")}.get(
                dbg, ("DNQT",) if dbg.startswith("dn2:") else ("DNQT", "DNKT", "DNK", "DNV", "OFB", "MIXT"))
            for nm in names:
                self.dump(nm, st[nm], [])

    def dump(self, name, src, rkeys):
        out = self.dram("dbg_" + name, list(src.shape), src.dtype, kind="ExternalOutput")
        self.dbg_out[name] = out
        self.P.add("sp", lambda e: [e.dma_start(out=out, in_=src)], r=list(rkeys) + ["Y"], slot="dbg")

    class WStream:
        def __init__(self, B, slots, pieces):
            self.B, self.slots, self.pieces = B, slots, pieces
            self.i = 0
            self.j = 0
            for _ in range(len(slots)):
                self._load()

        def _load(self):
            if self.j >= len(self.pieces):
                return
            key, n = self.pieces[self.j]
            s = self.j % len(self.slots)
            src = self.B.st["WS"][key]
            tile = self.slots[s]
            self.B.P.add("sp", lambda e: [e.dma_start(out=tile[:, :n], in_=src[:, :n])],
                         r=[("W", key)], w=[("wsl", s)], slot=("wsl", s))
            self.j += 1

        def get(self, key):
            k, n = self.pieces[self.i]
            assert k == key, (k, key)
            s = self.i % len(self.slots)
            self.i += 1
            return self.slots[s], ("wsl", s)

        def done(self):
            self._load()

    def norm(self, xt_t, kx, hout, kh, a_fn, b_fn, bf):
        P = self.P
        sq, pss, kpss, rstd, tmpn, ones_b = bf["sq"], bf["pss"], bf["kpss"], bf["rstd"], bf["tmpn"], self.st["ones_b"]
        P.add("act", lambda e: e.activation(out=sq[:], in_=xt_t[:], func=AF.Square), r=[kx], w=["sq"])

        def mm(e):
            for c in range(8):
                ins = e.matmul(pss[:], lhsT=ones_b[:], rhs=sq[:, c, :], start=(c == 0), stop=(c == 7))
            return ins
        P.add("pe", mm, r=["sq", "ones_b"], w=[kpss])
        P.add("act", lambda e: e.activation(out=bf["tmpn"][0][:], in_=pss[:], func=AF.Sqrt, bias=self.epsc[:, 0:1], scale=1.0),
              r=[kpss, ("tmpn", 0)], w=[("tmpn", 0)])
        P.add("dve", lambda e: e.reciprocal(rstd[:], bf["tmpn"][0][:]), r=[("tmpn", 0)], w=["rstd"])
        for c in range(8):
            tb = tmpn[c % 2]
            P.add("dve", lambda e, c=c, tb=tb: e.tensor_tensor(out=tb[:], in0=xt_t[:, c, :], in1=rstd[:], op=ALU.mult),
                  r=[kx, "rstd"], w=[("tmpn", c % 2)])
            bias = b_fn(c) if b_fn is not None else 0.0
            P.add("act", lambda e, c=c, tb=tb, bias=bias: e.activation(out=hout[:, c, :], in_=tb[:], func=AF.Identity,
                                                                      bias=bias, scale=a_fn(c)),
                  r=[("tmpn", c % 2), "coef"], w=[(kh, c)])

    def ffn(self, l, f, s, xt_t, kx, bf, ws):
        P, coef = self.P, self.st["coef"]
        i = 0 if f == 1 else 2
        h, act, sg, banks = bf["h"], bf["act"], bf["sg"], bf["banks"]
        self.norm(xt_t, kx, h, "h", lambda c: coef[:, l, 3 * i, c, s:s + 1], lambda c: coef[:, l, 3 * i + 1, c, s:s + 1], bf)
        hk = [("h", c) for c in range(8)]
        for m in range(NFC):
            wt, wk = ws.get((l, "gu", f, m))
            w4 = wt[:, :2048].rearrange("p (g k c) -> p g k c", g=2, k=8)
            pg, kpg = banks[(m % 2) * 2], ("bank", (m % 2) * 2)
            pu, kpu = banks[(m % 2) * 2 + 1], ("bank", (m % 2) * 2 + 1)

            def mm(e, w4=w4, pg=pg, pu=pu):
                for g, pp in ((0, pg), (1, pu)):
                    for kc in range(8):
                        ins = e.matmul(pp[:], lhsT=w4[:, g, kc, :], rhs=h[:, kc, :], start=(kc == 0), stop=(kc == 7))
                return ins
            P.add("pe", mm, r=[wk] + hk, w=[kpg, kpu])
            ws.done()
            sgb = sg[m % 2]
            P.add("act", lambda e, pg=pg, sgb=sgb: e.activation(out=sgb[:], in_=pg[:], func=AF.Silu),
                  r=[kpg], w=[("sg", m % 2)])
            P.add("dve", lambda e, pu=pu, sgb=sgb, m=m: e.tensor_tensor(out=act[:, m, :], in0=pu[:], in1=sgb[:], op=ALU.mult),
                  r=[kpu, ("sg", m % 2)], w=[("act", m)])
        ak = [("act", m) for m in range(NFC)]
        for dc in range(8):
            wt, wk = ws.get((l, "dn", f, dc))
            w3 = wt[:, :NFC * 128].rearrange("p (k c) -> p k c", k=NFC)
            py, kpy = banks[4 + dc % 2], ("bank", 4 + dc % 2)

            def mm2(e, w3=w3, py=py):
                for fc in range(NFC):
                    ins = e.matmul(py[:], lhsT=w3[:, fc, :], rhs=act[:, fc, :], start=(fc == 0), stop=(fc == NFC - 1))
                return ins
            P.add("pe", mm2, r=[wk] + ak, w=[kpy])
            ws.done()
            P.add("dve", lambda e, dc=dc, py=py: e.scalar_tensor_tensor(out=xt_t[:, dc, :], in0=py[:],
                                                                       scalar=coef[:, l, 3 * i + 2, dc, s:s + 1],
                                                                       in1=xt_t[:, dc, :], op0=ALU.mult, op1=ALU.add),
                  r=[kpy, kx, "coef"], w=[kx])

    def gemm_bufs(self, es):
        bf = {}
        bf["sq"] = self.sb(es, "sq", [128, 8, 512], BF16)
        bf["rstd"] = self.sb(es, "rstd", [128, 512], F32)
        bf["tmpn"] = [self.sb(es, "tmpn", [128, 512], F32) for _ in range(2)]
        bf["h"] = self.sb(es, "h", [128, 8, 512], BF16)
        bf["act"] = self.sb(es, "act", [128, NFC, 512], BF16)
        bf["sg"] = [self.sb(es, "sg", [128, 512], F32) for _ in range(2)]
        bf["wsl"] = [self.sb(es, "wsl", [128, 4096], BF16) for _ in range(4)]
        bf["banks"] = [self.ps(es, "bank", [128, 512]) for _ in range(8)]
        bf["pss"], bf["kpss"] = bf["banks"][6], ("bank", 6)
        return bf

    def phase_A(self, l):
        nc, P, st = self.nc, self.P, self.st
        NT, SEG = self.NT, self.SEG
        C, coef = self.C, st["coef"]
        es = ExitStack()
        bf = self.gemm_bufs(es)
        banks = bf["banks"]
        xt = [self.sb(es, "xt", [128, 8, 512], F32) for _ in range(2)]
        xtok = [self.sb(es, "xtok", [128, 1024], F32) for _ in range(2)]
        cosb = self.sb(es, "cosb", [128, 512], F32)
        sinb = self.sb(es, "sinb", [128, 512], F32)
        t1 = self.sb(es, "t1", [128, 512], F32)
        t2 = self.sb(es, "t2", [128, 512], F32)
        stq = [self.sb(es, "stq", [128, 512], BF16) for _ in range(2)]
        stg = [self.sb(es, "stg", [128, 512], F32) for _ in range(3)]
        vst = [self.sb(es, "vst", [128, 1024], BF16) for _ in range(2)]
        zst = [self.sb(es, "zst", [128, 512], F32) for _ in range(2)]
        gab = [self.sb(es, "gab", [128, 16], F32) for _ in range(2)]
        gt = [self.sb(es, "gt", [128, 8], F32) for _ in range(4)]
        gout = [self.sb(es, "gout", [128, 16], F32) for _ in range(2)]
        zpad = self.sb(es, "zpad", [128, 2], F32)
        h = bf["h"]
        NTILE = NT // TT
        PADW = st["PADW"]
        for i in range(2):
            P.add("dve", lambda e, i=i: e.memset(vst[i][:], 0.0), w=[("vst", i)])
        P.add("dve", lambda e: e.memset(zpad[:], 0.0), w=["zpad"])
        P.add("pool", lambda e: [e.dma_start(out=st["PDN"][c, :, 0:2], in_=zpad[:]) for c in range(12)]
              + [e.dma_start(out=st["PDN"][c, :, 2 * PADW - 2:2 * PADW], in_=zpad[:]) for c in range(12)],
              r=["zpad"], w=[("PDNpad", l)], slot="zpad", ndma=24)
        pieces = []
        for t in range(NTILE):
            pieces += [((l, "gu", 1, m), 2048) for m in range(NFC)] + [((l, "dn", 1, dc), NFC * 128) for dc in range(8)]
            pieces += [((l, "inF", 0), 4096), ((l, "inP", 0), 4096), ((l, "inF", 1), 4096), ((l, "inP", 1), 4096),
                       ((l, "inF", 2), 4096), ((l, "inF", 3), 4096), ((l, "inF", 4), 4096),
                       ((l, "inT", 0), 4096), ((l, "inT", 1), 4096), ((l, "inAB"), 128)]
        ws = Builder.WStream(self, bf["wsl"], pieces)
        ident = C(0)
        def _tile(t):
            s = (t * TT) // SEG
            tok0 = t * TT
            xt_t, kx = xt[t % 2], ("xt", t % 2)
            if l == 0:
                for sb_ in range(4):
                    xk = xtok[sb_ % 2]
                    kxk = ("xtok", sb_ % 2)
                    P.add("sp", lambda e, xk=xk, sb_=sb_: [e.dma_start(out=xk[:], in_=st["x_in"][tok0 + sb_ * 128:tok0 + (sb_ + 1) * 128, :])],
                          w=[kxk], slot=kxk)
                    for hf in range(2):
                        bk, kbk = banks[hf], ("bank", hf)

                        def tr(e, xk=xk, hf=hf, bk=bk):
                            for cc in range(4):
                                c = hf * 4 + cc
                                ins = e.transpose(bk[:, cc * 128:(cc + 1) * 128], xk[:, c * 128:(c + 1) * 128], ident)
                            return ins
                        P.add("pe", tr, r=[kxk, "cst"], w=[kbk])
                        P.add("act", lambda e, hf=hf, bk=bk, sb_=sb_: e.copy(out=xt_t[:, hf * 4:(hf + 1) * 4, sb_ * 128:(sb_ + 1) * 128],
                                                                            in_=bk[:].rearrange("p (c t) -> p c t", c=4)),
                              r=[kbk], w=[kx])
            else:
                P.add("pool", lambda e: [e.dma_start(out=xt_t[:], in_=st["XT"][:, :, tok0:tok0 + TT].rearrange("c p t -> p c t"))],
                      r=[("XT", t)], w=[kx], slot=kx)
            self.ffn(l, 1, s, xt_t, kx, bf, ws)
            P.add("pool", lambda e: [e.dma_start(out=st["X1T"][:, :, tok0:tok0 + TT].rearrange("c p t -> p c t"), in_=xt_t[:])],
                  r=[kx], w=[("X1T", t)], slot=("x1st", t % 2))
            self.norm(xt_t, kx, h, "h", lambda c: coef[:, l, 3, c, s:s + 1], lambda c: coef[:, l, 4, c, s:s + 1], bf)
            hk = [("h", c) for c in range(8)]
            P.add("sp", lambda e: [e.dma_start(out=cosb[:], in_=st["cos_in"][:, tok0:tok0 + TT]),
                                   e.dma_start(out=sinb[:], in_=st["sin_in"][:, tok0:tok0 + TT])],
                  w=["cs"], slot="cs", ndma=2)
            for qk in range(2):
                wt, wk = ws.get((l, "inF", qk))
                wp, wpk = ws.get((l, "inP", qk))
                w4 = wt[:].rearrange("p (j k c) -> p j k c", j=4, k=8)
                p4 = wp[:].rearrange("p (j k c) -> p j k c", j=4, k=8)
                dst = st["QT"] if qk == 0 else st["KT"]
                for j in range(4):
                    pa, kpa = banks[0 + (j % 2) * 2], ("bank", (j % 2) * 2)
                    pb, kpb = banks[1 + (j % 2) * 2], ("bank", 1 + (j % 2) * 2)

                    def mm(e, w4=w4, p4=p4, j=j, pa=pa, pb=pb):
                        for ww, pp in ((w4, pa), (p4, pb)):
                            for kc in range(8):
                                ins = e.matmul(pp[:], lhsT=ww[:, j, kc, :], rhs=h[:, kc, :], start=(kc == 0), stop=(kc == 7))
                        return ins
                    P.add("pe", mm, r=[wk, wpk] + hk, w=[kpa, kpb])
                    P.add("dve", lambda e, pa=pa: e.tensor_tensor(out=t1[:], in0=pa[:], in1=cosb[:], op=ALU.mult),
                          r=[kpa, "cs"], w=["t1"])
                    P.add("dve", lambda e, pb=pb: e.tensor_tensor(out=t2[:], in0=pb[:], in1=sinb[:], op=ALU.mult),
                          r=[kpb, "cs"], w=["t2"])
                    sq_, ksq = stq[j % 2], ("stq", j % 2)
                    P.add("dve", lambda e, sq_=sq_: e.tensor_tensor(out=sq_[:], in0=t1[:], in1=t2[:], op=ALU.add),
                          r=["t1", "t2"], w=[ksq])
                    P.add("pool", lambda e, sq_=sq_, j=j, dst=dst: [e.dma_start(out=dst[j, :, tok0:tok0 + TT], in_=sq_[:])],
                          r=[ksq], w=[("QK", qk, j, t)], slot=ksq)
                ws.done()
                ws.done()
            for g in range(3):
                wt, wk = ws.get((l, "inF", 2 + g))
                w4 = wt[:].rearrange("p (j k c) -> p j k c", j=4, k=8)
                for j in range(4):
                    cidx = g * 4 + j
                    pa, kpa = banks[cidx % 4], ("bank", cidx % 4)

                    def mm(e, w4=w4, j=j, pa=pa):
                        for kc in range(8):
                            ins = e.matmul(pa[:], lhsT=w4[:, j, kc, :], rhs=h[:, kc, :], start=(kc == 0), stop=(kc == 7))
                        return ins
                    P.add("pe", mm, r=[wk] + hk, w=[kpa])
                    sg_, ksg = stg[cidx % 3], ("stg", cidx % 3)
                    P.add("act", lambda e, pa=pa, sg_=sg_: e.copy(out=sg_[:], in_=pa[:]), r=[kpa], w=[ksg])
                    col0 = s * PADW + 2 + (tok0 - s * SEG)
                    P.add("pool", lambda e, sg_=sg_, cidx=cidx, col0=col0: [e.dma_start(out=st["PDN"][cidx, :, col0:col0 + TT], in_=sg_[:])],
                          r=[ksg], w=[("PDN", cidx, t)], slot=ksg)
                    if tok0 + TT == SEG or tok0 == SEG:
                        left = (tok0 + TT == SEG)
                        srcv = sg_[:, TT - 2:TT] if left else sg_[:, 0:2]
                        dcol = (PADW + 0) if left else (PADW - 2)
                        P.add("dve", lambda e, srcv=srcv: e.tensor_scalar(zpad[:], srcv, st["flag"][:, 0:1], None, op0=ALU.mult),
                              r=[ksg, "flag", "zpad"], w=["zpad"])
                        P.add("pool", lambda e, cidx=cidx, dcol=dcol: [e.dma_start(out=st["PDN"][cidx, :, dcol:dcol + 2], in_=zpad[:])],
                              r=["zpad"], w=[("PDNh", cidx, left)], slot="zpad")
                ws.done()
            wv, wvk = ws.get((l, "inT", 0))
            wz, wzk = ws.get((l, "inT", 1))
            wab, wabk = ws.get((l, "inAB"))
            wv3 = wv[:].rearrange("p (k c) -> p k c", k=8)
            wz3 = wz[:].rearrange("p (k c) -> p k c", k=8)
            wab3 = wab[:, :128].rearrange("p (k c) -> p k c", k=8)
            for sb_ in range(4):
                tk0 = tok0 + sb_ * 128
                pv, kpv = banks[0 + (sb_ % 2) * 3], ("bank", (sb_ % 2) * 3)
                pz, kpz = banks[1 + (sb_ % 2) * 3], ("bank", 1 + (sb_ % 2) * 3)
                pab, kpab = banks[2 + (sb_ % 2) * 3], ("bank", 2 + (sb_ % 2) * 3)

                def mm(e, sb_=sb_, pv=pv, pz=pz, pab=pab):
                    for kc in range(8):
                        e.matmul(pv[:], lhsT=h[:, kc, sb_ * 128:(sb_ + 1) * 128], rhs=wv3[:, kc, :], start=(kc == 0), stop=(kc == 7))
                    for kc in range(8):
                        e.matmul(pz[:], lhsT=h[:, kc, sb_ * 128:(sb_ + 1) * 128], rhs=wz3[:, kc, :], start=(kc == 0), stop=(kc == 7))
                    for kc in range(8):
                        ins = e.matmul(pab[:, 0:16], lhsT=h[:, kc, sb_ * 128:(sb_ + 1) * 128], rhs=wab3[:, kc, :], start=(kc == 0), stop=(kc == 7))
                    return ins
                P.add("pe", mm, r=[wvk, wzk, wabk] + hk, w=[kpv, kpz, kpab])
                vs, kvs = vst[sb_ % 2], ("vst", sb_ % 2)

                def cpv(e, vs=vs, pv=pv):
                    v4 = vs[:].rearrange("p (j e d) -> p j e d", j=4, e=2)
                    p4 = pv[:].rearrange("p (j e d) -> p j e d", j=4, e=2)
                    e.tensor_copy(v4[:, :, 0, 0:64], p4[:, :, 0, :])
                    return e.tensor_copy(v4[:, :, 1, 64:128], p4[:, :, 1, :])
                P.add("dve", cpv, r=[kpv], w=[kvs])
                P.add("pool", lambda e, vs=vs, tk0=tk0: [e.dma_start(out=st["VP"][tk0:tk0 + 128, :], in_=vs[:])],
                      r=[kvs], w=[("VP", tk0)], slot=kvs)
                zs_, kzs = zst[sb_ % 2], ("zst", sb_ % 2)
                P.add("act", lambda e, zs_=zs_, pz=pz: e.activation(out=zs_[:], in_=pz[:], func=AF.Silu), r=[kpz], w=[kzs])
                P.add("pool", lambda e, zs_=zs_, tk0=tk0: [e.dma_start(out=st["ZS"][tk0:tk0 + 128, :], in_=zs_[:])],
                      r=[kzs], w=[("ZS", tk0)], slot=kzs)
                ga, kga = gab[sb_ % 2], ("gab", sb_ % 2)
                go, kgo = gout[sb_ % 2], ("gout", sb_ % 2)
                gc_ = st["gconst"]
                P.add("dve", lambda e, ga=ga, pab=pab: e.tensor_copy(ga[:], pab[:, 0:16]), r=[kpab], w=[kga])
                P.add("dve", lambda e, ga=ga: e.tensor_tensor(out=gt[0][:], in0=ga[:, 0:8], in1=gc_[:, l, 1, :], op=ALU.add),
                      r=[kga, "coef"], w=["gt0"])
                P.add("dve", lambda e: e.scalar_tensor_tensor(out=gt[1][:], in0=gt[0][:], scalar=-1.0, in1=gt[0][:],
                                                             op0=ALU.mult, op1=ALU.max), r=["gt0"], w=["gt1"])
                P.add("act", lambda e: e.activation(out=gt[2][:], in_=gt[1][:], func=AF.Exp, scale=-1.0), r=["gt1"], w=["gt2"])
                P.add("act", lambda e: e.activation(out=gt[3][:], in_=gt[2][:], func=AF.Ln, bias=self.epsc[:, 2:3], scale=1.0), r=["gt2"], w=["gt3"])
                P.add("dve", lambda e: e.scalar_tensor_tensor(out=gt[1][:], in0=gt[0][:], scalar=0.0, in1=gt[3][:],
                                                             op0=ALU.max, op1=ALU.add), r=["gt0", "gt3", "gt1"], w=["gt1"])
                P.add("dve", lambda e, go=go: e.tensor_tensor(out=go[:, 0:8], in0=gt[1][:], in1=gc_[:, l, 0, :], op=ALU.mult),
                      r=["gt1", "coef"], w=[(kgo, 0)])
                P.add("act", lambda e, go=go, ga=ga: e.activation(out=go[:, 8:16], in_=ga[:, 8:16], func=AF.Sigmoid),
                      r=[kga], w=[(kgo, 1)])
                P.add("pool", lambda e, go=go, tk0=tk0: [e.dma_start(out=st["GB"][tk0:tk0 + 128, :], in_=go[:])],
                      r=[(kgo, 0), (kgo, 1)], w=[("GB", tk0), (kgo, 0), (kgo, 1)], slot=kgo)
            ws.done()
            ws.done()
            ws.done()
        for t in range(NTILE):
            _tile(t)
        P.flush()
        es.close()

    def phase_C(self, l):
        nc, P, st = self.nc, self.P, self.st
        NT, SEG = self.NT, self.SEG
        C, coef = self.C, st["coef"]
        last = (l == self.depth - 1)
        es = ExitStack()
        bf = self.gemm_bufs(es)
        banks = bf["banks"]
        xt = [self.sb(es, "xt", [128, 8, 512], F32) for _ in range(2)]
        mx = [self.sb(es, "mx", [128, 8, 512], BF16) for _ in range(2)]
        yt = self.sb(es, "yt", [128, 8, 512], F32)
        ytok = [self.sb(es, "ytok", [128, 1024], F32) for _ in range(2)]
        NTILE = NT // TT
        pieces = []
        for t in range(NTILE):
            pieces += [((l, "out", 0), 4096), ((l, "out", 1), 4096)]
            pieces += [((l, "gu", 2, m), 2048) for m in range(NFC)] + [((l, "dn", 2, dc), NFC * 128) for dc in range(8)]
        ws = Builder.WStream(self, bf["wsl"], pieces)
        ident = C(0)
        def _tile(t):
            s = (t * TT) // SEG
            tok0 = t * TT
            xt_t, kx = xt[t % 2], ("xt", t % 2)
            mx_t, kmx = mx[t % 2], ("mx", t % 2)
            P.add("pool", lambda e: [e.dma_start(out=xt_t[:], in_=st["X1T"][:, :, tok0:tok0 + TT].rearrange("c p t -> p c t"))],
                  r=[("X1T", t)], w=[kx], slot=kx)
            P.add("pool", lambda e: [e.dma_start(out=mx_t[:], in_=st["MIXT"][:, :, tok0:tok0 + TT].rearrange("c p t -> p c t"))],
                  r=["MIXT"], w=[kmx], slot=kmx)
            for j in range(2):
                wt, wk = ws.get((l, "out", j))
                w4 = wt[:].rearrange("p (j k c) -> p j k c", j=4, k=8)
                for jj in range(4):
                    dc = j * 4 + jj
                    py, kpy = banks[dc % 2], ("bank", dc % 2)

                    def mm(e, w4=w4, jj=jj, py=py):
                        for kc in range(8):
                            ins = e.matmul(py[:], lhsT=w4[:, jj, kc, :], rhs=mx_t[:, kc, :], start=(kc == 0), stop=(kc == 7))
                        return ins
                    P.add("pe", mm, r=[wk, kmx], w=[kpy])
                    P.add("dve", lambda e, dc=dc, py=py: e.scalar_tensor_tensor(out=xt_t[:, dc, :], in0=py[:],
                                                                               scalar=coef[:, l, 5, dc, s:s + 1],
                                                                               in1=xt_t[:, dc, :], op0=ALU.mult, op1=ALU.add),
                          r=[kpy, kx, "coef"], w=[kx])
                ws.done()
            self.ffn(l, 2, s, xt_t, kx, bf, ws)
            if not last:
                P.add("pool", lambda e: [e.dma_start(out=st["XT"][:, :, tok0:tok0 + TT].rearrange("c p t -> p c t"), in_=xt_t[:])],
                      r=[kx], w=[("XT", t)], slot=("x1st", t % 2))
            else:
                nf = st["nfin"]
                self.norm(xt_t, kx, yt, "yt", lambda c: nf[:, c:c + 1], None, bf)
                yk = [("yt", c) for c in range(8)]
                for sb_ in range(4):
                    yk_, kyk = ytok[sb_ % 2], ("ytok", sb_ % 2)
                    for hf in range(2):
                        bk, kbk = banks[hf], ("bank", hf)

                        def tr(e, hf=hf, bk=bk, sb_=sb_):
                            for cc in range(4):
                                c = hf * 4 + cc
                                ins = e.transpose(bk[:, cc * 128:(cc + 1) * 128], yt[:, c, sb_ * 128:(sb_ + 1) * 128], ident)
                            return ins
                        P.add("pe", tr, r=yk + ["cst"], w=[kbk])
                        P.add("act", lambda e, hf=hf, bk=bk, yk_=yk_: e.copy(out=yk_[:, hf * 512:(hf + 1) * 512], in_=bk[:]),
                              r=[kbk], w=[(kyk, hf)])
                    P.add("sp", lambda e, yk_=yk_, sb_=sb_: [e.dma_start(out=st["y_out"][tok0 + sb_ * 128:tok0 + (sb_ + 1) * 128, :], in_=yk_[:])],
                          r=[(kyk, 0), (kyk, 1), "Y"], w=[(kyk, 0), (kyk, 1)], slot=kyk)
        for t in range(NTILE):
            _tile(t)
        P.flush()
        es.close()

    def phase_attn(self, l):
        nc, P, st = self.nc, self.P, self.st
        NT, SEG = self.NT, self.SEG
        CB = self.CB
        es = ExitStack()
        NBLK = NT // 128
        qt = self.sb(es, "qt", [128, NT], BF16)
        kt = self.sb(es, "kt", [128, NT], BF16)
        vpad = self.sb(es, "vpad", [128, NBLK, 256], BF16)
        acc = self.sb(es, "acc", [128, 2, NT], F32)
        pe_ = [self.sb(es, "pe", [128, 2, 384], BF16) for _ in range(2)]
        pm_ = [self.sb(es, "pm", [128, 2, 384], BF16) for _ in range(2)]
        mkv = {v: self.sb(es, "mk" + v, [128, 2, 384], BF16) for v in ("n", "pf", "nf")}
        rden = self.sb(es, "rden", [128, 512], F32)
        ost = [self.sb(es, "ost", [128, 512], BF16) for _ in range(2)]
        sps = [self.ps(es, "sps", [128, 2, 512]) for _ in range(2)]
        nd = [self.ps(es, "nd", [128, 512]) for _ in range(2)]
        flag = st["flag"]
        g1 = self.dn_stage1_gen(l, es)
        g1_state = dict(alive=True)

        def advance_dn1():
            if g1_state["alive"]:
                try:
                    next(g1)
                except StopIteration:
                    g1_state["alive"] = False

        def mkbuild(e):
            for v in ("n", "pf", "nf"):
                for hh in range(2):
                    for j in range(3):
                        dst = mkv[v][:, hh, j * 128:(j + 1) * 128]
                        if (v == "pf" and j == 0) or (v == "nf" and j == 2):
                            ins = e.tensor_scalar(dst, CB(7 + [1, 0, 2][j]), flag[:, 0:1], None, op0=ALU.mult)
                        else:
                            ins = e.tensor_copy(dst, CB(7 + [1, 0, 2][j]))
            return ins
        P.add("dve", mkbuild, r=["cstb", "flag"], w=["mkv"])
        it = 0
        for hp in range(4):
            P.add("sp", lambda e, hp=hp: [e.dma_start(out=qt[:], in_=st["QT"][hp]), e.dma_start(out=kt[:], in_=st["KT"][hp])],
                  w=["qk"], slot="qk", ndma=2)
            for pi, dil in enumerate((1, 4, 16)):
                nb = NT // dil // 128
                vsrc = st["VP"][:, hp * 256:(hp + 1) * 256].rearrange("(b p r) c -> r p b c", p=128, r=dil)
                vchunks = [(r, b0, min(b0 + 8, nb)) for r in range(dil) for b0 in range(0, nb, 8)]
                P.add("sp", lambda e, vsrc=vsrc, nb=nb, vchunks=vchunks: [e.dma_start(out=vpad[:, r * nb + b0:r * nb + b1, :], in_=vsrc[r][:, b0:b1, :])
                                                                         for (r, b0, b1) in vchunks],
                      w=["vpad"], slot="vpad", ndma=len(vchunks))
                def block_iter(r, b, par, pi=pi, dil=dil, nb=nb):

                        def tsl(blk, r=r, dil=dil):
                            s0 = r + dil * 128 * blk
                            return slice(s0, s0 + 127 * dil + 1, dil) if dil > 1 else slice(s0, s0 + 128)
                        kbs = [(j, b + j - 1) for j in range(3) if 0 <= b + j - 1 < nb]
                        j0, j1 = kbs[0][0], kbs[-1][0] + 1
                        v = "pf" if b == nb // 2 else ("nf" if b == nb // 2 - 1 else "n")
                        sp_, ksp = sps[par], ("sps", par)
                        nd_, knd = nd[par], ("nd", par)
                        pe__, kpe = pe_[par], ("pe", par)
                        pm__, kpm = pm_[par], ("pm", par)
                        qs = tsl(b)

                        def mm(e, kbs=kbs, sp_=sp_, qs=qs, tsl=tsl):
                            for hh in range(2):
                                for j, kb in kbs:
                                    ins = e.matmul(sp_[:, hh, j * 128:(j + 1) * 128], lhsT=kt[hh * 64:(hh + 1) * 64, tsl(kb)],
                                                   rhs=qt[hh * 64:(hh + 1) * 64, qs], start=True, stop=True)
                            return ins
                        P.add("pe", mm, r=["qk"], w=[ksp])
                        P.add("act", lambda e, sp_=sp_, pe__=pe__, j0=j0, j1=j1: e.activation(
                            out=pe__[:, :, j0 * 128:j1 * 128], in_=sp_[:, :, j0 * 128:j1 * 128], func=AF.Exp, scale=0.125),
                            r=[ksp], w=[kpe])
                        P.add("dve", lambda e, pe__=pe__, pm__=pm__, j0=j0, j1=j1, v=v: e.tensor_tensor(
                            out=pm__[:, :, j0 * 128:j1 * 128], in0=pe__[:, :, j0 * 128:j1 * 128],
                            in1=mkv[v][:, :, j0 * 128:j1 * 128], op=ALU.mult), r=[kpe, "mkv"], w=[kpm])

                        def mm2(e, kbs=kbs, nd_=nd_, pm__=pm__, r=r, nb=nb):
                            n = 2 * len(kbs)
                            for which in range(2):
                                i = 0
                                for hh in range(2):
                                    for j, kb in kbs:
                                        lhsT = vpad[:, r * nb + kb, hh * 128:(hh + 1) * 128] if which == 0 else CB(10 + hh)
                                        ins = e.matmul(nd_[:, which * 128:(which + 1) * 128], lhsT=lhsT,
                                                       rhs=pm__[:, hh, j * 128:(j + 1) * 128], start=(i == 0), stop=(i == n - 1))
                                        i += 1
                            return ins
                        yield
                        P.add("pe", mm2, r=[kpm, "vpad", "cstb"], w=[knd])
                        ndv = nd_[:, 0:256].rearrange("p (a q) -> p a q", a=2)
                        if pi == 0:
                            P.add("act", lambda e, ndv=ndv, qs=qs: e.copy(out=acc[:, :, qs], in_=ndv), r=[knd], w=["acc"])
                        else:
                            P.add("dve", lambda e, ndv=ndv, qs=qs: e.tensor_tensor(out=acc[:, :, qs], in0=acc[:, :, qs], in1=ndv,
                                                                                  op=ALU.add), r=[knd, "acc"], w=["acc"])
                prev = None
                for r in range(dil):
                    for b in range(nb):
                        it += 1
                        if it % 3 == 0:
                            advance_dn1()
                        g = block_iter(r, b, it % 2)
                        next(g)
                        if prev is not None:
                            for _ in prev:
                                pass
                        prev = g
                for _ in prev:
                    pass
            for tq in range(NT // 512):
                sl = slice(tq * 512, (tq + 1) * 512)
                o_, ko = ost[tq % 2], ("ost", tq % 2)
                P.add("dve", lambda e, sl=sl: e.reciprocal(rden[:], acc[:, 1, sl]), r=["acc"], w=["rden"])
                P.add("dve", lambda e, sl=sl, o_=o_: e.tensor_tensor(out=o_[:], in0=acc[:, 0, sl], in1=rden[:], op=ALU.mult),
                      r=["acc", "rden"], w=[ko])
                P.add("pool", lambda e, sl=sl, o_=o_, hp=hp: [e.dma_start(out=st["MIXT"][hp, :, sl], in_=o_[:])],
                      r=[ko], w=[ko + ("d",)], slot=ko)
        while g1_state["alive"]:
            advance_dn1()
        P.flush()
        es.close()

    def dn_stage1_gen(self, l, es):
        nc, P, st = self.nc, self.P, self.st
        NT, SEG, PADW = self.NT, self.SEG, self.st["PADW"]
        C, CB = self.C, self.CB
        ones_b = st["ones_b"]
        xin = [self.sb(es, "xin", [128, 516], F32) for _ in range(2)]
        ca = [self.sb(es, "ca", [128, 512], F32) for _ in range(2)]
        yv = [self.sb(es, "yv", [128, 512], F32) for _ in range(2)]
        sqb = self.sb(es, "sqb", [128, 512], BF16)
        rn = self.sb(es, "rn", [128, 512], F32)
        rn0 = self.sb(es, "rn0", [128, 512], F32)
        yn = [self.sb(es, "yn", [128, 512], F32) for _ in range(2)]
        tst = [self.sb(es, "tst", [128, 4, 128], F32) for _ in range(2)]
        banks = [self.ps(es, "dnbank", [128, 512]) for _ in range(2)]
        cw = st["convw"]
        it = 0
        def _tile(t):
            tok0 = t * TT
            s = tok0 // SEG
            col0 = s * PADW + (tok0 - s * SEG)
            for cidx in range(12):
                if cidx > 0:
                    yield
                par = (t * 12 + cidx) % 2
                kind, hh = cidx // 4, cidx % 4
                xi, kxi = xin[par], ("xin", par)
                P.add("sp", lambda e, xi=xi, cidx=cidx, col0=col0: [e.dma_start(out=xi[:], in_=st["PDN"][cidx, :, col0:col0 + 516])],
                      w=[kxi], slot=kxi)
                a_, ka = ca[par], ("ca", par)
                P.add("dve", lambda e, a_=a_, xi=xi, cidx=cidx: e.tensor_scalar(a_[:], xi[:, 0:512], cw[:, l, 0, cidx:cidx + 1], None, op0=ALU.mult),
                      r=[kxi, "coef"], w=[ka])
                for j in range(1, 5):
                    P.add("dve", lambda e, a_=a_, xi=xi, cidx=cidx, j=j: e.scalar_tensor_tensor(
                        out=a_[:], in0=xi[:, j:j + 512], scalar=cw[:, l, j, cidx:cidx + 1], in1=a_[:], op0=ALU.mult, op1=ALU.add),
                        r=[kxi, ka, "coef"], w=[ka])
                y_, ky = yv[par], ("yv", par)
                P.add("act", lambda e, a_=a_, y_=y_: e.activation(out=y_[:], in_=a_[:], func=AF.Silu), r=[ka], w=[ky])
                src_t = y_
                ksrc = ky
                if kind < 2:
                    P.add("act", lambda e, y_=y_: e.activation(out=sqb[:], in_=y_[:], func=AF.Square), r=[ky], w=["sqb"])
                    bk, kbk = banks[0], ("dnbank", 0)
                    P.add("pe", lambda e, bk=bk: e.matmul(bk[:], lhsT=ones_b[:], rhs=sqb[:], start=True, stop=True),
                          r=["sqb", "ones_b"], w=[kbk])
                    P.add("act", lambda e, bk=bk: e.activation(out=rn0[:], in_=bk[:], func=AF.Sqrt, bias=self.epsc[:, 1:2], scale=1.0),
                          r=[kbk], w=["rn0"])
                    P.add("dve", lambda e: e.reciprocal(rn[:], rn0[:]), r=["rn0"], w=["rn"])
                    n_, kn = yn[par], ("yn", par)
                    scl = float(DK ** -0.5) if kind == 0 else 1.0
                    P.add("dve", lambda e, n_=n_, y_=y_, scl=scl: e.scalar_tensor_tensor(out=n_[:], in0=y_[:], scalar=scl, in1=rn[:],
                                                                                        op0=ALU.mult, op1=ALU.mult),
                          r=[ky, "rn"], w=[kn])
                    dstT = st["DNQT"] if kind == 0 else st["DNKT"]
                    P.add("pool", lambda e, n_=n_, dstT=dstT, hh=hh: [e.dma_start(out=dstT[hh, :, tok0:tok0 + TT], in_=n_[:])],
                          r=[kn], w=[kn + ("d",)], slot=kn)
                    src_t, ksrc = n_, kn
                if kind >= 1:
                    bk, kbk = banks[1], ("dnbank", 1)

                    def tr(e, bk=bk, src_t=src_t):
                        for sb_ in range(4):
                            ins = e.transpose(bk[:, sb_ * 128:(sb_ + 1) * 128], src_t[:, sb_ * 128:(sb_ + 1) * 128], C(0))
                        return ins
                    P.add("pe", tr, r=[ksrc, "cst"], w=[kbk])
                    ts_, kts = tst[par], ("tst", par)
                    P.add("act", lambda e, bk=bk, ts_=ts_: e.copy(out=ts_[:], in_=bk[:].rearrange("p (a d) -> p a d", a=4)),
                          r=[kbk], w=[kts])
                    dstM = st["DNK"] if kind == 1 else st["DNV"]
                    P.add("pool", lambda e, ts_=ts_, dstM=dstM, hh=hh: [e.dma_start(
                        out=dstM[tok0:tok0 + TT, hh * 128:(hh + 1) * 128].rearrange("(a p) d -> p a d", p=128), in_=ts_[:])],
                        r=[kts], w=[kts + ("d",)], slot=kts)
        for t in range(NT // TT):
            yield from _tile(t)
            yield

    def phase_dn(self, l):
        nc, P, st = self.nc, self.P, self.st
        NT, SEG, PADW = self.NT, self.SEG, self.st["PADW"]
        C, CB = self.C, self.CB
        flag = st["flag"]
        ones_b = st["ones_b"]
        if self.debug == "dn1":
            return
        es = ExitStack()
        NCH = NT // 128

        def T2(name, shape=(128, 512), dt=F32, n=2):
            return [self.sb(es, name, list(shape), dt) for _ in range(n)]
        kT4, qT4, k4, v4 = T2("kT4"), T2("qT4"), T2("k4"), T2("v4")
        gb = T2("gb", (128, 16))
        eg = T2("eg", (128, 12))
        bege = T2("bege", (128, 4))
        G4, nabs, E, EA, tA, R, Pm, X = T2("G4"), T2("nabs"), T2("E"), T2("EA"), T2("tA"), T2("R", n=4), T2("Pm", n=4), T2("X")
        Vb4, Kbg4, kdec4, u4, nw4, EQ, qk4, vn4, o1s, o4 = (T2("Vb4"), T2("Kbg4"), T2("kdec4"), T2("u4"), T2("nw4"), T2("EQ"),
                                                           T2("qk4"), T2("vn4"), T2("o1s"), T2("o4"))
        S = T2("S")
        m4 = {nm: self.sb(es, "m4" + nm, [128, 512], F32) for nm in ("Ui", "Li", "Us", "Ls", "I", "bd", "o16", "o32", "o64")}
        Ad, Aoff = T2("Ad"), [T2("Ao16"), T2("Ao32"), T2("Ao64")]
        Wt, M1 = T2("Wt"), T2("M1")
        banks = [self.ps(es, "bank", [128, 512]) for _ in range(8)]
        bstate = dict(i=0)

        def nbank():
            i = bstate["i"] % 8
            bstate["i"] += 1
            return banks[i], ("bank", i)

        def v3(t_):
            return t_[:].rearrange("p (h d) -> p h d", h=4)

        def bc(ap4):
            return ap4.unsqueeze(2).broadcast_to([128, 4, 128])

        def m4build(e):
            for nm, ci in (("Ui", 2), ("Li", 3), ("Us", 4), ("Ls", 5), ("I", 0), ("bd", 12), ("o16", 13), ("o32", 14), ("o64", 15)):
                for hh in range(4):
                    ins = e.tensor_copy(m4[nm][:, hh * 128:(hh + 1) * 128], C(ci))
            return ins
        P.add("dve", m4build, r=["cst"], w=["m4"])
        for d in range(2):
            if DN_R != "none":
                P.add("dve", lambda e, d=d: e.tensor_scalar(S[d][:].bitcast(F32R), m4["I"][:], 0.0, None, op0=ALU.mult), r=["m4"], w=[("S", d)])
            else:
                P.add("dve", lambda e, d=d: e.memset(S[d][:], 0.0), w=[("S", d)])

        cut = float(self.debug.split(":")[1]) if (self.debug and self.debug.startswith("dn2:")) else None

        def step(c, d, par):
            rO = (lambda a: a.bitcast(F32R)) if DN_R in ("outer", "all") else (lambda a: a)
            rI = (lambda a: a.bitcast(F32R)) if DN_R == "all" else (lambda a: a)
            wO = rO
            wI = rI
            wX = rO
            Mincl = C(2) if d == 0 else C(3)
            Mrem = C(5) if d == 0 else C(4)
            MA4 = m4["Ls"] if d == 0 else m4["Us"]
            MQ4 = m4["Ui"] if d == 0 else m4["Li"]
            tk = slice(c * 128, (c + 1) * 128)
            K = lambda nm: (nm, par)
            P.add("sp", lambda e: [e.dma_start(out=v3(kT4[par]), in_=st["DNKT"][:, :, tk].rearrange("h p t -> p h t")),
                                   e.dma_start(out=v3(qT4[par]), in_=st["DNQT"][:, :, tk].rearrange("h p t -> p h t")),
                                   e.dma_start(out=k4[par][:], in_=st["DNK"][tk, :]),
                                   e.dma_start(out=v4[par][:], in_=st["DNV"][tk, :]),
                                   e.dma_start(out=gb[par][:], in_=st["GB"][tk, :])],
                  w=[K("ld")], slot=K("ld"), ndma=5)
            g4 = gb[par][:, d * 4:(d + 1) * 4]
            beta4 = gb[par][:, 8 + d * 4:8 + (d + 1) * 4]
            gp, kgp = nbank()

            def mm1(e):
                e.matmul(gp[:, 0:4], lhsT=Mincl, rhs=g4, start=True, stop=True)
                e.matmul(gp[:, 4:8], lhsT=Mrem, rhs=g4, start=True, stop=True)
                return e.matmul(gp[:, 8:12], lhsT=C(1), rhs=g4, start=True, stop=True)
            P.add("pe", mm1, r=[K("ld"), "cst"], w=[kgp])
            P.add("act", lambda e: e.activation(out=eg[par][:], in_=gp[:, 0:12], func=AF.Exp), r=[kgp], w=[K("eg")])
            egc, erem, etot = eg[par][:, 0:4], eg[par][:, 4:8], eg[par][:, 8:12]
            P.add("dve", lambda e: e.tensor_tensor(out=bege[par][:], in0=beta4, in1=egc, op=ALU.mult), r=[K("ld"), K("eg")], w=[K("bege")])
            yield
            if cut is not None and cut <= 1:
                return
            P.add("dve", lambda e: e.tensor_tensor(out=v3(G4[par]), in0=v3(m4["Ui" if d == 0 else "Li"]), in1=bc(g4), op=ALU.mult),
                  r=[K("ld"), "m4"], w=[K("G4")])
            Dp, kDp = nbank()

            def mm2(e):
                for hh in range(4):
                    sl = slice(hh * 128, (hh + 1) * 128)
                    e.matmul(Dp[:, sl], lhsT=G4[par][:, sl], rhs=C(1), start=True, stop=False)
                    ins = e.matmul(Dp[:, sl], lhsT=C(6), rhs=G4[par][:, sl], start=False, stop=True)
                return ins
            P.add("pe", mm2, r=[K("G4"), "cst"], w=[kDp])
            P.add("dve", lambda e: e.tensor_scalar(tA[par][:], Dp[:], 0.0, None, op0=ALU.min), r=[kDp, K("tA")], w=[K("tA")])
            P.add("dve", lambda e: e.scalar_tensor_tensor(out=nabs[par][:], in0=Dp[:], scalar=0.0, in1=tA[par][:], op0=ALU.max, op1=ALU.subtract),
                  r=[kDp, K("tA")], w=[K("nabs")])
            P.add("act", lambda e: e.activation(out=E[par][:], in_=nabs[par][:], func=AF.Exp, scale=-1.0), r=[K("nabs")], w=[K("E")])
            yield
            if cut is not None and cut <= 2:
                return
            kk, kkk = nbank()

            def mm3(e):
                for hh in range(4):
                    sl = slice(hh * 128, (hh + 1) * 128)
                    ins = e.matmul(kk[:, sl], lhsT=rO(kT4[par][:, sl]), rhs=rO(kT4[par][:, sl]), start=True, stop=True)
                return ins
            P.add("pe", mm3, r=[K("ld")], w=[kkk])
            P.add("dve", lambda e: e.tensor_tensor(out=EA[par][:], in0=E[par][:], in1=MA4[:], op=ALU.mult), r=[K("E"), "m4"], w=[K("EA")])
            P.add("dve", lambda e: e.tensor_tensor(out=tA[par][:], in0=kk[:], in1=EA[par][:], op=ALU.mult), r=[kkk, K("EA")], w=[K("tA")])
            r0 = R[par * 2]
            P.add("dve", lambda e: e.tensor_tensor(out=wI(v3(r0)), in0=v3(tA[par]), in1=bc(beta4), op=ALU.mult),
                  r=[K("tA"), K("ld")], w=[("R", par * 2)])
            yield
            if cut is not None and cut <= 3:
                return
            P.add("dve", lambda e: e.tensor_tensor(out=wI(Ad[par][:]), in0=r0[:], in1=m4["bd"][:], op=ALU.mult),
                  r=[("R", par * 2), "m4"], w=[K("Ad")])
            for li, nm in enumerate(("o16", "o32", "o64")):
                P.add("dve", lambda e, li=li, nm=nm: e.tensor_tensor(out=wI(Aoff[li][par][:]), in0=r0[:], in1=m4[nm][:], op=ALU.mult),
                      r=[("R", par * 2), "m4"], w=[K("Ao%d" % li)])
            Bp, kBp = nbank()

            def mm4(e):
                for hh in range(4):
                    sl = slice(hh * 128, (hh + 1) * 128)
                    ins = e.transpose(Bp[:, sl], Ad[par][:, sl], C(0))
                return ins
            P.add("pe", mm4, r=[K("Ad"), "cst"], w=[kBp])
            p0 = Pm[par * 2]
            P.add("act", lambda e: e.copy(out=wI(p0[:]), in_=Bp[:]), r=[kBp], w=[("Pm", par * 2)])
            P.add("dve", lambda e: e.scalar_tensor_tensor(out=wX(X[par][:]), in0=p0[:], scalar=-1.0, in1=m4["I"][:], op0=ALU.mult, op1=ALU.add),
                  r=[("Pm", par * 2), "m4"], w=[K("X")])
            yield
            if cut is not None and cut <= 3.2:
                return
            Rb = [(Ad[par], K("Ad")), (R[par * 2 + 1], ("R", par * 2 + 1)), (R[par * 2], ("R", par * 2)), (R[par * 2 + 1], ("R", par * 2 + 1))]
            Pb = [(Pm[par * 2], ("Pm", par * 2)), (Pm[par * 2 + 1], ("Pm", par * 2 + 1)), (Pm[par * 2], ("Pm", par * 2))]
            NLEV = 3
            for lev in range(1, NLEV + 1):
                (Rc, kRc), (Pc, kPc) = Rb[lev - 1], Pb[lev - 1]
                Rn, kRn = Rb[lev]
                Pp, kPp = nbank()
                Rp, kRp = nbank()

                def mm5(e, Rc=Rc, Pc=Pc, Pp=Pp, Rp=Rp, lev=lev):
                    for hh in range(4):
                        sl = slice(hh * 128, (hh + 1) * 128)
                        if lev < NLEV:
                            e.matmul(Pp[:, sl], lhsT=rI(Rc[:, sl]), rhs=rI(Pc[:, sl]), start=True, stop=True)
                        ins = e.matmul(Rp[:, sl], lhsT=rI(Pc[:, sl]), rhs=rI(Rc[:, sl]), start=True, stop=True)
                    return ins
                P.add("pe", mm5, r=[kRc, kPc], w=([kPp, kRp] if lev < NLEV else [kRp]))
                if lev < NLEV:
                    Pn, kPn = Pb[lev]
                    P.add("act", lambda e, Pn=Pn, Pp=Pp: e.copy(out=wI(Pn[:]), in_=Pp[:]), r=[kPp], w=[kPn])
                P.add("dve", lambda e, Rn=Rn, Rp=Rp: e.tensor_copy(wI(Rn[:]), Rp[:]), r=[kRp], w=[kRn])
                Xp, kXp = nbank()

                def mm6(e, Rn=Rn, Xp=Xp):
                    for hh in range(4):
                        sl = slice(hh * 128, (hh + 1) * 128)
                        ins = e.matmul(Xp[:, sl], lhsT=rI(Rn[:, sl]), rhs=rI(X[par][:, sl]), start=True, stop=True)
                    return ins
                P.add("pe", mm6, r=[kRn, K("X")], w=[kXp])
                P.add("dve", lambda e, Xp=Xp: e.tensor_tensor(out=wX(X[par][:]), in0=X[par][:], in1=Xp[:], op=ALU.add),
                      r=[kXp, K("X")], w=[K("X")])
                yield
                if cut is not None and cut <= 3.2 + 0.2 * lev:
                    return
            yield
            if cut is not None and cut <= 4:
                return
            for li in range(3):
                Wp, kWp = nbank()

                def mmw(e, Wp=Wp):
                    for hh in range(4):
                        sl = slice(hh * 128, (hh + 1) * 128)
                        ins = e.transpose(Wp[:, sl], X[par][:, sl], C(0))
                    return ins
                P.add("pe", mmw, r=[K("X"), "cst"], w=[kWp])
                P.add("act", lambda e, Wp=Wp: e.copy(out=wI(Wt[par][:]), in_=Wp[:]), r=[kWp], w=[K("Wt")])
                M1p, kM1p = nbank()

                def mmm(e, M1p=M1p, li=li):
                    for hh in range(4):
                        sl = slice(hh * 128, (hh + 1) * 128)
                        ins = e.matmul(M1p[:, sl], lhsT=rI(Aoff[li][par][:, sl]), rhs=rI(X[par][:, sl]), start=True, stop=True)
                    return ins
                P.add("pe", mmm, r=[K("Ao%d" % li), K("X")], w=[kM1p])
                P.add("act", lambda e, M1p=M1p: e.copy(out=wI(M1[par][:]), in_=M1p[:]), r=[kM1p], w=[K("M1")])
                X2p, kX2p = nbank()

                def mmx(e, X2p=X2p):
                    for hh in range(4):
                        sl = slice(hh * 128, (hh + 1) * 128)
                        ins = e.matmul(X2p[:, sl], lhsT=rI(Wt[par][:, sl]), rhs=rI(M1[par][:, sl]), start=True, stop=True)
                    return ins
                P.add("pe", mmx, r=[K("Wt"), K("M1")], w=[kX2p])
                P.add("dve", lambda e, X2p=X2p: e.tensor_tensor(out=wX(X[par][:]), in0=X[par][:], in1=X2p[:], op=ALU.subtract),
                      r=[kX2p, K("X")], w=[K("X")])
                yield
            yield
            if cut is not None and cut <= 5:
                return
            P.add("dve", lambda e: e.tensor_tensor(out=wO(v3(Vb4[par])), in0=v3(v4[par]), in1=bc(beta4), op=ALU.mult), r=[K("ld")], w=[K("Vb4")])
            P.add("dve", lambda e: e.tensor_tensor(out=wO(v3(Kbg4[par])), in0=v3(k4[par]), in1=bc(bege[par][:]), op=ALU.mult),
                  r=[K("ld"), K("bege")], w=[K("Kbg4")])
            P.add("dve", lambda e: e.tensor_tensor(out=wO(v3(kdec4[par])), in0=v3(k4[par]), in1=bc(erem), op=ALU.mult),
                  r=[K("ld"), K("eg")], w=[K("kdec4")])
            up, kup = nbank()
            wp, kwp = nbank()

            def mm7(e):
                for hh in range(4):
                    sl = slice(hh * 128, (hh + 1) * 128)
                    e.matmul(up[:, sl], lhsT=rO(X[par][:, sl]), rhs=rO(Vb4[par][:, sl]), start=True, stop=True)
                    ins = e.matmul(wp[:, sl], lhsT=rO(Kbg4[par][:, sl]), rhs=rO(X[par][:, sl]), start=True, stop=True)
                return ins
            P.add("pe", mm7, r=[K("X"), K("Vb4"), K("Kbg4")], w=[kup, kwp])
            P.add("act", lambda e: e.copy(out=u4[par][:], in_=up[:]), r=[kup], w=[K("u4")])
            P.add("act", lambda e: e.mul(out=wO(nw4[par][:]), in_=wp[:], mul=-1.0), r=[kwp], w=[K("nw4")])
            yield
            if cut is not None and cut <= 6:
                return
            qkp, kqkp = nbank()

            def mm8(e):
                for hh in range(4):
                    sl = slice(hh * 128, (hh + 1) * 128)
                    ins = e.matmul(qkp[:, sl], lhsT=rO(kT4[par][:, sl]), rhs=rO(qT4[par][:, sl]), start=True, stop=True)
                return ins
            P.add("pe", mm8, r=[K("ld")], w=[kqkp])
            P.add("dve", lambda e: e.tensor_tensor(out=EQ[par][:], in0=E[par][:], in1=MQ4[:], op=ALU.mult), r=[K("E"), "m4"], w=[K("EQ")])
            P.add("dve", lambda e: e.tensor_tensor(out=wO(qk4[par][:]), in0=qkp[:], in1=EQ[par][:], op=ALU.mult), r=[kqkp, K("EQ")], w=[K("qk4")])
            yield
            if cut is not None and cut <= 7:
                return
            Sd, kS = S[d], ("S", d)
            link = (c == NCH // 2) if d == 0 else (c == NCH // 2 - 1)
            if link:
                P.add("dve", lambda e: e.tensor_scalar(wO(Sd[:]), Sd[:], flag[:, 0:1], None, op0=ALU.mult), r=[kS, "flag"], w=[kS])
            vnp, kvnp = nbank()
            O1p, kO1p = nbank()

            def mm9(e):
                for hh in range(4):
                    sl = slice(hh * 128, (hh + 1) * 128)
                    e.matmul(vnp[:, sl], lhsT=rO(nw4[par][:, sl]), rhs=rO(Sd[:, sl]), start=True, stop=True)
                    ins = e.matmul(O1p[:, sl], lhsT=rO(qT4[par][:, sl]), rhs=rO(Sd[:, sl]), start=True, stop=True)
                return ins
            P.add("pe", mm9, r=[K("nw4"), K("ld"), kS], w=[kvnp, kO1p])
            P.add("dve", lambda e: e.tensor_tensor(out=wO(vn4[par][:]), in0=vnp[:], in1=u4[par][:], op=ALU.add), r=[kvnp, K("u4")], w=[K("vn4")])
            P.add("dve", lambda e: e.tensor_tensor(out=v3(o1s[par]), in0=O1p[:].rearrange("p (h d) -> p h d", h=4), in1=bc(egc), op=ALU.mult),
                  r=[kO1p, K("eg")], w=[K("o1s")])
            yield
            O2p, kO2p = nbank()
            dSp, kdSp = nbank()

            def mm10(e):
                for hh in range(4):
                    sl = slice(hh * 128, (hh + 1) * 128)
                    e.matmul(O2p[:, sl], lhsT=rO(qk4[par][:, sl]), rhs=rO(vn4[par][:, sl]), start=True, stop=True)
                    ins = e.matmul(dSp[:, sl], lhsT=rO(kdec4[par][:, sl]), rhs=rO(vn4[par][:, sl]), start=True, stop=True)
                return ins
            P.add("pe", mm10, r=[K("qk4"), K("vn4"), K("kdec4")], w=[kO2p, kdSp])
            P.add("dve", lambda e: e.tensor_tensor(out=o4[par][:], in0=O2p[:], in1=o1s[par][:], op=ALU.add), r=[kO2p, K("o1s")], w=[K("o4")])
            P.add("pool", lambda e: [e.dma_start(out=st["OFB"][d, tk, :], in_=o4[par][:])], r=[K("o4")], w=[K("o4d")], slot=K("o4"))
            P.add("dve", lambda e: e.tensor_tensor(out=wO(v3(Sd)), in0=v3(Sd), in1=bc(etot), op=ALU.mult), r=[kS, K("eg")], w=[kS])
            P.add("dve", lambda e: e.tensor_tensor(out=wO(Sd[:]), in0=Sd[:], in1=dSp[:], op=ALU.add), r=[kS, kdSp], w=[kS])

        for sidx in range(NCH if cut is None else 1):
            gens = [step(sidx, 0, 0)] + ([step(NCH - 1 - sidx, 1, 1)] if cut is None else [])
            while gens:
                for g in list(gens):
                    try:
                        next(g)
                    except StopIteration:
                        gens.remove(g)
        P.flush()
        es.close()

        if self.debug and self.debug.startswith("dn2"):
            return
        es = ExitStack()
        of_ = [self.sb(es, "of", [128, 512], F32) for _ in range(2)]
        ob_ = [self.sb(es, "ob", [128, 512], F32) for _ in range(2)]
        zs_ = [self.sb(es, "zs", [128, 512], F32) for _ in range(2)]
        osum = self.sb(es, "osum", [128, 512], F32)
        osq = self.sb(es, "osq", [128, 512], F32)
        ss = self.sb(es, "ss", [128, 4], F32)
        rs = self.sb(es, "rs", [128, 4], F32)
        y1 = self.sb(es, "y1", [128, 512], F32)
        y2 = self.sb(es, "y2", [128, 512], F32)
        y3 = self.sb(es, "y3", [128, 512], F32)
        mst = [self.sb(es, "mst", [128, 4, 128], BF16) for _ in range(2)]
        banks = [self.ps(es, "bank", [128, 512]) for _ in range(2)]
        dnw = st["dnw"]

        def v3b(t_):
            return t_[:].rearrange("p (h d) -> p h d", h=4)
        for t in range(NT // 128):
            par = t % 2
            tk = slice(t * 128, (t + 1) * 128)
            kl = ("s3ld", par)
            P.add("sp", lambda e, par=par, tk=tk: [e.dma_start(out=of_[par][:], in_=st["OFB"][0, tk, :]),
                                                  e.dma_start(out=ob_[par][:], in_=st["OFB"][1, tk, :]),
                                                  e.dma_start(out=zs_[par][:], in_=st["ZS"][tk, :])],
                  w=[kl], slot=kl, ndma=3)
            P.add("dve", lambda e, par=par: e.tensor_tensor(out=osum[:], in0=of_[par][:], in1=ob_[par][:], op=ALU.add), r=[kl], w=["osum"])
            P.add("dve", lambda e: e.tensor_tensor(out=osq[:], in0=osum[:], in1=osum[:], op=ALU.mult), r=["osum"], w=["osq"])
            P.add("dve", lambda e: e.reduce_sum(out=ss[:], in_=v3b(osq), axis=AX.X), r=["osq"], w=["ss"])
            P.add("dve", lambda e: e.tensor_scalar(rs[:], ss[:], 1.0 / 128.0, float(EPS), op0=ALU.mult, op1=ALU.add), r=["ss"], w=["rs0"])
            P.add("act", lambda e: e.activation(out=ss[:], in_=rs[:], func=AF.Sqrt), r=["rs0", "ss"], w=["ss"])
            P.add("dve", lambda e: e.reciprocal(rs[:], ss[:]), r=["ss", "rs0"], w=["rs"])
            P.add("dve", lambda e: e.tensor_tensor(out=v3b(y1), in0=v3b(osum), in1=rs[:].unsqueeze(2).broadcast_to([128, 4, 128]), op=ALU.mult),
                  r=["osum", "rs"], w=["y1"])
            P.add("dve", lambda e: e.tensor_tensor(out=y2[:], in0=y1[:], in1=dnw[:, l, :], op=ALU.mult), r=["y1", "coef"], w=["y2"])
            P.add("dve", lambda e, par=par: e.tensor_tensor(out=y3[:], in0=y2[:], in1=zs_[par][:], op=ALU.mult), r=["y2", kl], w=["y3"])
            bk, kbk = banks[par], ("bank", par)

            def tr(e, bk=bk):
                for hh in range(4):
                    ins = e.transpose(bk[:, hh * 128:(hh + 1) * 128], y3[:, hh * 128:(hh + 1) * 128], C(0))
                return ins
            P.add("pe", tr, r=["y3", "cst"], w=[kbk])
            km = ("mst", par)
            P.add("act", lambda e, bk=bk, par=par: e.copy(out=mst[par][:], in_=bk[:].rearrange("p (h d) -> p h d", h=4)), r=[kbk], w=[km])
            P.add("pool", lambda e, par=par, tk=tk: [e.dma_start(out=st["MIXT"][4:8, :, tk].rearrange("h p t -> p h t"), in_=mst[par][:])],
                  r=[km], w=[km + ("d",)], slot=km)
        P.flush()
        es.close()


def make_consts():
    k = np.arange(128)[:, None]
    m = np.arange(128)[None, :]
    t = np.zeros((16, 128, 128), np.float32)
    t[0] = (k == m)
    t[1] = 1.0
    t[2] = (k <= m)
    t[3] = (k >= m)
    t[4] = (k < m)
    t[5] = (k > m)
    t[6] = -1.0
    t[7] = (np.abs(k - m) <= 64)
    t[8] = (k >= m + 64)
    t[9] = (k <= m - 64)
    t[10] = (m < 64) * np.ones((128, 1))
    t[11] = (m >= 64) * np.ones((128, 1))
    t[12] = (k // 16 == m // 16)
    t[13] = (k // 32 == m // 32) & (k // 16 != m // 16)
    t[14] = (k // 64 == m // 64) & (k // 32 != m // 32)
    t[15] = (k // 64 != m // 64)
    return np.ascontiguousarray(t.transpose(1, 0, 2).reshape(128, 16 * 128)).astype(np.float32)


def make_rope(pos):
    half = 8
    inv = (np.float32(ROPE_THETA) ** (-(np.arange(half, dtype=np.float32) / np.float32(half)))).astype(np.float32)
    ang = pos.astype(np.float32)[None, :] * inv[:, None]
    cos = np.cos(ang).astype(np.float32)
    sin = np.sin(ang).astype(np.float32)
    NT = pos.shape[0]
    cosT = np.ones((128, NT), np.float32)
    sinT = np.zeros((128, NT), np.float32)
    for e in range(2):
        cosT[e * 64:e * 64 + 8] = cos
        cosT[e * 64 + 8:e * 64 + 16] = cos
        sinT[e * 64:e * 64 + 8] = -sin
        sinT[e * 64 + 8:e * 64 + 16] = sin
    return cosT, sinT


_WNAMES = ["ada_w", "ada_b", "norm_ffn1", "ffn1_w_gate", "ffn1_w_up", "ffn1_w_down", "norm_mix", "w_in", "conv_w",
           "a_log", "dt_bias", "dn_norm", "w_out", "norm_ffn2", "ffn2_w_gate", "ffn2_w_up", "ffn2_w_down", "norm_final"]


def core_inputs(xc, c2, cont, weights, depth):
    NT = xc.shape[0]
    seg = NT // 2
    pos = np.arange(NT) if cont else (np.arange(NT) % seg)
    cosT, sinT = make_rope(pos)
    m = {"x": np.ascontiguousarray(xc, dtype=np.float32), "c2": np.ascontiguousarray(c2, dtype=np.float32),
         "flag": np.full((128, 1), 1.0 if cont else 0.0, np.float32), "cosT": cosT, "sinT": sinT,
         "consts": make_consts()}
    for n in _WNAMES:
        w = np.asarray(weights[n], dtype=np.float32)
        if n in ("a_log", "dt_bias"):
            w = w.reshape(depth, 8)
        if n == "norm_final":
            w = w.reshape(1, D)
        m[n] = np.ascontiguousarray(w)
    return m


_NC_CACHE = {}


def kernel(**inputs):
    xp = np.asarray(inputs["x_prompt"], dtype=np.float32)
    xs = np.asarray(inputs["x_sample"], dtype=np.float32)
    cp = np.asarray(inputs["c_prompt"], dtype=np.float32)
    cs = np.asarray(inputs["c_sample"], dtype=np.float32)
    depth = np.asarray(inputs["ada_w"]).shape[0]
    Bp, Sp, _ = xp.shape
    Bs, Ss, _ = xs.shape
    seg = Sp
    assert Ss == 2 * Sp and Bp == 2 * Bs and Bp + Bs * 2 == 16 or True
    in_maps = []
    npc = Bp // 2
    for i in range(npc):
        xc = xp[2 * i:2 * i + 2].reshape(2 * Sp, D)
        in_maps.append(core_inputs(xc, cp[2 * i:2 * i + 2], False, inputs, depth))
    for i in range(Bs):
        xc = xs[i]
        in_maps.append(core_inputs(xc, np.stack([cs[i], cs[i]]), True, inputs, depth))
    key = (seg, depth)
    if key not in _NC_CACHE:
        _NC_CACHE[key] = Builder(seg, depth).build()
    nc = _NC_CACHE[key]
    res = run_bass_kernel_spmd(nc, in_maps, core_ids=list(range(len(in_maps))))
    ys = [r["y"] for r in res.results]
    y_prompt = np.stack(ys[:npc]).reshape(Bp, Sp, D).astype(np.float32)
    y_sample = np.stack(ys[npc:]).reshape(Bs, Ss, D).astype(np.float32)
    return (y_prompt, y_sample)
```

```python
import numpy as np
from contextlib import ExitStack
import concourse.bass as bass
import concourse.mybir as mybir
from concourse.bass_utils import run_bass_kernel_spmd

F32 = mybir.dt.float32
BF16 = mybir.dt.bfloat16
AF = mybir.ActivationFunctionType
ALU = mybir.AluOpType
AX = mybir.AxisListType

D = 1024
FF = 2816
NFC = FF // 128
NH = 8
HD = 64
NDH = 4
DK = 128
INW = 3600
EPS = 1e-6
ROPE_THETA = 500000.0
TT = 512
NEG = -30000.0
F32R = mybir.dt.float32r
DN_R = "none"


class Prog:
    CENG = ("pe", "act", "dve", "pool")

    def __init__(self, nc, es):
        self.nc = nc
        self.es = es
        self.ops = []
        self.last_w = {}
        self.readers = {}
        self.flushed = 0
        self.sems = {}
        self.cnt = {}
        self.seen = {}
        self.nblock = 0

    def _sem(self, key):
        if key not in self.sems:
            self.sems[key] = self.es.enter_context(self.nc.semaphore("s%d" % len(self.sems)))
            self.cnt[key] = 0
        return self.sems[key]

    def add(self, eng, fn, r=(), w=(), slot=None, ndma=1):
        idx = len(self.ops)
        hard, war = set(), set()
        for k in r:
            if k in self.last_w:
                hard.add(self.last_w[k])
        for k in w:
            if k in self.last_w:
                hard.add(self.last_w[k])
            for i in self.readers.get(k, ()):
                war.add(i)
        for k in w:
            self.last_w[k] = idx
            self.readers[k] = []
        for k in r:
            if k not in w:
                self.readers.setdefault(k, []).append(idx)
        deps = set()
        isdma = slot is not None
        for d in hard:
            od = self.ops[d]
            if od["eng"] == eng and eng == "pe" and not isdma:
                continue
            deps.add(d)
        for d in war:
            od = self.ops[d]
            if od["eng"] == eng and eng == "pe" and not isdma and od["slot"] is None:
                continue
            deps.add(d)
        deps.discard(idx)
        deps = {d for d in deps if d >= self.flushed}
        for d in deps:
            self.ops[d]["users"] = True
        self.ops.append(dict(eng=eng, fn=fn, deps=deps, slot=slot, ndma=ndma, users=False, sig=None))
        return idx

    def flush(self):
        nc = self.nc
        ops = self.ops[self.flushed:]
        slot_map = {}
        for op in ops:
            if op["slot"] is not None:
                sk = (op["eng"], op["slot"])
                if sk not in slot_map:
                    slot_map[sk] = sum(1 for k2 in slot_map if k2[0] == op["eng"])
                key = ("dmap", op["eng"], slot_map[sk])
                self._sem(key)
                self.cnt[key] += 16 * op["ndma"]
                op["sig"] = (key, self.cnt[key])
            elif op["users"]:
                key = ("eng", op["eng"])
                self._sem(key)
                self.cnt[key] += 1
                op["sig"] = (key, self.cnt[key])
        engs = {"pe": "tensor", "act": "scalar", "dve": "vector", "pool": "gpsimd", "sp": "sync"}
        with nc.Block() as block:
            for eng, attr in engs.items():
                mine = [op for op in ops if op["eng"] == eng]

                def body(e, mine=mine, eng=eng):
                    seen = self.seen.setdefault(eng, {})
                    for op in mine:
                        need = {}
                        for d in op["deps"]:
                            sg = self.ops[d]["sig"]
                            assert sg is not None
                            if need.get(sg[0], 0) < sg[1]:
                                need[sg[0]] = sg[1]
                        for k, v in need.items():
                            if seen.get(k, 0) < v:
                                e.wait_ge(self.sems[k], v)
                                seen[k] = v
                        res = op["fn"](e)
                        if op["slot"] is not None:
                            assert res is not None and len(res) == op["ndma"], (len(res), op["ndma"])
                            for ins in res:
                                ins.then_inc(self.sems[op["sig"][0]], 16)
                        elif op["sig"] is not None:
                            ins = res[-1] if isinstance(res, (list, tuple)) else res
                            ins.then_inc(self.sems[op["sig"][0]], 1)
                        op["fn"] = None
                    for k, v in self.cnt.items():
                        if seen.get(k, 0) < v:
                            e.wait_ge(self.sems[k], v)
                            seen[k] = v
                getattr(block, attr)(body)
        self.flushed = len(self.ops)


class Builder:
    def __init__(self, seg, depth, debug=None):
        self.SEG = seg
        self.NT = 2 * seg
        self.depth = depth
        self.debug = debug
        self.nc = bass.Bass("TRN2", target_bir_lowering=False)
        try:
            self.nc.allow_low_precision("bf16 matmuls by design")
        except Exception:
            pass
        try:
            self.nc.allow_non_contiguous_dma("small strided param loads")
        except Exception:
            pass
        self.uid = 0

    def dram(self, name, shape, dt, kind="Internal"):
        return self.nc.dram_tensor(name, list(shape), dt, kind=kind).ap()

    def sb(self, es, name, shape, dt):
        self.uid += 1
        return es.enter_context(self.nc.sbuf_tensor("%s_%d" % (name, self.uid), list(shape), dt))

    def ps(self, es, name, shape, dt=F32):
        self.uid += 1
        return es.enter_context(self.nc.psum_tensor("%s_%d" % (name, self.uid), list(shape), dt))

    def build(self):
        nc = self.nc
        NT, SEG, depth = self.NT, self.SEG, self.depth
        L = depth
        inp = {}

        def ext(name, shape, dt=F32):
            inp[name] = self.dram(name, shape, dt, kind="ExternalInput")
            return inp[name]

        x_in = ext("x", [NT, D])
        c_in = ext("c2", [2, D])
        flag_in = ext("flag", [128, 1])
        cos_in = ext("cosT", [128, NT])
        sin_in = ext("sinT", [128, NT])
        cst_in = ext("consts", [128, 16 * 128])
        ada_w = ext("ada_w", [L, D, 9 * D])
        ada_b = ext("ada_b", [L, 9 * D])
        nrm1 = ext("norm_ffn1", [L, D])
        w1g = ext("ffn1_w_gate", [L, D, FF])
        w1u = ext("ffn1_w_up", [L, D, FF])
        w1d = ext("ffn1_w_down", [L, FF, D])
        nrm2 = ext("norm_mix", [L, D])
        w_in = ext("w_in", [L, D, INW])
        conv_w = ext("conv_w", [L, 5, 1536])
        a_log = ext("a_log", [L, 8])
        dt_bias = ext("dt_bias", [L, 8])
        dn_norm = ext("dn_norm", [L, 128])
        w_out = ext("w_out", [L, D, D])
        nrm3 = ext("norm_ffn2", [L, D])
        w2g = ext("ffn2_w_gate", [L, D, FF])
        w2u = ext("ffn2_w_up", [L, D, FF])
        w2d = ext("ffn2_w_down", [L, FF, D])
        nrmf = ext("norm_final", [1, D])
        y_out = self.dram("y", [NT, D], F32, kind="ExternalOutput")
        self.dbg_out = {}

        PADW = SEG + 4
        XT = self.dram("XT", [8, 128, NT], F32)
        X1T = self.dram("X1T", [8, 128, NT], F32)
        QT = self.dram("QT", [4, 128, NT], BF16)
        KT = self.dram("KT", [4, 128, NT], BF16)
        VP = self.dram("VP", [NT, 1024], BF16)
        PDN = self.dram("PDN", [12, 128, 2 * PADW], F32)
        ZS = self.dram("ZS", [NT, 512], F32)
        GB = self.dram("GB", [NT, 16], F32)
        MIXT = self.dram("MIXT", [8, 128, NT], BF16)
        DNQT = self.dram("DNQT", [4, 128, NT], F32)
        DNKT = self.dram("DNKT", [4, 128, NT], F32)
        DNK = self.dram("DNK", [NT, 512], F32)
        DNV = self.dram("DNV", [NT, 512], F32)
        OFB = self.dram("OFB", [2, NT, 512], F32)
        WS = {}
        for l in range(L):
            for f in (1, 2):
                for m in range(NFC):
                    WS[(l, "gu", f, m)] = self.dram("wgu%d_%d_%d" % (l, f, m), [128, 2048], BF16)
                for dc in range(8):
                    WS[(l, "dn", f, dc)] = self.dram("wdn%d_%d_%d" % (l, f, dc), [128, NFC * 128], BF16)
            for j in range(5):
                WS[(l, "inF", j)] = self.dram("winF%d_%d" % (l, j), [128, 4096], BF16)
            for j in range(2):
                WS[(l, "inP", j)] = self.dram("winP%d_%d" % (l, j), [128, 4096], BF16)
            for j in range(2):
                WS[(l, "inT", j)] = self.dram("winT%d_%d" % (l, j), [128, 4096], BF16)
            WS[(l, "inAB")] = self.dram("winAB%d" % l, [128, 128], BF16)
            for j in range(2):
                WS[(l, "out", j)] = self.dram("wout%d_%d" % (l, j), [128, 4096], BF16)

        es0 = ExitStack()
        self.es0 = es0
        P = Prog(nc, es0)
        self.P = P

        cst = self.sb(es0, "cst", [128, 16 * 128], F32)
        cstb = self.sb(es0, "cstb", [128, 16 * 128], BF16)
        flag = self.sb(es0, "flag", [128, 1], F32)
        modT = self.sb(es0, "modT", [128, L, 72, 2], F32)
        coef = self.sb(es0, "coef", [128, L, 9, 8, 2], F32)
        nfin = self.sb(es0, "nfin", [128, 8], F32)
        convw = self.sb(es0, "convw", [128, L, 5, 12], F32)
        gconst = self.sb(es0, "gconst", [128, L, 2, 8], F32)
        dnw = self.sb(es0, "dnw", [128, L, 512], F32)
        ones_b = self.sb(es0, "ones_b", [128, 128], BF16)
        epsD = self.sb(es0, "epsD", [128, 1], F32)

        def C(i):
            return cst[:, i * 128:(i + 1) * 128]

        def CB(i):
            return cstb[:, i * 128:(i + 1) * 128]
        self.C, self.CB = C, CB

        P.add("sp", lambda e: [e.dma_start(out=cst[:], in_=cst_in[:, :])], w=["cst"], slot="c0")
        P.add("sp", lambda e: [e.dma_start(out=flag[:], in_=flag_in[:, :])], w=["flag"], slot="c1")
        P.add("dve", lambda e: e.tensor_copy(cstb[:], cst[:]), r=["cst"], w=["cstb"])
        P.add("dve", lambda e: e.memset(ones_b[:], 1.0), w=["ones_b"])
        self.epsc = self.sb(es0, "epsc", [128, 4], F32)
        P.add("dve", lambda e: e.memset(self.epsc[:, 0:1], float(D * EPS)), w=["epsc0"])
        P.add("dve", lambda e: e.memset(self.epsc[:, 1:2], float(EPS)), w=["epsc1"])
        P.add("dve", lambda e: e.memset(self.epsc[:, 2:3], 1.0), w=["epsc2"])
        P.add("dve", lambda e: e.memset(self.epsc[:, 3:4], 0.0), w=["epsc3"])

        self.prologue_params(inp, modT, coef, nfin, convw, gconst, dnw, flag)
        P.flush()
        self.prologue_weights(inp, WS)
        P.flush()

        st = dict(XT=XT, X1T=X1T, QT=QT, KT=KT, VP=VP, PDN=PDN, ZS=ZS, GB=GB, MIXT=MIXT, DNQT=DNQT,
                  DNKT=DNKT, DNK=DNK, DNV=DNV, OFB=OFB, WS=WS, coef=coef, nfin=nfin, convw=convw,
                  gconst=gconst, dnw=dnw, ones_b=ones_b, flag=flag, x_in=x_in, y_out=y_out,
                  cos_in=cos_in, sin_in=sin_in, PADW=PADW)
        self.st = st
        for l in range(L):
            if self.debug == "P":
                break
            self.phase_A(l)
            P.flush()
            if self.debug == "A":
                break
            self.phase_attn(l)
            P.flush()
            if self.debug == "attn":
                break
            self.phase_dn(l)
            P.flush()
            if self.debug and self.debug.startswith("dn"):
                break
            self.phase_C(l)
            P.flush()
        if self.debug:
            self.dumps()
            P.flush()
        P.add("sp", lambda e: None, w=["Y"], slot=None)
        P.flush()
        es0.close()
        return nc

    def prologue_params(self, inp, modT, coef, nfin, convw, gconst, dnw, flag):
        nc, P, L = self.nc, self.P, self.depth
        es = ExitStack()
        cT = self.sb(es, "cT", [128, 8, 2], F32)
        scT = self.sb(es, "scT", [128, 8, 2], F32)
        adab = self.sb(es, "adab", [128, L, 72], F32)
        nw = self.sb(es, "nw", [128, L, 3, 8], F32)
        aw = [self.sb(es, "aw%d" % i, [128, 8, 512], F32) for i in range(2)]
        mps = self.ps(es, "mps", [128, 72, 2])
        tmpc = self.sb(es, "tmpc", [128, 8, 2], F32)

        c_in = inp["c2"]
        C = self.C
        prm = [self.sb(es, "prm%d" % i, [128, 128], F32) for i in range(3 + L)]
        tps = self.ps(es, "tps", [128, 512])

        def rows(ap2, i):
            return ap2[i:i + 1, :].rearrange("o (c p) -> (o c) p", p=128)

        def load_T(k, tile, srcs, nrows, col0, readers):
            P.add("sp", lambda e: [e.dma_start(out=tile[r0:r0 + n, :], in_=src) for (r0, n, src) in srcs],
                  w=[("prm", k)], slot=("prm", k), ndma=len(srcs))
            P.add("pe", lambda e: e.transpose(tps[:, col0:col0 + nrows], tile[0:nrows, :], C(0)[0:nrows, 0:nrows]),
                  r=[("prm", k), "cst"], w=["tps"])
        load_T(0, prm[0], [(s_ * 8, 8, rows(c_in, s_)) for s_ in range(2)], 16, 0, None)
        P.add("dve", lambda e: e.tensor_copy(cT[:].rearrange("p c s -> p s c"), tps[:, 0:16].rearrange("p (s c) -> p s c", s=2)),
              r=["tps"], w=["cT"])
        P.add("act", lambda e: e.activation(out=scT[:], in_=cT[:], func=AF.Silu), r=["cT"], w=["scT"])
        names = ["norm_ffn1", "norm_mix", "norm_ffn2"]
        srcs = [((i * L + l) * 8, 8, rows(inp[names[i]], l)) for i in range(3) for l in range(L)]
        srcs.append((3 * L * 8, 8, rows(inp["norm_final"], 0)))
        nr = 3 * L * 8 + 8
        load_T(1, prm[1], srcs, nr, 16, None)
        P.add("dve", lambda e: e.tensor_copy(nw[:].rearrange("p l i c -> p i l c"),
                                             tps[:, 16:16 + 3 * L * 8].rearrange("p (i l c) -> p i l c", i=3, l=L)),
              r=["tps"], w=["nw"])
        P.add("dve", lambda e: e.tensor_copy(nfin[:], tps[:, 16 + 3 * L * 8:16 + nr]), r=["tps"], w=["nfin"])
        for l in range(L):
            col0 = 16 + nr + l * 60
            load_T(2 + l, prm[2 + l], [(j * 12, 12, rows(inp["conv_w"][l], j)) for j in range(5)], 60, col0, None)
            P.add("dve", lambda e, l=l, col0=col0: e.tensor_copy(convw[:, l, :, :].rearrange("p j c -> p (j c)"), tps[:, col0:col0 + 60]),
                  r=["tps"], w=[("convw", l)])
        P.add("dve", lambda e: e.memset(tmpc[:], 0.0), r=[("convw", l) for l in range(L)], w=["convw", "tmpc"])
        tps2 = self.ps(es, "tps2", [128, 512])
        prmb = [self.sb(es, "prmb%d" % l, [128, 128], F32) for l in range(L)]
        for l in range(L):
            P.add("sp", lambda e, l=l: [e.dma_start(out=prmb[l][0:72, :], in_=rows(inp["ada_b"], l))], w=[("prmb", l)], slot=("prmb", l))
            P.add("pe", lambda e, l=l: e.transpose(tps2[:, l * 72:(l + 1) * 72], prmb[l][0:72, :], C(0)[0:72, 0:72]),
                  r=[("prmb", l), "cst"], w=["tps2"])
            P.add("dve", lambda e, l=l: e.tensor_copy(adab[:, l, :], tps2[:, l * 72:(l + 1) * 72]), r=["tps2"], w=[("adab", l)])
        P.add("dve", lambda e: e.memset(tmpc[:], 0.0), r=[("adab", l) for l in range(L)] + ["tmpc"], w=["adab", "tmpc"])
        P.add("sp", lambda e: [e.dma_start(out=gconst[:, l, i, :], in_=inp[nm][l:l + 1, :].partition_broadcast(128))
                               for l in range(L) for i, nm in enumerate(("a_log", "dt_bias"))],
              w=["gconst0"], slot="p5", ndma=2 * L)
        P.add("dve", lambda e: e.tensor_scalar_mul(nfin[:], nfin[:], float(np.sqrt(D))), r=["nfin"], w=["nfin"])
        P.add("act", lambda e: e.activation(out=gconst[:, :, 0, :], in_=gconst[:, :, 0, :], func=AF.Exp),
              r=["gconst0"], w=["gconst1"])
        P.add("dve", lambda e: e.tensor_scalar_mul(gconst[:, :, 0, :], gconst[:, :, 0, :], -1.0),
              r=["gconst1"], w=["gconst"])
        for l in range(L):
            P.add("sp", lambda e, l=l: [e.dma_start(out=dnw[:, l, h * 128:(h + 1) * 128],
                                                   in_=inp["dn_norm"][l:l + 1, :].partition_broadcast(128))
                                        for h in range(4)], w=["dnw"], slot="p6", ndma=4)
        for l in range(L):
            for pc in range(18):
                buf = aw[pc % 2]
                bk = ("aw", pc % 2)
                src = inp["ada_w"][l].rearrange("(kc p) n -> p kc n", p=128)[:, :, pc * 512:(pc + 1) * 512]
                P.add("sp", lambda e, buf=buf, src=src: [e.dma_start(out=buf[:], in_=src)], w=[bk], slot=bk)
                for jj in range(4):
                    j = pc * 4 + jj

                    def mm(e, buf=buf, jj=jj, j=j):
                        ins = None
                        for kc in range(8):
                            ins = e.matmul(mps[:, j, :], lhsT=buf[:, kc, jj * 128:(jj + 1) * 128], rhs=scT[:, kc, :],
                                           start=(kc == 0), stop=(kc == 7))
                        return ins
                    P.add("pe", mm, r=[bk, "scT"], w=[("mps", j)])
            P.add("dve", lambda e, l=l: e.tensor_tensor(out=modT[:, l, :, :], in0=mps[:],
                                                        in1=adab[:, l, :].unsqueeze(2).broadcast_to([128, 72, 2]),
                                                        op=ALU.add),
                  r=[("mps", j) for j in range(72)] + ["adab"], w=[("mps", j) for j in range(72)] + [("modT", l)])
        sqD = float(np.sqrt(D))
        for l in range(L):
            for i in range(3):
                sh = modT[:, l, (3 * i) * 8:(3 * i + 1) * 8, :]
                sc = modT[:, l, (3 * i + 1) * 8:(3 * i + 2) * 8, :]
                gt = modT[:, l, (3 * i + 2) * 8:(3 * i + 3) * 8, :]
                gs = 1.0 if i == 1 else 0.5
                wv = nw[:, l, i, :].unsqueeze(2).broadcast_to([128, 8, 2])
                P.add("dve", lambda e, sc=sc: e.tensor_scalar(tmpc[:], sc, 1.0, sqD, op0=ALU.add, op1=ALU.mult),
                      r=[("modT", l)], w=["tmpc"])
                P.add("dve", lambda e, l=l, i=i, wv=wv: e.tensor_tensor(out=coef[:, l, 3 * i, :, :], in0=tmpc[:], in1=wv,
                                                                       op=ALU.mult),
                      r=["tmpc", "nw"], w=[("coefa", l, i)])
                P.add("dve", lambda e, l=l, i=i, sh=sh: e.tensor_copy(coef[:, l, 3 * i + 1, :, :], sh),
                      r=[("modT", l)], w=[("coefb", l, i)])
                P.add("dve", lambda e, l=l, i=i, gt=gt, gs=gs: e.tensor_scalar_mul(coef[:, l, 3 * i + 2, :, :], gt, gs),
                      r=[("modT", l)], w=[("coefg", l, i)])
        P.add("dve", lambda e: e.memset(tmpc[:], 0.0),
              r=[("coefa", l, i) for l in range(L) for i in range(3)] + [("coefb", l, i) for l in range(L) for i in range(3)]
              + [("coefg", l, i) for l in range(L) for i in range(3)] + ["nfin", "convw", "gconst", "dnw", "tmpc"],
              w=["coef", "tmpc"])
        P.flush()
        es.close()

    def prologue_weights(self, inp, WS):
        nc, P, L = self.nc, self.P, self.depth
        es = ExitStack()
        NB = 3
        s32 = [self.sb(es, "s32_%d" % i, [128, 4096], F32) for i in range(NB)]
        s16 = [self.sb(es, "s16_%d" % i, [128, 4096], BF16) for i in range(NB)]
        s16p = [self.sb(es, "s16p_%d" % i, [128, 4096], BF16) for i in range(2)]
        state = dict(i=0, ip=0)
        cast_eng = ["dve", "act", "pool"]

        def piece(dkey, n, srcs, perm_key=None):
            dst = WS[dkey]
            perm_dst = WS[perm_key] if perm_key is not None else None
            i = state["i"]
            state["i"] += 1
            b = i % NB
            k32, k16 = ("s32", b), ("s16", b)

            def ld(e):
                return [e.dma_start(out=dv(s32[b]), in_=sv) for dv, sv in srcs]
            P.add("sp", ld, w=[k32], slot=k32, ndma=len(srcs))
            ce = cast_eng[i % 2]
            if ce == "act":
                P.add("act", lambda e: e.copy(out=s16[b][:, :n], in_=s32[b][:, :n]), r=[k32], w=[k16])
            else:
                P.add(ce, lambda e: e.tensor_copy(s16[b][:, :n], s32[b][:, :n]), r=[k32], w=[k16])
            P.add("pool", lambda e: [e.dma_start(out=dst[:, :n], in_=s16[b][:, :n])], r=[k16], w=[("W", dkey)], slot=k16)
            if perm_dst is not None:
                ip = state["ip"] % 2
                state["ip"] += 1
                kp = ("s16p", ip)
                src4 = s16[b][:].rearrange("p (j k e d) -> p j k e d", j=4, k=8, e=2)
                dst4 = s16p[ip][:].rearrange("p (j k e d) -> p j k e d", j=4, k=8, e=2)

                def pm(e):
                    for j in range(4):
                        e.tensor_copy(dst4[:, j, :, :, 16:64], src4[:, j, :, :, 16:64])
                        e.tensor_copy(dst4[:, j, :, :, 0:8], src4[:, j, :, :, 8:16])
                        ins = e.tensor_copy(dst4[:, j, :, :, 8:16], src4[:, j, :, :, 0:8])
                    return ins
                P.add("dve", pm, r=[k16], w=[kp])
                P.add("pool", lambda e: [e.dma_start(out=perm_dst[:, :], in_=s16p[ip][:])], r=[kp], w=[("W", perm_key)], slot=kp)

        def view(shape_str, lo, hi, **kw):
            return lambda t: t[:, lo:hi].rearrange(shape_str, **kw)

        for l in range(L):
            for f, (wg, wu, wd) in ((1, ("ffn1_w_gate", "ffn1_w_up", "ffn1_w_down")), (2, ("ffn2_w_gate", "ffn2_w_up", "ffn2_w_down"))):
                g3 = inp[wg][l].rearrange("(kc p) n -> p kc n", p=128)
                u3 = inp[wu][l].rearrange("(kc p) n -> p kc n", p=128)
                d3 = inp[wd][l].rearrange("(fc p) n -> p fc n", p=128)
                for m in range(NFC):
                    piece((l, "gu", f, m), 2048,
                          [(view("p (k c) -> p k c", 0, 1024, k=8), g3[:, :, m * 128:(m + 1) * 128]),
                           (view("p (k c) -> p k c", 1024, 2048, k=8), u3[:, :, m * 128:(m + 1) * 128])])
                for dc in range(8):
                    piece((l, "dn", f, dc), NFC * 128,
                          [(view("p (k c) -> p k c", f0 * 128, f1 * 128, k=f1 - f0), d3[:, f0:f1, dc * 128:(dc + 1) * 128])
                           for (f0, f1) in ((0, 8), (8, 16), (16, NFC))])
            i3 = inp["w_in"][l].rearrange("(kc p) n -> p kc n", p=128)
            fcols = [0, 512, 1536, 2048, 2560]
            for j in range(5):
                c0 = fcols[j]
                srcs = [(view("p (k c) -> p k c", jj * 1024, (jj + 1) * 1024, k=8), i3[:, :, c0 + jj * 128:c0 + (jj + 1) * 128])
                        for jj in range(4)]
                piece((l, "inF", j), 4096, srcs, perm_key=((l, "inP", j) if j < 2 else None))
            for j, c0 in enumerate((1024, 3072)):
                piece((l, "inT", j), 4096, [(view("p (k c) -> p k c", 0, 4096, k=8), i3[:, :, c0:c0 + 512])])
            piece((l, "inAB"), 128, [(view("p (k c) -> p k c", 0, 128, k=8), i3[:, :, 3584:3600])])
            o3 = inp["w_out"][l].rearrange("(kc p) n -> p kc n", p=128)
            for j in range(2):
                srcs = [(view("p (k c) -> p k c", jj * 1024, (jj + 1) * 1024, k=8),
                         o3[:, :, (j * 4 + jj) * 128:(j * 4 + jj + 1) * 128]) for jj in range(4)]
                piece((l, "out", j), 4096, srcs)
        P.flush()
        es.close()


    def dumps(self):
        st = self.st
        dbg = self.debug
        if dbg == "P":
            self.dump("coef", st["coef"][:].rearrange("p a b c d -> p (a b c d)"), [])
            self.dump("w_gu0", st["WS"][(0, "gu", 1, 0)], [])
            self.dump("w_inP0", st["WS"][(0, "inP", 0)], [])
            self.dump("w_dn3", st["WS"][(0, "dn", 2, 3)], [])
        if dbg == "A":
            for nm in ("X1T", "QT", "KT", "VP", "PDN", "ZS", "GB"):
                self.dump(nm, st[nm], [])
        if dbg == "attn":
            self.dump("MIXT", st["MIXT"][0:4], [])
        if dbg and dbg.startswith("dn"):
            names = {"dn1": ("DNQT", "DNKT", "DNK", "DNV"), "dn2": ("DNQT", "DNKT", "DNK", "DNV", "OFB")}.get(
                dbg, ("DNQT",) if dbg.startswith("dn2:") else ("DNQT", "DNKT", "DNK", "DNV", "OFB", "MIXT"))
            for nm in names:
                self.dump(nm, st[nm], [])

    def dump(self, name, src, rkeys):
        out = self.dram("dbg_" + name, list(src.shape), src.dtype, kind="ExternalOutput")
        self.dbg_out[name] = out
        self.P.add("sp", lambda e: [e.dma_start(out=out, in_=src)], r=list(rkeys) + ["Y"], slot="dbg")

    class WStream:
        def __init__(self, B, slots, pieces):
            self.B, self.slots, self.pieces = B, slots, pieces
            self.i = 0
            self.j = 0
            for _ in range(len(slots)):
                self._load()

        def _load(self):
            if self.j >= len(self.pieces):
                return
            key, n = self.pieces[self.j]
            s = self.j % len(self.slots)
            src = self.B.st["WS"][key]
            tile = self.slots[s]
            self.B.P.add("sp", lambda e: [e.dma_start(out=tile[:, :n], in_=src[:, :n])],
                         r=[("W", key)], w=[("wsl", s)], slot=("wsl", s))
            self.j += 1

        def get(self, key):
            k, n = self.pieces[self.i]
            assert k == key, (k, key)
            s = self.i % len(self.slots)
            self.i += 1
            return self.slots[s], ("wsl", s)

        def done(self):
            self._load()

    def norm(self, xt_t, kx, hout, kh, a_fn, b_fn, bf):
        P = self.P
        sq, pss, kpss, rstd, tmpn, ones_b = bf["sq"], bf["pss"], bf["kpss"], bf["rstd"], bf["tmpn"], self.st["ones_b"]
        P.add("act", lambda e: e.activation(out=sq[:], in_=xt_t[:], func=AF.Square), r=[kx], w=["sq"])

        def mm(e):
            for c in range(8):
                ins = e.matmul(pss[:], lhsT=ones_b[:], rhs=sq[:, c, :], start=(c == 0), stop=(c == 7))
            return ins
        P.add("pe", mm, r=["sq", "ones_b"], w=[kpss])
        P.add("act", lambda e: e.activation(out=bf["tmpn"][0][:], in_=pss[:], func=AF.Sqrt, bias=self.epsc[:, 0:1], scale=1.0),
              r=[kpss, ("tmpn", 0)], w=[("tmpn", 0)])
        P.add("dve", lambda e: e.reciprocal(rstd[:], bf["tmpn"][0][:]), r=[("tmpn", 0)], w=["rstd"])
        for c in range(8):
            tb = tmpn[c % 2]
            P.add("dve", lambda e, c=c, tb=tb: e.tensor_tensor(out=tb[:], in0=xt_t[:, c, :], in1=rstd[:], op=ALU.mult),
                  r=[kx, "rstd"], w=[("tmpn", c % 2)])
            bias = b_fn(c) if b_fn is not None else 0.0
            P.add("act", lambda e, c=c, tb=tb, bias=bias: e.activation(out=hout[:, c, :], in_=tb[:], func=AF.Identity,
                                                                      bias=bias, scale=a_fn(c)),
                  r=[("tmpn", c % 2), "coef"], w=[(kh, c)])

    def ffn(self, l, f, s, xt_t, kx, bf, ws):
        P, coef = self.P, self.st["coef"]
        i = 0 if f == 1 else 2
        h, act, sg, banks = bf["h"], bf["act"], bf["sg"], bf["banks"]
        self.norm(xt_t, kx, h, "h", lambda c: coef[:, l, 3 * i, c, s:s + 1], lambda c: coef[:, l, 3 * i + 1, c, s:s + 1], bf)
        hk = [("h", c) for c in range(8)]
        for m in range(NFC):
            wt, wk = ws.get((l, "gu", f, m))
            w4 = wt[:, :2048].rearrange("p (g k c) -> p g k c", g=2, k=8)
            pg, kpg = banks[(m % 2) * 2], ("bank", (m % 2) * 2)
            pu, kpu = banks[(m % 2) * 2 + 1], ("bank", (m % 2) * 2 + 1)

            def mm(e, w4=w4, pg=pg, pu=pu):
                for g, pp in ((0, pg), (1, pu)):
                    for kc in range(8):
                        ins = e.matmul(pp[:], lhsT=w4[:, g, kc, :], rhs=h[:, kc, :], start=(kc == 0), stop=(kc == 7))
                return ins
            P.add("pe", mm, r=[wk] + hk, w=[kpg, kpu])
            ws.done()
            sgb = sg[m % 2]
            P.add("act", lambda e, pg=pg, sgb=sgb: e.activation(out=sgb[:], in_=pg[:], func=AF.Silu),
                  r=[kpg], w=[("sg", m % 2)])
            P.add("dve", lambda e, pu=pu, sgb=sgb, m=m: e.tensor_tensor(out=act[:, m, :], in0=pu[:], in1=sgb[:], op=ALU.mult),
                  r=[kpu, ("sg", m % 2)], w=[("act", m)])
        ak = [("act", m) for m in range(NFC)]
        for dc in range(8):
            wt, wk = ws.get((l, "dn", f, dc))
            w3 = wt[:, :NFC * 128].rearrange("p (k c) -> p k c", k=NFC)
            py, kpy = banks[4 + dc % 2], ("bank", 4 + dc % 2)

            def mm2(e, w3=w3, py=py):
                for fc in range(NFC):
                    ins = e.matmul(py[:], lhsT=w3[:, fc, :], rhs=act[:, fc, :], start=(fc == 0), stop=(fc == NFC - 1))
                return ins
            P.add("pe", mm2, r=[wk] + ak, w=[kpy])
            ws.done()
            P.add("dve", lambda e, dc=dc, py=py: e.scalar_tensor_tensor(out=xt_t[:, dc, :], in0=py[:],
                                                                       scalar=coef[:, l, 3 * i + 2, dc, s:s + 1],
                                                                       in1=xt_t[:, dc, :], op0=ALU.mult, op1=ALU.add),
                  r=[kpy, kx, "coef"], w=[kx])

    def gemm_bufs(self, es):
        bf = {}
        bf["sq"] = self.sb(es, "sq", [128, 8, 512], BF16)
        bf["rstd"] = self.sb(es, "rstd", [128, 512], F32)
        bf["tmpn"] = [self.sb(es, "tmpn", [128, 512], F32) for _ in range(2)]
        bf["h"] = self.sb(es, "h", [128, 8, 512], BF16)
        bf["act"] = self.sb(es, "act", [128, NFC, 512], BF16)
        bf["sg"] = [self.sb(es, "sg", [128, 512], F32) for _ in range(2)]
        bf["wsl"] = [self.sb(es, "wsl", [128, 4096], BF16) for _ in range(4)]
        bf["banks"] = [self.ps(es, "bank", [128, 512]) for _ in range(8)]
        bf["pss"], bf["kpss"] = bf["banks"][6], ("bank", 6)
        return bf

    def phase_A(self, l):
        nc, P, st = self.nc, self.P, self.st
        NT, SEG = self.NT, self.SEG
        C, coef = self.C, st["coef"]
        es = ExitStack()
        bf = self.gemm_bufs(es)
        banks = bf["banks"]
        xt = [self.sb(es, "xt", [128, 8, 512], F32) for _ in range(2)]
        xtok = [self.sb(es, "xtok", [128, 1024], F32) for _ in range(2)]
        cosb = self.sb(es, "cosb", [128, 512], F32)
        sinb = self.sb(es, "sinb", [128, 512], F32)
        t1 = self.sb(es, "t1", [128, 512], F32)
        t2 = self.sb(es, "t2", [128, 512], F32)
        stq = [self.sb(es, "stq", [128, 512], BF16) for _ in range(2)]
        stg = [self.sb(es, "stg", [128, 512], F32) for _ in range(3)]
        vst = [self.sb(es, "vst", [128, 1024], BF16) for _ in range(2)]
        zst = [self.sb(es, "zst", [128, 512], F32) for _ in range(2)]
        gab = [self.sb(es, "gab", [128, 16], F32) for _ in range(2)]
        gt = [self.sb(es, "gt", [128, 8], F32) for _ in range(4)]
        gout = [self.sb(es, "gout", [128, 16], F32) for _ in range(2)]
        zpad = self.sb(es, "zpad", [128, 2], F32)
        h = bf["h"]
        NTILE = NT // TT
        PADW = st["PADW"]
        for i in range(2):
            P.add("dve", lambda e, i=i: e.memset(vst[i][:], 0.0), w=[("vst", i)])
        P.add("dve", lambda e: e.memset(zpad[:], 0.0), w=["zpad"])
        P.add("pool", lambda e: [e.dma_start(out=st["PDN"][c, :, 0:2], in_=zpad[:]) for c in range(12)]
              + [e.dma_start(out=st["PDN"][c, :, 2 * PADW - 2:2 * PADW], in_=zpad[:]) for c in range(12)],
              r=["zpad"], w=[("PDNpad", l)], slot="zpad", ndma=24)
        pieces = []
        for t in range(NTILE):
            pieces += [((l, "gu", 1, m), 2048) for m in range(NFC)] + [((l, "dn", 1, dc), NFC * 128) for dc in range(8)]
            pieces += [((l, "inF", 0), 4096), ((l, "inP", 0), 4096), ((l, "inF", 1), 4096), ((l, "inP", 1), 4096),
                       ((l, "inF", 2), 4096), ((l, "inF", 3), 4096), ((l, "inF", 4), 4096),
                       ((l, "inT", 0), 4096), ((l, "inT", 1), 4096), ((l, "inAB"), 128)]
        ws = Builder.WStream(self, bf["wsl"], pieces)
        ident = C(0)
        def _tile(t):
            s = (t * TT) // SEG
            tok0 = t * TT
            xt_t, kx = xt[t % 2], ("xt", t % 2)
            if l == 0:
                for sb_ in range(4):
                    xk = xtok[sb_ % 2]
                    kxk = ("xtok", sb_ % 2)
                    P.add("sp", lambda e, xk=xk, sb_=sb_: [e.dma_start(out=xk[:], in_=st["x_in"][tok0 + sb_ * 128:tok0 + (sb_ + 1) * 128, :])],
                          w=[kxk], slot=kxk)
                    for hf in range(2):
                        bk, kbk = banks[hf], ("bank", hf)

                        def tr(e, xk=xk, hf=hf, bk=bk):
                            for cc in range(4):
                                c = hf * 4 + cc
                                ins = e.transpose(bk[:, cc * 128:(cc + 1) * 128], xk[:, c * 128:(c + 1) * 128], ident)
                            return ins
                        P.add("pe", tr, r=[kxk, "cst"], w=[kbk])
                        P.add("act", lambda e, hf=hf, bk=bk, sb_=sb_: e.copy(out=xt_t[:, hf * 4:(hf + 1) * 4, sb_ * 128:(sb_ + 1) * 128],
                                                                            in_=bk[:].rearrange("p (c t) -> p c t", c=4)),
                              r=[kbk], w=[kx])
            else:
                P.add("pool", lambda e: [e.dma_start(out=xt_t[:], in_=st["XT"][:, :, tok0:tok0 + TT].rearrange("c p t -> p c t"))],
                      r=[("XT", t)], w=[kx], slot=kx)
            self.ffn(l, 1, s, xt_t, kx, bf, ws)
            P.add("pool", lambda e: [e.dma_start(out=st["X1T"][:, :, tok0:tok0 + TT].rearrange("c p t -> p c t"), in_=xt_t[:])],
                  r=[kx], w=[("X1T", t)], slot=("x1st", t % 2))
            self.norm(xt_t, kx, h, "h", lambda c: coef[:, l, 3, c, s:s + 1], lambda c: coef[:, l, 4, c, s:s + 1], bf)
            hk = [("h", c) for c in range(8)]
            P.add("sp", lambda e: [e.dma_start(out=cosb[:], in_=st["cos_in"][:, tok0:tok0 + TT]),
                                   e.dma_start(out=sinb[:], in_=st["sin_in"][:, tok0:tok0 + TT])],
                  w=["cs"], slot="cs", ndma=2)
            for qk in range(2):
                wt, wk = ws.get((l, "inF", qk))
                wp, wpk = ws.get((l, "inP", qk))
                w4 = wt[:].rearrange("p (j k c) -> p j k c", j=4, k=8)
                p4 = wp[:].rearrange("p (j k c) -> p j k c", j=4, k=8)
                dst = st["QT"] if qk == 0 else st["KT"]
                for j in range(4):
                    pa, kpa = banks[0 + (j % 2) * 2], ("bank", (j % 2) * 2)
                    pb, kpb = banks[1 + (j % 2) * 2], ("bank", 1 + (j % 2) * 2)

                    def mm(e, w4=w4, p4=p4, j=j, pa=pa, pb=pb):
                        for ww, pp in ((w4, pa), (p4, pb)):
                            for kc in range(8):
                                ins = e.matmul(pp[:], lhsT=ww[:, j, kc, :], rhs=h[:, kc, :], start=(kc == 0), stop=(kc == 7))
                        return ins
                    P.add("pe", mm, r=[wk, wpk] + hk, w=[kpa, kpb])
                    P.add("dve", lambda e, pa=pa: e.tensor_tensor(out=t1[:], in0=pa[:], in1=cosb[:], op=ALU.mult),
                          r=[kpa, "cs"], w=["t1"])
                    P.add("dve", lambda e, pb=pb: e.tensor_tensor(out=t2[:], in0=pb[:], in1=sinb[:], op=ALU.mult),
                          r=[kpb, "cs"], w=["t2"])
                    sq_, ksq = stq[j % 2], ("stq", j % 2)
                    P.add("dve", lambda e, sq_=sq_: e.tensor_tensor(out=sq_[:], in0=t1[:], in1=t2[:], op=ALU.add),
                          r=["t1", "t2"], w=[ksq])
                    P.add("pool", lambda e, sq_=sq_, j=j, dst=dst: [e.dma_start(out=dst[j, :, tok0:tok0 + TT], in_=sq_[:])],
                          r=[ksq], w=[("QK", qk, j, t)], slot=ksq)
                ws.done()
                ws.done()
            for g in range(3):
                wt, wk = ws.get((l, "inF", 2 + g))
                w4 = wt[:].rearrange("p (j k c) -> p j k c", j=4, k=8)
                for j in range(4):
                    cidx = g * 4 + j
                    pa, kpa = banks[cidx % 4], ("bank", cidx % 4)

                    def mm(e, w4=w4, j=j, pa=pa):
                        for kc in range(8):
                            ins = e.matmul(pa[:], lhsT=w4[:, j, kc, :], rhs=h[:, kc, :], start=(kc == 0), stop=(kc == 7))
                        return ins
                    P.add("pe", mm, r=[wk] + hk, w=[kpa])
                    sg_, ksg = stg[cidx % 3], ("stg", cidx % 3)
                    P.add("act", lambda e, pa=pa, sg_=sg_: e.copy(out=sg_[:], in_=pa[:]), r=[kpa], w=[ksg])
                    col0 = s * PADW + 2 + (tok0 - s * SEG)
                    P.add("pool", lambda e, sg_=sg_, cidx=cidx, col0=col0: [e.dma_start(out=st["PDN"][cidx, :, col0:col0 + TT], in_=sg_[:])],
                          r=[ksg], w=[("PDN", cidx, t)], slot=ksg)
                    if tok0 + TT == SEG or tok0 == SEG:
                        left = (tok0 + TT == SEG)
                        srcv = sg_[:, TT - 2:TT] if left else sg_[:, 0:2]
                        dcol = (PADW + 0) if left else (PADW - 2)
                        P.add("dve", lambda e, srcv=srcv: e.tensor_scalar(zpad[:], srcv, st["flag"][:, 0:1], None, op0=ALU.mult),
                              r=[ksg, "flag", "zpad"], w=["zpad"])
                        P.add("pool", lambda e, cidx=cidx, dcol=dcol: [e.dma_start(out=st["PDN"][cidx, :, dcol:dcol + 2], in_=zpad[:])],
                              r=["zpad"], w=[("PDNh", cidx, left)], slot="zpad")
                ws.done()
            wv, wvk = ws.get((l, "inT", 0))
            wz, wzk = ws.get((l, "inT", 1))
            wab, wabk = ws.get((l, "inAB"))
            wv3 = wv[:].rearrange("p (k c) -> p k c", k=8)
            wz3 = wz[:].rearrange("p (k c) -> p k c", k=8)
            wab3 = wab[:, :128].rearrange("p (k c) -> p k c", k=8)
            for sb_ in range(4):
                tk0 = tok0 + sb_ * 128
                pv, kpv = banks[0 + (sb_ % 2) * 3], ("bank", (sb_ % 2) * 3)
                pz, kpz = banks[1 + (sb_ % 2) * 3], ("bank", 1 + (sb_ % 2) * 3)
                pab, kpab = banks[2 + (sb_ % 2) * 3], ("bank", 2 + (sb_ % 2) * 3)

                def mm(e, sb_=sb_, pv=pv, pz=pz, pab=pab):
                    for kc in range(8):
                        e.matmul(pv[:], lhsT=h[:, kc, sb_ * 128:(sb_ + 1) * 128], rhs=wv3[:, kc, :], start=(kc == 0), stop=(kc == 7))
                    for kc in range(8):
                        e.matmul(pz[:], lhsT=h[:, kc, sb_ * 128:(sb_ + 1) * 128], rhs=wz3[:, kc, :], start=(kc == 0), stop=(kc == 7))
                    for kc in range(8):
                        ins = e.matmul(pab[:, 0:16], lhsT=h[:, kc, sb_ * 128:(sb_ + 1) * 128], rhs=wab3[:, kc, :], start=(kc == 0), stop=(kc == 7))
                    return ins
                P.add("pe", mm, r=[wvk, wzk, wabk] + hk, w=[kpv, kpz, kpab])
                vs, kvs = vst[sb_ % 2], ("vst", sb_ % 2)

                def cpv(e, vs=vs, pv=pv):
                    v4 = vs[:].rearrange("p (j e d) -> p j e d", j=4, e=2)
                    p4 = pv[:].rearrange("p (j e d) -> p j e d", j=4, e=2)
                    e.tensor_copy(v4[:, :, 0, 0:64], p4[:, :, 0, :])
                    return e.tensor_copy(v4[:, :, 1, 64:128], p4[:, :, 1, :])
                P.add("dve", cpv, r=[kpv], w=[kvs])
                P.add("pool", lambda e, vs=vs, tk0=tk0: [e.dma_start(out=st["VP"][tk0:tk0 + 128, :], in_=vs[:])],
                      r=[kvs], w=[("VP", tk0)], slot=kvs)
                zs_, kzs = zst[sb_ % 2], ("zst", sb_ % 2)
                P.add("act", lambda e, zs_=zs_, pz=pz: e.activation(out=zs_[:], in_=pz[:], func=AF.Silu), r=[kpz], w=[kzs])
                P.add("pool", lambda e, zs_=zs_, tk0=tk0: [e.dma_start(out=st["ZS"][tk0:tk0 + 128, :], in_=zs_[:])],
                      r=[kzs], w=[("ZS", tk0)], slot=kzs)
                ga, kga = gab[sb_ % 2], ("gab", sb_ % 2)
                go, kgo = gout[sb_ % 2], ("gout", sb_ % 2)
                gc_ = st["gconst"]
                P.add("dve", lambda e, ga=ga, pab=pab: e.tensor_copy(ga[:], pab[:, 0:16]), r=[kpab], w=[kga])
                P.add("dve", lambda e, ga=ga: e.tensor_tensor(out=gt[0][:], in0=ga[:, 0:8], in1=gc_[:, l, 1, :], op=ALU.add),
                      r=[kga, "coef"], w=["gt0"])
                P.add("dve", lambda e: e.scalar_tensor_tensor(out=gt[1][:], in0=gt[0][:], scalar=-1.0, in1=gt[0][:],
                                                             op0=ALU.mult, op1=ALU.max), r=["gt0"], w=["gt1"])
                P.add("act", lambda e: e.activation(out=gt[2][:], in_=gt[1][:], func=AF.Exp, scale=-1.0), r=["gt1"], w=["gt2"])
                P.add("act", lambda e: e.activation(out=gt[3][:], in_=gt[2][:], func=AF.Ln, bias=self.epsc[:, 2:3], scale=1.0), r=["gt2"], w=["gt3"])
                P.add("dve", lambda e: e.scalar_tensor_tensor(out=gt[1][:], in0=gt[0][:], scalar=0.0, in1=gt[3][:],
                                                             op0=ALU.max, op1=ALU.add), r=["gt0", "gt3", "gt1"], w=["gt1"])
                P.add("dve", lambda e, go=go: e.tensor_tensor(out=go[:, 0:8], in0=gt[1][:], in1=gc_[:, l, 0, :], op=ALU.mult),
                      r=["gt1", "coef"], w=[(kgo, 0)])
                P.add("act", lambda e, go=go, ga=ga: e.activation(out=go[:, 8:16], in_=ga[:, 8:16], func=AF.Sigmoid),
                      r=[kga], w=[(kgo, 1)])
                P.add("pool", lambda e, go=go, tk0=tk0: [e.dma_start(out=st["GB"][tk0:tk0 + 128, :], in_=go[:])],
                      r=[(kgo, 0), (kgo, 1)], w=[("GB", tk0), (kgo, 0), (kgo, 1)], slot=kgo)
            ws.done()
            ws.done()
            ws.done()
        for t in range(NTILE):
            _tile(t)
        P.flush()
        es.close()

    def phase_C(self, l):
        nc, P, st = self.nc, self.P, self.st
        NT, SEG = self.NT, self.SEG
        C, coef = self.C, st["coef"]
        last = (l == self.depth - 1)
        es = ExitStack()
        bf = self.gemm_bufs(es)
        banks = bf["banks"]
        xt = [self.sb(es, "xt", [128, 8, 512], F32) for _ in range(2)]
        mx = [self.sb(es, "mx", [128, 8, 512], BF16) for _ in range(2)]
        yt = self.sb(es, "yt", [128, 8, 512], F32)
        ytok = [self.sb(es, "ytok", [128, 1024], F32) for _ in range(2)]
        NTILE = NT // TT
        pieces = []
        for t in range(NTILE):
            pieces += [((l, "out", 0), 4096), ((l, "out", 1), 4096)]
            pieces += [((l, "gu", 2, m), 2048) for m in range(NFC)] + [((l, "dn", 2, dc), NFC * 128) for dc in range(8)]
        ws = Builder.WStream(self, bf["wsl"], pieces)
        ident = C(0)
        def _tile(t):
            s = (t * TT) // SEG
            tok0 = t * TT
            xt_t, kx = xt[t % 2], ("xt", t % 2)
            mx_t, kmx = mx[t % 2], ("mx", t % 2)
            P.add("pool", lambda e: [e.dma_start(out=xt_t[:], in_=st["X1T"][:, :, tok0:tok0 + TT].rearrange("c p t -> p c t"))],
                  r=[("X1T", t)], w=[kx], slot=kx)
            P.add("pool", lambda e: [e.dma_start(out=mx_t[:], in_=st["MIXT"][:, :, tok0:tok0 + TT].rearrange("c p t -> p c t"))],
                  r=["MIXT"], w=[kmx], slot=kmx)
            for j in range(2):
                wt, wk = ws.get((l, "out", j))
                w4 = wt[:].rearrange("p (j k c) -> p j k c", j=4, k=8)
                for jj in range(4):
                    dc = j * 4 + jj
                    py, kpy = banks[dc % 2], ("bank", dc % 2)

                    def mm(e, w4=w4, jj=jj, py=py):
                        for kc in range(8):
                            ins = e.matmul(py[:], lhsT=w4[:, jj, kc, :], rhs=mx_t[:, kc, :], start=(kc == 0), stop=(kc == 7))
                        return ins
                    P.add("pe", mm, r=[wk, kmx], w=[kpy])
                    P.add("dve", lambda e, dc=dc, py=py: e.scalar_tensor_tensor(out=xt_t[:, dc, :], in0=py[:],
                                                                               scalar=coef[:, l, 5, dc, s:s + 1],
                                                                               in1=xt_t[:, dc, :], op0=ALU.mult, op1=ALU.add),
                          r=[kpy, kx, "coef"], w=[kx])
                ws.done()
            self.ffn(l, 2, s, xt_t, kx, bf, ws)
            if not last:
                P.add("pool", lambda e: [e.dma_start(out=st["XT"][:, :, tok0:tok0 + TT].rearrange("c p t -> p c t"), in_=xt_t[:])],
                      r=[kx], w=[("XT", t)], slot=("x1st", t % 2))
            else:
                nf = st["nfin"]
                self.norm(xt_t, kx, yt, "yt", lambda c: nf[:, c:c + 1], None, bf)
                yk = [("yt", c) for c in range(8)]
                for sb_ in range(4):
                    yk_, kyk = ytok[sb_ % 2], ("ytok", sb_ % 2)
                    for hf in range(2):
                        bk, kbk = banks[hf], ("bank", hf)

                        def tr(e, hf=hf, bk=bk, sb_=sb_):
                            for cc in range(4):
                                c = hf * 4 + cc
                                ins = e.transpose(bk[:, cc * 128:(cc + 1) * 128], yt[:, c, sb_ * 128:(sb_ + 1) * 128], ident)
                            return ins
                        P.add("pe", tr, r=yk + ["cst"], w=[kbk])
                        P.add("act", lambda e, hf=hf, bk=bk, yk_=yk_: e.copy(out=yk_[:, hf * 512:(hf + 1) * 512], in_=bk[:]),
                              r=[kbk], w=[(kyk, hf)])
                    P.add("sp", lambda e, yk_=yk_, sb_=sb_: [e.dma_start(out=st["y_out"][tok0 + sb_ * 128:tok0 + (sb_ + 1) * 128, :], in_=yk_[:])],
                          r=[(kyk, 0), (kyk, 1), "Y"], w=[(kyk, 0), (kyk, 1)], slot=kyk)
        for t in range(NTILE):
            _tile(t)
        P.flush()
        es.close()

    def phase_attn(self, l):
        nc, P, st = self.nc, self.P, self.st
        NT, SEG = self.NT, self.SEG
        CB = self.CB
        es = ExitStack()
        NBLK = NT // 128
        qt = self.sb(es, "qt", [128, NT], BF16)
        kt = self.sb(es, "kt", [128, NT], BF16)
        vpad = self.sb(es, "vpad", [128, NBLK, 256], BF16)
        acc = self.sb(es, "acc", [128, 2, NT], F32)
        pe_ = [self.sb(es, "pe", [128, 2, 384], BF16) for _ in range(2)]
        pm_ = [self.sb(es, "pm", [128, 2, 384], BF16) for _ in range(2)]
        mkv = {v: self.sb(es, "mk" + v, [128, 2, 384], BF16) for v in ("n", "pf", "nf")}
        rden = self.sb(es, "rden", [128, 512], F32)
        ost = [self.sb(es, "ost", [128, 512], BF16) for _ in range(2)]
        sps = [self.ps(es, "sps", [128, 2, 512]) for _ in range(2)]
        nd = [self.ps(es, "nd", [128, 512]) for _ in range(2)]
        flag = st["flag"]

        def mkbuild(e):
            for v in ("n", "pf", "nf"):
                for hh in range(2):
                    for j in range(3):
                        dst = mkv[v][:, hh, j * 128:(j + 1) * 128]
                        if (v == "pf" and j == 0) or (v == "nf" and j == 2):
                            ins = e.tensor_scalar(dst, CB(7 + [1, 0, 2][j]), flag[:, 0:1], None, op0=ALU.mult)
                        else:
                            ins = e.tensor_copy(dst, CB(7 + [1, 0, 2][j]))
            return ins
        P.add("dve", mkbuild, r=["cstb", "flag"], w=["mkv"])
        it = 0
        for hp in range(4):
            P.add("sp", lambda e, hp=hp: [e.dma_start(out=qt[:], in_=st["QT"][hp]), e.dma_start(out=kt[:], in_=st["KT"][hp])],
                  w=["qk"], slot="qk", ndma=2)
            for pi, dil in enumerate((1, 4, 16)):
                nb = NT // dil // 128
                vsrc = st["VP"][:, hp * 256:(hp + 1) * 256].rearrange("(b p r) c -> r p b c", p=128, r=dil)
                vchunks = [(r, b0, min(b0 + 8, nb)) for r in range(dil) for b0 in range(0, nb, 8)]
                P.add("sp", lambda e, vsrc=vsrc, nb=nb, vchunks=vchunks: [e.dma_start(out=vpad[:, r * nb + b0:r * nb + b1, :], in_=vsrc[r][:, b0:b1, :])
                                                                         for (r, b0, b1) in vchunks],
                      w=["vpad"], slot="vpad", ndma=len(vchunks))
                def block_iter(r, b, par, pi=pi, dil=dil, nb=nb):

                        def tsl(blk, r=r, dil=dil):
                            s0 = r + dil * 128 * blk
                            return slice(s0, s0 + 127 * dil + 1, dil) if dil > 1 else slice(s0, s0 + 128)
                        kbs = [(j, b + j - 1) for j in range(3) if 0 <= b + j - 1 < nb]
                        j0, j1 = kbs[0][0], kbs[-1][0] + 1
                        v = "pf" if b == nb // 2 else ("nf" if b == nb // 2 - 1 else "n")
                        sp_, ksp = sps[par], ("sps", par)
                        nd_, knd = nd[par], ("nd", par)
                        pe__, kpe = pe_[par], ("pe", par)
                        pm__, kpm = pm_[par], ("pm", par)
                        qs = tsl(b)

                        def mm(e, kbs=kbs, sp_=sp_, qs=qs, tsl=tsl):
                            for hh in range(2):
                                for j, kb in kbs:
                                    ins = e.matmul(sp_[:, hh, j * 128:(j + 1) * 128], lhsT=kt[hh * 64:(hh + 1) * 64, tsl(kb)],
                                                   rhs=qt[hh * 64:(hh + 1) * 64, qs], start=True, stop=True)
                            return ins
                        P.add("pe", mm, r=["qk"], w=[ksp])
                        P.add("act", lambda e, sp_=sp_, pe__=pe__, j0=j0, j1=j1: e.activation(
                            out=pe__[:, :, j0 * 128:j1 * 128], in_=sp_[:, :, j0 * 128:j1 * 128], func=AF.Exp, scale=0.125),
                            r=[ksp], w=[kpe])
                        P.add("dve", lambda e, pe__=pe__, pm__=pm__, j0=j0, j1=j1, v=v: e.tensor_tensor(
                            out=pm__[:, :, j0 * 128:j1 * 128], in0=pe__[:, :, j0 * 128:j1 * 128],
                            in1=mkv[v][:, :, j0 * 128:j1 * 128], op=ALU.mult), r=[kpe, "mkv"], w=[kpm])

                        def mm2(e, kbs=kbs, nd_=nd_, pm__=pm__, r=r, nb=nb):
                            n = 2 * len(kbs)
                            for which in range(2):
                                i = 0
                                for hh in range(2):
                                    for j, kb in kbs:
                                        lhsT = vpad[:, r * nb + kb, hh * 128:(hh + 1) * 128] if which == 0 else CB(10 + hh)
                                        ins = e.matmul(nd_[:, which * 128:(which + 1) * 128], lhsT=lhsT,
                                                       rhs=pm__[:, hh, j * 128:(j + 1) * 128], start=(i == 0), stop=(i == n - 1))
                                        i += 1
                            return ins
                        yield
                        P.add("pe", mm2, r=[kpm, "vpad", "cstb"], w=[knd])
                        ndv = nd_[:, 0:256].rearrange("p (a q) -> p a q", a=2)
                        if pi == 0:
                            P.add("act", lambda e, ndv=ndv, qs=qs: e.copy(out=acc[:, :, qs], in_=ndv), r=[knd], w=["acc"])
                        else:
                            P.add("dve", lambda e, ndv=ndv, qs=qs: e.tensor_tensor(out=acc[:, :, qs], in0=acc[:, :, qs], in1=ndv,
                                                                                  op=ALU.add), r=[knd, "acc"], w=["acc"])
                prev = None
                for r in range(dil):
                    for b in range(nb):
                        it += 1
                        g = block_iter(r, b, it % 2)
                        next(g)
                        if prev is not None:
                            for _ in prev:
                                pass
                        prev = g
                for _ in prev:
                    pass
            for tq in range(NT // 512):
                sl = slice(tq * 512, (tq + 1) * 512)
                o_, ko = ost[tq % 2], ("ost", tq % 2)
                P.add("dve", lambda e, sl=sl: e.reciprocal(rden[:], acc[:, 1, sl]), r=["acc"], w=["rden"])
                P.add("dve", lambda e, sl=sl, o_=o_: e.tensor_tensor(out=o_[:], in0=acc[:, 0, sl], in1=rden[:], op=ALU.mult),
                      r=["acc", "rden"], w=[ko])
                P.add("pool", lambda e, sl=sl, o_=o_, hp=hp: [e.dma_start(out=st["MIXT"][hp, :, sl], in_=o_[:])],
                      r=[ko], w=[ko + ("d",)], slot=ko)
        P.flush()
        es.close()

    def dn_stage1(self, l):
        nc, P, st = self.nc, self.P, self.st
        NT, SEG, PADW = self.NT, self.SEG, self.st["PADW"]
        C = self.C
        ones_b = st["ones_b"]
        es = ExitStack()
        NG = 4
        xin = [self.sb(es, "xin", [128, 516], F32) for _ in range(2 * NG)]
        ca = [self.sb(es, "ca", [128, 512], F32) for _ in range(NG)]
        yv = [self.sb(es, "yv", [128, 512], F32) for _ in range(NG)]
        sqb = [self.sb(es, "sqb", [128, 512], BF16) for _ in range(NG)]
        rn0 = [self.sb(es, "rn0", [128, 512], F32) for _ in range(NG)]
        rn = [self.sb(es, "rn", [128, 512], F32) for _ in range(NG)]
        yn = [self.sb(es, "yn", [128, 512], F32) for _ in range(NG)]
        tst = [self.sb(es, "tst", [128, 4, 128], F32) for _ in range(NG)]
        bss = [self.ps(es, "bss", [128, 512]) for _ in range(NG)]
        btr = [self.ps(es, "btr", [128, 512]) for _ in range(NG)]
        cw = st["convw"]
        groups = [(t, kind) for t in range(NT // TT) for kind in range(3)]

        def loads(gi):
            t, kind = groups[gi]
            tok0 = t * TT
            s_ = tok0 // SEG
            col0 = s_ * PADW + (tok0 - s_ * SEG)
            for i in range(NG):
                cidx = kind * 4 + i
                xi, kxi = xin[(gi % 2) * NG + i], ("xin", (gi % 2) * NG + i)
                P.add("sp", lambda e, xi=xi, cidx=cidx, col0=col0: [e.dma_start(out=xi[:], in_=st["PDN"][cidx, :, col0:col0 + 516])],
                      w=[kxi], slot=kxi)
        loads(0)
        for gi, (t, kind) in enumerate(groups):
            tok0 = t * TT
            if gi + 1 < len(groups):
                loads(gi + 1)
            X = [(xin[(gi % 2) * NG + i], ("xin", (gi % 2) * NG + i)) for i in range(NG)]
            cids = [kind * 4 + i for i in range(NG)]
            for j in range(5):
                for i in range(NG):
                    xi, kxi = X[i]
                    a_, ka, cidx = ca[i], ("ca", i), cids[i]
                    if j == 0:
                        P.add("dve", lambda e, a_=a_, xi=xi, cidx=cidx: e.tensor_scalar(a_[:], xi[:, 0:512], cw[:, l, 0, cidx:cidx + 1], None, op0=ALU.mult),
                              r=[kxi, "coef"], w=[ka])
                    else:
                        P.add("dve", lambda e, a_=a_, xi=xi, cidx=cidx, j=j: e.scalar_tensor_tensor(
                            out=a_[:], in0=xi[:, j:j + 512], scalar=cw[:, l, j, cidx:cidx + 1], in1=a_[:], op0=ALU.mult, op1=ALU.add),
                            r=[kxi, ka, "coef"], w=[ka])
            for i in range(NG):
                P.add("act", lambda e, i=i: e.activation(out=yv[i][:], in_=ca[i][:], func=AF.Silu), r=[("ca", i)], w=[("yv", i)])
            src = [(yv[i], ("yv", i)) for i in range(NG)]
            if kind < 2:
                for i in range(NG):
                    P.add("act", lambda e, i=i: e.activation(out=sqb[i][:], in_=yv[i][:], func=AF.Square), r=[("yv", i)], w=[("sqb", i)])
                for i in range(NG):
                    P.add("pe", lambda e, i=i: e.matmul(bss[i][:], lhsT=ones_b[:], rhs=sqb[i][:], start=True, stop=True),
                          r=[("sqb", i), "ones_b"], w=[("bss", i)])
                for i in range(NG):
                    P.add("act", lambda e, i=i: e.activation(out=rn0[i][:], in_=bss[i][:], func=AF.Sqrt, bias=self.epsc[:, 1:2], scale=1.0),
                          r=[("bss", i)], w=[("rn0", i)])
                for i in range(NG):
                    P.add("dve", lambda e, i=i: e.reciprocal(rn[i][:], rn0[i][:]), r=[("rn0", i)], w=[("rn", i)])
                scl = float(DK ** -0.5) if kind == 0 else 1.0
                for i in range(NG):
                    P.add("dve", lambda e, i=i, scl=scl: e.scalar_tensor_tensor(out=yn[i][:], in0=yv[i][:], scalar=scl, in1=rn[i][:],
                                                                               op0=ALU.mult, op1=ALU.mult),
                          r=[("yv", i), ("rn", i)], w=[("yn", i)])
                dstT = st["DNQT"] if kind == 0 else st["DNKT"]
                for i in range(NG):
                    P.add("pool", lambda e, i=i, dstT=dstT, tok0=tok0: [e.dma_start(out=dstT[i, :, tok0:tok0 + TT], in_=yn[i][:])],
                          r=[("yn", i)], w=[("yn", i, "d")], slot=("yn", i))
                src = [(yn[i], ("yn", i)) for i in range(NG)]
            if kind >= 1:
                for i in range(NG):
                    sr, ks = src[i]

                    def tr(e, i=i, sr=sr):
                        for sb_ in range(4):
                            ins = e.transpose(btr[i][:, sb_ * 128:(sb_ + 1) * 128], sr[:, sb_ * 128:(sb_ + 1) * 128], C(0))
                        return ins
                    P.add("pe", tr, r=[ks, "cst"], w=[("btr", i)])
                for i in range(NG):
                    P.add("act", lambda e, i=i: e.copy(out=tst[i][:], in_=btr[i][:].rearrange("p (a d) -> p a d", a=4)),
                          r=[("btr", i)], w=[("tst", i)])
                dstM = st["DNK"] if kind == 1 else st["DNV"]
                for i in range(NG):
                    P.add("pool", lambda e, i=i, dstM=dstM, tok0=tok0: [e.dma_start(
                        out=dstM[tok0:tok0 + TT, i * 128:(i + 1) * 128].rearrange("(a p) d -> p a d", p=128), in_=tst[i][:])],
                        r=[("tst", i)], w=[("tst", i, "d")], slot=("tst", i))
        P.flush()
        es.close()

    def phase_dn(self, l):
        nc, P, st = self.nc, self.P, self.st
        NT, SEG, PADW = self.NT, self.SEG, self.st["PADW"]
        C, CB = self.C, self.CB
        flag = st["flag"]
        ones_b = st["ones_b"]
        self.dn_stage1(l)
        if self.debug == "dn1":
            return
        es = ExitStack()
        NCH = NT // 128

        def T2(name, shape=(128, 512), dt=F32, n=2):
            return [self.sb(es, name, list(shape), dt) for _ in range(n)]
        kT4, qT4, k4, v4 = T2("kT4"), T2("qT4"), T2("k4"), T2("v4")
        gb = T2("gb", (128, 16))
        eg = T2("eg", (128, 12))
        bege = T2("bege", (128, 4))
        G4, nabs, E, EA, tA, R, Pm, X = T2("G4"), T2("nabs"), T2("E"), T2("EA"), T2("tA"), T2("R", n=4), T2("Pm", n=4), T2("X")
        Vb4, Kbg4, kdec4, u4, nw4, EQ, qk4, vn4, o1s, o4 = (T2("Vb4"), T2("Kbg4"), T2("kdec4"), T2("u4"), T2("nw4"), T2("EQ"),
                                                           T2("qk4"), T2("vn4"), T2("o1s"), T2("o4"))
        S = T2("S")
        m4 = {nm: self.sb(es, "m4" + nm, [128, 512], F32) for nm in ("Ui", "Li", "Us", "Ls", "I", "bd", "o16", "o32", "o64")}
        Ad, Aoff = T2("Ad"), [T2("Ao16"), T2("Ao32"), T2("Ao64")]
        Wt, M1 = T2("Wt"), T2("M1")
        banks = [self.ps(es, "bank", [128, 512]) for _ in range(8)]
        bstate = dict(i=0)

        def nbank():
            i = bstate["i"] % 8
            bstate["i"] += 1
            return banks[i], ("bank", i)

        def v3(t_):
            return t_[:].rearrange("p (h d) -> p h d", h=4)

        def bc(ap4):
            return ap4.unsqueeze(2).broadcast_to([128, 4, 128])

        def m4build(e):
            for nm, ci in (("Ui", 2), ("Li", 3), ("Us", 4), ("Ls", 5), ("I", 0), ("bd", 12), ("o16", 13), ("o32", 14), ("o64", 15)):
                for hh in range(4):
                    ins = e.tensor_copy(m4[nm][:, hh * 128:(hh + 1) * 128], C(ci))
            return ins
        P.add("dve", m4build, r=["cst"], w=["m4"])
        for d in range(2):
            if DN_R != "none":
                P.add("dve", lambda e, d=d: e.tensor_scalar(S[d][:].bitcast(F32R), m4["I"][:], 0.0, None, op0=ALU.mult), r=["m4"], w=[("S", d)])
            else:
                P.add("dve", lambda e, d=d: e.memset(S[d][:], 0.0), w=[("S", d)])

        cut = float(self.debug.split(":")[1]) if (self.debug and self.debug.startswith("dn2:")) else None

        def step(c, d, par):
            rO = (lambda a: a.bitcast(F32R)) if DN_R in ("outer", "all") else (lambda a: a)
            rI = (lambda a: a.bitcast(F32R)) if DN_R == "all" else (lambda a: a)
            wO = rO
            wI = rI
            wX = rO
            Mincl = C(2) if d == 0 else C(3)
            Mrem = C(5) if d == 0 else C(4)
            MA4 = m4["Ls"] if d == 0 else m4["Us"]
            MQ4 = m4["Ui"] if d == 0 else m4["Li"]
            tk = slice(c * 128, (c + 1) * 128)
            K = lambda nm: (nm, par)
            P.add("sp", lambda e: [e.dma_start(out=v3(kT4[par]), in_=st["DNKT"][:, :, tk].rearrange("h p t -> p h t")),
                                   e.dma_start(out=v3(qT4[par]), in_=st["DNQT"][:, :, tk].rearrange("h p t -> p h t")),
                                   e.dma_start(out=k4[par][:], in_=st["DNK"][tk, :]),
                                   e.dma_start(out=v4[par][:], in_=st["DNV"][tk, :]),
                                   e.dma_start(out=gb[par][:], in_=st["GB"][tk, :])],
                  w=[K("ld")], slot=K("ld"), ndma=5)
            g4 = gb[par][:, d * 4:(d + 1) * 4]
            beta4 = gb[par][:, 8 + d * 4:8 + (d + 1) * 4]
            gp, kgp = nbank()

            def mm1(e):
                e.matmul(gp[:, 0:4], lhsT=Mincl, rhs=g4, start=True, stop=True)
                e.matmul(gp[:, 4:8], lhsT=Mrem, rhs=g4, start=True, stop=True)
                return e.matmul(gp[:, 8:12], lhsT=C(1), rhs=g4, start=True, stop=True)
            P.add("pe", mm1, r=[K("ld"), "cst"], w=[kgp])
            P.add("act", lambda e: e.activation(out=eg[par][:], in_=gp[:, 0:12], func=AF.Exp), r=[kgp], w=[K("eg")])
            egc, erem, etot = eg[par][:, 0:4], eg[par][:, 4:8], eg[par][:, 8:12]
            P.add("dve", lambda e: e.tensor_tensor(out=bege[par][:], in0=beta4, in1=egc, op=ALU.mult), r=[K("ld"), K("eg")], w=[K("bege")])
            yield
            if cut is not None and cut <= 1:
                return
            P.add("dve", lambda e: e.tensor_tensor(out=v3(G4[par]), in0=v3(m4["Ui" if d == 0 else "Li"]), in1=bc(g4), op=ALU.mult),
                  r=[K("ld"), "m4"], w=[K("G4")])
            Dp, kDp = nbank()

            def mm2(e):
                for hh in range(4):
                    sl = slice(hh * 128, (hh + 1) * 128)
                    e.matmul(Dp[:, sl], lhsT=G4[par][:, sl], rhs=C(1), start=True, stop=False)
                    ins = e.matmul(Dp[:, sl], lhsT=C(6), rhs=G4[par][:, sl], start=False, stop=True)
                return ins
            P.add("pe", mm2, r=[K("G4"), "cst"], w=[kDp])
            P.add("dve", lambda e: e.tensor_scalar(tA[par][:], Dp[:], 0.0, None, op0=ALU.min), r=[kDp, K("tA")], w=[K("tA")])
            P.add("dve", lambda e: e.scalar_tensor_tensor(out=nabs[par][:], in0=Dp[:], scalar=0.0, in1=tA[par][:], op0=ALU.max, op1=ALU.subtract),
                  r=[kDp, K("tA")], w=[K("nabs")])
            P.add("act", lambda e: e.activation(out=E[par][:], in_=nabs[par][:], func=AF.Exp, scale=-1.0), r=[K("nabs")], w=[K("E")])
            yield
            if cut is not None and cut <= 2:
                return
            kk, kkk = nbank()

            def mm3(e):
                for hh in range(4):
                    sl = slice(hh * 128, (hh + 1) * 128)
                    ins = e.matmul(kk[:, sl], lhsT=rO(kT4[par][:, sl]), rhs=rO(kT4[par][:, sl]), start=True, stop=True)
                return ins
            P.add("pe", mm3, r=[K("ld")], w=[kkk])
            P.add("dve", lambda e: e.tensor_tensor(out=EA[par][:], in0=E[par][:], in1=MA4[:], op=ALU.mult), r=[K("E"), "m4"], w=[K("EA")])
            P.add("dve", lambda e: e.tensor_tensor(out=tA[par][:], in0=kk[:], in1=EA[par][:], op=ALU.mult), r=[kkk, K("EA")], w=[K("tA")])
            r0 = R[par * 2]
            P.add("dve", lambda e: e.tensor_tensor(out=wI(v3(r0)), in0=v3(tA[par]), in1=bc(beta4), op=ALU.mult),
                  r=[K("tA"), K("ld")], w=[("R", par * 2)])
            yield
            if cut is not None and cut <= 3:
                return
            P.add("dve", lambda e: e.tensor_tensor(out=wI(Ad[par][:]), in0=r0[:], in1=m4["bd"][:], op=ALU.mult),
                  r=[("R", par * 2), "m4"], w=[K("Ad")])
            for li, nm in enumerate(("o16", "o32", "o64")):
                P.add("dve", lambda e, li=li, nm=nm: e.tensor_tensor(out=wI(Aoff[li][par][:]), in0=r0[:], in1=m4[nm][:], op=ALU.mult),
                      r=[("R", par * 2), "m4"], w=[K("Ao%d" % li)])
            Bp, kBp = nbank()

            def mm4(e):
                for hh in range(4):
                    sl = slice(hh * 128, (hh + 1) * 128)
                    ins = e.transpose(Bp[:, sl], Ad[par][:, sl], C(0))
                return ins
            P.add("pe", mm4, r=[K("Ad"), "cst"], w=[kBp])
            p0 = Pm[par * 2]
            P.add("act", lambda e: e.copy(out=wI(p0[:]), in_=Bp[:]), r=[kBp], w=[("Pm", par * 2)])
            P.add("dve", lambda e: e.scalar_tensor_tensor(out=wX(X[par][:]), in0=p0[:], scalar=-1.0, in1=m4["I"][:], op0=ALU.mult, op1=ALU.add),
                  r=[("Pm", par * 2), "m4"], w=[K("X")])
            yield
            if cut is not None and cut <= 3.2:
                return
            Rb = [(Ad[par], K("Ad")), (R[par * 2 + 1], ("R", par * 2 + 1)), (R[par * 2], ("R", par * 2)), (R[par * 2 + 1], ("R", par * 2 + 1))]
            Pb = [(Pm[par * 2], ("Pm", par * 2)), (Pm[par * 2 + 1], ("Pm", par * 2 + 1)), (Pm[par * 2], ("Pm", par * 2))]
            NLEV = 3
            for lev in range(1, NLEV + 1):
                (Rc, kRc), (Pc, kPc) = Rb[lev - 1], Pb[lev - 1]
                Rn, kRn = Rb[lev]
                Pp, kPp = nbank()
                Rp, kRp = nbank()

                def mm5(e, Rc=Rc, Pc=Pc, Pp=Pp, Rp=Rp, lev=lev):
                    for hh in range(4):
                        sl = slice(hh * 128, (hh + 1) * 128)
                        if lev < NLEV:
                            e.matmul(Pp[:, sl], lhsT=rI(Rc[:, sl]), rhs=rI(Pc[:, sl]), start=True, stop=True)
                        ins = e.matmul(Rp[:, sl], lhsT=rI(Pc[:, sl]), rhs=rI(Rc[:, sl]), start=True, stop=True)
                    return ins
                P.add("pe", mm5, r=[kRc, kPc], w=([kPp, kRp] if lev < NLEV else [kRp]))
                if lev < NLEV:
                    Pn, kPn = Pb[lev]
                    P.add("act", lambda e, Pn=Pn, Pp=Pp: e.copy(out=wI(Pn[:]), in_=Pp[:]), r=[kPp], w=[kPn])
                P.add("dve", lambda e, Rn=Rn, Rp=Rp: e.tensor_copy(wI(Rn[:]), Rp[:]), r=[kRp], w=[kRn])
                Xp, kXp = nbank()

                def mm6(e, Rn=Rn, Xp=Xp):
                    for hh in range(4):
                        sl = slice(hh * 128, (hh + 1) * 128)
                        ins = e.matmul(Xp[:, sl], lhsT=rI(Rn[:, sl]), rhs=rI(X[par][:, sl]), start=True, stop=True)
                    return ins
                P.add("pe", mm6, r=[kRn, K("X")], w=[kXp])
                P.add("dve", lambda e, Xp=Xp: e.tensor_tensor(out=wX(X[par][:]), in0=X[par][:], in1=Xp[:], op=ALU.add),
                      r=[kXp, K("X")], w=[K("X")])
                yield
                if cut is not None and cut <= 3.2 + 0.2 * lev:
                    return
            yield
            if cut is not None and cut <= 4:
                return
            for li in range(3):
                Wp, kWp = nbank()

                def mmw(e, Wp=Wp):
                    for hh in range(4):
                        sl = slice(hh * 128, (hh + 1) * 128)
                        ins = e.transpose(Wp[:, sl], X[par][:, sl], C(0))
                    return ins
                P.add("pe", mmw, r=[K("X"), "cst"], w=[kWp])
                P.add("act", lambda e, Wp=Wp: e.copy(out=wI(Wt[par][:]), in_=Wp[:]), r=[kWp], w=[K("Wt")])
                M1p, kM1p = nbank()

                def mmm(e, M1p=M1p, li=li):
                    for hh in range(4):
                        sl = slice(hh * 128, (hh + 1) * 128)
                        ins = e.matmul(M1p[:, sl], lhsT=rI(Aoff[li][par][:, sl]), rhs=rI(X[par][:, sl]), start=True, stop=True)
                    return ins
                P.add("pe", mmm, r=[K("Ao%d" % li), K("X")], w=[kM1p])
                P.add("act", lambda e, M1p=M1p: e.copy(out=wI(M1[par][:]), in_=M1p[:]), r=[kM1p], w=[K("M1")])
                X2p, kX2p = nbank()

                def mmx(e, X2p=X2p):
                    for hh in range(4):
                        sl = slice(hh * 128, (hh + 1) * 128)
                        ins = e.matmul(X2p[:, sl], lhsT=rI(Wt[par][:, sl]), rhs=rI(M1[par][:, sl]), start=True, stop=True)
                    return ins
                P.add("pe", mmx, r=[K("Wt"), K("M1")], w=[kX2p])
                P.add("dve", lambda e, X2p=X2p: e.tensor_tensor(out=wX(X[par][:]), in0=X[par][:], in1=X2p[:], op=ALU.subtract),
                      r=[kX2p, K("X")], w=[K("X")])
                yield
            yield
            if cut is not None and cut <= 5:
                return
            P.add("dve", lambda e: e.tensor_tensor(out=wO(v3(Vb4[par])), in0=v3(v4[par]), in1=bc(beta4), op=ALU.mult), r=[K("ld")], w=[K("Vb4")])
            P.add("dve", lambda e: e.tensor_tensor(out=wO(v3(Kbg4[par])), in0=v3(k4[par]), in1=bc(bege[par][:]), op=ALU.mult),
                  r=[K("ld"), K("bege")], w=[K("Kbg4")])
            P.add("dve", lambda e: e.tensor_tensor(out=wO(v3(kdec4[par])), in0=v3(k4[par]), in1=bc(erem), op=ALU.mult),
                  r=[K("ld"), K("eg")], w=[K("kdec4")])
            up, kup = nbank()
            wp, kwp = nbank()

            def mm7(e):
                for hh in range(4):
                    sl = slice(hh * 128, (hh + 1) * 128)
                    e.matmul(up[:, sl], lhsT=rO(X[par][:, sl]), rhs=rO(Vb4[par][:, sl]), start=True, stop=True)
                    ins = e.matmul(wp[:, sl], lhsT=rO(Kbg4[par][:, sl]), rhs=rO(X[par][:, sl]), start=True, stop=True)
                return ins
            P.add("pe", mm7, r=[K("X"), K("Vb4"), K("Kbg4")], w=[kup, kwp])
            P.add("act", lambda e: e.copy(out=u4[par][:], in_=up[:]), r=[kup], w=[K("u4")])
            P.add("act", lambda e: e.mul(out=wO(nw4[par][:]), in_=wp[:], mul=-1.0), r=[kwp], w=[K("nw4")])
            yield
            if cut is not None and cut <= 6:
                return
            qkp, kqkp = nbank()

            def mm8(e):
                for hh in range(4):
                    sl = slice(hh * 128, (hh + 1) * 128)
                    ins = e.matmul(qkp[:, sl], lhsT=rO(kT4[par][:, sl]), rhs=rO(qT4[par][:, sl]), start=True, stop=True)
                return ins
            P.add("pe", mm8, r=[K("ld")], w=[kqkp])
            P.add("dve", lambda e: e.tensor_tensor(out=EQ[par][:], in0=E[par][:], in1=MQ4[:], op=ALU.mult), r=[K("E"), "m4"], w=[K("EQ")])
            P.add("dve", lambda e: e.tensor_tensor(out=wO(qk4[par][:]), in0=qkp[:], in1=EQ[par][:], op=ALU.mult), r=[kqkp, K("EQ")], w=[K("qk4")])
            yield
            if cut is not None and cut <= 7:
                return
            Sd, kS = S[d], ("S", d)
            link = (c == NCH // 2) if d == 0 else (c == NCH // 2 - 1)
            if link:
                P.add("dve", lambda e: e.tensor_scalar(wO(Sd[:]), Sd[:], flag[:, 0:1], None, op0=ALU.mult), r=[kS, "flag"], w=[kS])
            vnp, kvnp = nbank()
            O1p, kO1p = nbank()

            def mm9(e):
                for hh in range(4):
                    sl = slice(hh * 128, (hh + 1) * 128)
                    e.matmul(vnp[:, sl], lhsT=rO(nw4[par][:, sl]), rhs=rO(Sd[:, sl]), start=True, stop=True)
                    ins = e.matmul(O1p[:, sl], lhsT=rO(qT4[par][:, sl]), rhs=rO(Sd[:, sl]), start=True, stop=True)
                return ins
            P.add("pe", mm9, r=[K("nw4"), K("ld"), kS], w=[kvnp, kO1p])
            P.add("dve", lambda e: e.tensor_tensor(out=wO(vn4[par][:]), in0=vnp[:], in1=u4[par][:], op=ALU.add), r=[kvnp, K("u4")], w=[K("vn4")])
            P.add("dve", lambda e: e.tensor_tensor(out=v3(o1s[par]), in0=O1p[:].rearrange("p (h d) -> p h d", h=4), in1=bc(egc), op=ALU.mult),
                  r=[kO1p, K("eg")], w=[K("o1s")])
            yield
            O2p, kO2p = nbank()
            dSp, kdSp = nbank()

            def mm10(e):
                for hh in range(4):
                    sl = slice(hh * 128, (hh + 1) * 128)
                    e.matmul(O2p[:, sl], lhsT=rO(qk4[par][:, sl]), rhs=rO(vn4[par][:, sl]), start=True, stop=True)
                    ins = e.matmul(dSp[:, sl], lhsT=rO(kdec4[par][:, sl]), rhs=rO(vn4[par][:, sl]), start=True, stop=True)
                return ins
            P.add("pe", mm10, r=[K("qk4"), K("vn4"), K("kdec4")], w=[kO2p, kdSp])
            P.add("dve", lambda e: e.tensor_tensor(out=o4[par][:], in0=O2p[:], in1=o1s[par][:], op=ALU.add), r=[kO2p, K("o1s")], w=[K("o4")])
            P.add("pool", lambda e: [e.dma_start(out=st["OFB"][d, tk, :], in_=o4[par][:])], r=[K("o4")], w=[K("o4d")], slot=K("o4"))
            P.add("dve", lambda e: e.tensor_tensor(out=wO(v3(Sd)), in0=v3(Sd), in1=bc(etot), op=ALU.mult), r=[kS, K("eg")], w=[kS])
            P.add("dve", lambda e: e.tensor_tensor(out=wO(Sd[:]), in0=Sd[:], in1=dSp[:], op=ALU.add), r=[kS, kdSp], w=[kS])

        for sidx in range(NCH if cut is None else 1):
            gens = [step(sidx, 0, 0)] + ([step(NCH - 1 - sidx, 1, 1)] if cut is None else [])
            while gens:
                for g in list(gens):
                    try:
                        next(g)
                    except StopIteration:
                        gens.remove(g)
        P.flush()
        es.close()

        if self.debug and self.debug.startswith("dn2"):
            return
        es = ExitStack()
        of_ = [self.sb(es, "of", [128, 512], F32) for _ in range(2)]
        ob_ = [self.sb(es, "ob", [128, 512], F32) for _ in range(2)]
        zs_ = [self.sb(es, "zs", [128, 512], F32) for _ in range(2)]
        osum = self.sb(es, "osum", [128, 512], F32)
        osq = self.sb(es, "osq", [128, 512], F32)
        ss = self.sb(es, "ss", [128, 4], F32)
        rs = self.sb(es, "rs", [128, 4], F32)
        y1 = self.sb(es, "y1", [128, 512], F32)
        y2 = self.sb(es, "y2", [128, 512], F32)
        y3 = self.sb(es, "y3", [128, 512], F32)
        mst = [self.sb(es, "mst", [128, 4, 128], BF16) for _ in range(2)]
        banks = [self.ps(es, "bank", [128, 512]) for _ in range(2)]
        dnw = st["dnw"]

        def v3b(t_):
            return t_[:].rearrange("p (h d) -> p h d", h=4)
        for t in range(NT // 128):
            par = t % 2
            tk = slice(t * 128, (t + 1) * 128)
            kl = ("s3ld", par)
            P.add("sp", lambda e, par=par, tk=tk: [e.dma_start(out=of_[par][:], in_=st["OFB"][0, tk, :]),
                                                  e.dma_start(out=ob_[par][:], in_=st["OFB"][1, tk, :]),
                                                  e.dma_start(out=zs_[par][:], in_=st["ZS"][tk, :])],
                  w=[kl], slot=kl, ndma=3)
            P.add("dve", lambda e, par=par: e.tensor_tensor(out=osum[:], in0=of_[par][:], in1=ob_[par][:], op=ALU.add), r=[kl], w=["osum"])
            P.add("dve", lambda e: e.tensor_tensor(out=osq[:], in0=osum[:], in1=osum[:], op=ALU.mult), r=["osum"], w=["osq"])
            P.add("dve", lambda e: e.reduce_sum(out=ss[:], in_=v3b(osq), axis=AX.X), r=["osq"], w=["ss"])
            P.add("dve", lambda e: e.tensor_scalar(rs[:], ss[:], 1.0 / 128.0, float(EPS), op0=ALU.mult, op1=ALU.add), r=["ss"], w=["rs0"])
            P.add("act", lambda e: e.activation(out=ss[:], in_=rs[:], func=AF.Sqrt), r=["rs0", "ss"], w=["ss"])
            P.add("dve", lambda e: e.reciprocal(rs[:], ss[:]), r=["ss", "rs0"], w=["rs"])
            P.add("dve", lambda e: e.tensor_tensor(out=v3b(y1), in0=v3b(osum), in1=rs[:].unsqueeze(2).broadcast_to([128, 4, 128]), op=ALU.mult),
                  r=["osum", "rs"], w=["y1"])
            P.add("dve", lambda e: e.tensor_tensor(out=y2[:], in0=y1[:], in1=dnw[:, l, :], op=ALU.mult), r=["y1", "coef"], w=["y2"])
            P.add("dve", lambda e, par=par: e.tensor_tensor(out=y3[:], in0=y2[:], in1=zs_[par][:], op=ALU.mult), r=["y2", kl], w=["y3"])
            bk, kbk = banks[par], ("bank", par)

            def tr(e, bk=bk):
                for hh in range(4):
                    ins = e.transpose(bk[:, hh * 128:(hh + 1) * 128], y3[:, hh * 128:(hh + 1) * 128], C(0))
                return ins
            P.add("pe", tr, r=["y3", "cst"], w=[kbk])
            km = ("mst", par)
            P.add("act", lambda e, bk=bk, par=par: e.copy(out=mst[par][:], in_=bk[:].rearrange("p (h d) -> p h d", h=4)), r=[kbk], w=[km])
            P.add("pool", lambda e, par=par, tk=tk: [e.dma_start(out=st["MIXT"][4:8, :, tk].rearrange("h p t -> p h t"), in_=mst[par][:])],
                  r=[km], w=[km + ("d",)], slot=km)
        P.flush()
        es.close()


def make_consts():
    k = np.arange(128)[:, None]
    m = np.arange(128)[None, :]
    t = np.zeros((16, 128, 128), np.float32)
    t[0] = (k == m)
    t[1] = 1.0
    t[2] = (k <= m)
    t[3] = (k >= m)
    t[4] = (k < m)
    t[5] = (k > m)
    t[6] = -1.0
    t[7] = (np.abs(k - m) <= 64)
    t[8] = (k >= m + 64)
    t[9] = (k <= m - 64)
    t[10] = (m < 64) * np.ones((128, 1))
    t[11] = (m >= 64) * np.ones((128, 1))
    t[12] = (k // 16 == m // 16)
    t[13] = (k // 32 == m // 32) & (k // 16 != m // 16)
    t[14] = (k // 64 == m // 64) & (k // 32 != m // 32)
    t[15] = (k // 64 != m // 64)
    return np.ascontiguousarray(t.transpose(1, 0, 2).reshape(128, 16 * 128)).astype(np.float32)


def make_rope(pos):
    half = 8
    inv = (np.float32(ROPE_THETA) ** (-(np.arange(half, dtype=np.float32) / np.float32(half)))).astype(np.float32)
    ang = pos.astype(np.float32)[None, :] * inv[:, None]
    cos = np.cos(ang).astype(np.float32)
    sin = np.sin(ang).astype(np.float32)
    NT = pos.shape[0]
    cosT = np.ones((128, NT), np.float32)
    sinT = np.zeros((128, NT), np.float32)
    for e in range(2):
        cosT[e * 64:e * 64 + 8] = cos
        cosT[e * 64 + 8:e * 64 + 16] = cos
        sinT[e * 64:e * 64 + 8] = -sin
        sinT[e * 64 + 8:e * 64 + 16] = sin
    return cosT, sinT


_WNAMES = ["ada_w", "ada_b", "norm_ffn1", "ffn1_w_gate", "ffn1_w_up", "ffn1_w_down", "norm_mix", "w_in", "conv_w",
           "a_log", "dt_bias", "dn_norm", "w_out", "norm_ffn2", "ffn2_w_gate", "ffn2_w_up", "ffn2_w_down", "norm_final"]


def core_inputs(xc, c2, cont, weights, depth):
    NT = xc.shape[0]
    seg = NT // 2
    pos = np.arange(NT) if cont else (np.arange(NT) % seg)
    cosT, sinT = make_rope(pos)
    m = {"x": np.ascontiguousarray(xc, dtype=np.float32), "c2": np.ascontiguousarray(c2, dtype=np.float32),
         "flag": np.full((128, 1), 1.0 if cont else 0.0, np.float32), "cosT": cosT, "sinT": sinT,
         "consts": make_consts()}
    for n in _WNAMES:
        w = np.asarray(weights[n], dtype=np.float32)
        if n in ("a_log", "dt_bias"):
            w = w.reshape(depth, 8)
        if n == "norm_final":
            w = w.reshape(1, D)
        m[n] = np.ascontiguousarray(w)
    return m


_NC_CACHE = {}


def kernel(**inputs):
    xp = np.asarray(inputs["x_prompt"], dtype=np.float32)
    xs = np.asarray(inputs["x_sample"], dtype=np.float32)
    cp = np.asarray(inputs["c_prompt"], dtype=np.float32)
    cs = np.asarray(inputs["c_sample"], dtype=np.float32)
    depth = np.asarray(inputs["ada_w"]).shape[0]
    Bp, Sp, _ = xp.shape
    Bs, Ss, _ = xs.shape
    seg = Sp
    assert Ss == 2 * Sp and Bp == 2 * Bs and Bp + Bs * 2 == 16 or True
    in_maps = []
    npc = Bp // 2
    for i in range(npc):
        xc = xp[2 * i:2 * i + 2].reshape(2 * Sp, D)
        in_maps.append(core_inputs(xc, cp[2 * i:2 * i + 2], False, inputs, depth))
    for i in range(Bs):
        xc = xs[i]
        in_maps.append(core_inputs(xc, np.stack([cs[i], cs[i]]), True, inputs, depth))
    key = (seg, depth)
    if key not in _NC_CACHE:
        _NC_CACHE[key] = Builder(seg, depth).build()
    nc = _NC_CACHE[key]
    res = run_bass_kernel_spmd(nc, in_maps, core_ids=list(range(len(in_maps))))
    ys = [r["y"] for r in res.results]
    y_prompt = np.stack(ys[:npc]).reshape(Bp, Sp, D).astype(np.float32)
    y_sample = np.stack(ys[npc:]).reshape(Bs, Ss, D).astype(np.float32)
    return (y_prompt, y_sample)
```

```python
import numpy as np
from contextlib import ExitStack
import concourse.bass as bass
import concourse.mybir as mybir
from concourse.bass_utils import run_bass_kernel_spmd

F32 = mybir.dt.float32
BF16 = mybir.dt.bfloat16
AF = mybir.ActivationFunctionType
ALU = mybir.AluOpType
AX = mybir.AxisListType

D = 1024
FF = 2816
NFC = FF // 128
NH = 8
HD = 64
NDH = 4
DK = 128
INW = 3600
EPS = 1e-6
ROPE_THETA = 500000.0
TT = 512
NEG = -30000.0
F32R = mybir.dt.float32r
DN_R = "none"


class Prog:
    CENG = ("pe", "act", "dve", "pool")

    def __init__(self, nc, es):
        self.nc = nc
        self.es = es
        self.ops = []
        self.last_w = {}
        self.readers = {}
        self.flushed = 0
        self.sems = {}
        self.cnt = {}
        self.seen = {}
        self.nblock = 0

    def _sem(self, key):
        if key not in self.sems:
            self.sems[key] = self.es.enter_context(self.nc.semaphore("s%d" % len(self.sems)))
            self.cnt[key] = 0
        return self.sems[key]

    def add(self, eng, fn, r=(), w=(), slot=None, ndma=1):
        idx = len(self.ops)
        hard, war = set(), set()
        for k in r:
            if k in self.last_w:
                hard.add(self.last_w[k])
        for k in w:
            if k in self.last_w:
                hard.add(self.last_w[k])
            for i in self.readers.get(k, ()):
                war.add(i)
        for k in w:
            self.last_w[k] = idx
            self.readers[k] = []
        for k in r:
            if k not in w:
                self.readers.setdefault(k, []).append(idx)
        deps = set()
        isdma = slot is not None
        for d in hard:
            od = self.ops[d]
            if od["eng"] == eng and eng == "pe" and not isdma:
                continue
            deps.add(d)
        for d in war:
            od = self.ops[d]
            if od["eng"] == eng and eng == "pe" and not isdma and od["slot"] is None:
                continue
            deps.add(d)
        deps.discard(idx)
        deps = {d for d in deps if d >= self.flushed}
        for d in deps:
            self.ops[d]["users"] = True
        self.ops.append(dict(eng=eng, fn=fn, deps=deps, slot=slot, ndma=ndma, users=False, sig=None))
        return idx

    def flush(self):
        nc = self.nc
        ops = self.ops[self.flushed:]
        slot_map = {}
        for op in ops:
            if op["slot"] is not None:
                sk = (op["eng"], op["slot"])
                if sk not in slot_map:
                    slot_map[sk] = sum(1 for k2 in slot_map if k2[0] == op["eng"])
                key = ("dmap", op["eng"], slot_map[sk])
                self._sem(key)
                self.cnt[key] += 16 * op["ndma"]
                op["sig"] = (key, self.cnt[key])
            elif op["users"]:
                key = ("eng", op["eng"])
                self._sem(key)
                self.cnt[key] += 1
                op["sig"] = (key, self.cnt[key])
        engs = {"pe": "tensor", "act": "scalar", "dve": "vector", "pool": "gpsimd", "sp": "sync"}
        with nc.Block() as block:
            for eng, attr in engs.items():
                mine = [op for op in ops if op["eng"] == eng]

                def body(e, mine=mine, eng=eng):
                    seen = self.seen.setdefault(eng, {})
                    for op in mine:
                        need = {}
                        for d in op["deps"]:
                            sg = self.ops[d]["sig"]
                            assert sg is not None
                            if need.get(sg[0], 0) < sg[1]:
                                need[sg[0]] = sg[1]
                        for k, v in need.items():
                            if seen.get(k, 0) < v:
                                e.wait_ge(self.sems[k], v)
                                seen[k] = v
                        res = op["fn"](e)
                        if op["slot"] is not None:
                            assert res is not None and len(res) == op["ndma"], (len(res), op["ndma"])
                            for ins in res:
                                ins.then_inc(self.sems[op["sig"][0]], 16)
                        elif op["sig"] is not None:
                            ins = res[-1] if isinstance(res, (list, tuple)) else res
                            ins.then_inc(self.sems[op["sig"][0]], 1)
                        op["fn"] = None
                    for k, v in self.cnt.items():
                        if seen.get(k, 0) < v:
                            e.wait_ge(self.sems[k], v)
                            seen[k] = v
                getattr(block, attr)(body)
        self.flushed = len(self.ops)


class Builder:
    def __init__(self, seg, depth, debug=None):
        self.SEG = seg
        self.NT = 2 * seg
        self.depth = depth
        self.debug = debug
        self.nc = bass.Bass("TRN2", target_bir_lowering=False)
        try:
            self.nc.allow_low_precision("bf16 matmuls by design")
        except Exception:
            pass
        try:
            self.nc.allow_non_contiguous_dma("small strided param loads")
        except Exception:
            pass
        self.uid = 0

    def dram(self, name, shape, dt, kind="Internal"):
        return self.nc.dram_tensor(name, list(shape), dt, kind=kind).ap()

    def sb(self, es, name, shape, dt):
        self.uid += 1
        return es.enter_context(self.nc.sbuf_tensor("%s_%d" % (name, self.uid), list(shape), dt))

    def ps(self, es, name, shape, dt=F32):
        self.uid += 1
        return es.enter_context(self.nc.psum_tensor("%s_%d" % (name, self.uid), list(shape), dt))

    def build(self):
        nc = self.nc
        NT, SEG, depth = self.NT, self.SEG, self.depth
        L = depth
        inp = {}

        def ext(name, shape, dt=F32):
            inp[name] = self.dram(name, shape, dt, kind="ExternalInput")
            return inp[name]

        x_in = ext("x", [NT, D])
        c_in = ext("c2", [2, D])
        flag_in = ext("flag", [128, 1])
        cos_in = ext("cosT", [128, NT])
        sin_in = ext("sinT", [128, NT])
        cst_in = ext("consts", [128, 16 * 128])
        ada_w = ext("ada_w", [L, D, 9 * D])
        ada_b = ext("ada_b", [L, 9 * D])
        nrm1 = ext("norm_ffn1", [L, D])
        w1g = ext("ffn1_w_gate", [L, D, FF])
        w1u = ext("ffn1_w_up", [L, D, FF])
        w1d = ext("ffn1_w_down", [L, FF, D])
        nrm2 = ext("norm_mix", [L, D])
        w_in = ext("w_in", [L, D, INW])
        conv_w = ext("conv_w", [L, 5, 1536])
        a_log = ext("a_log", [L, 8])
        dt_bias = ext("dt_bias", [L, 8])
        dn_norm = ext("dn_norm", [L, 128])
        w_out = ext("w_out", [L, D, D])
        nrm3 = ext("norm_ffn2", [L, D])
        w2g = ext("ffn2_w_gate", [L, D, FF])
        w2u = ext("ffn2_w_up", [L, D, FF])
        w2d = ext("ffn2_w_down", [L, FF, D])
        nrmf = ext("norm_final", [1, D])
        y_out = self.dram("y", [NT, D], F32, kind="ExternalOutput")
        self.dbg_out = {}

        PADW = SEG + 4
        XT = self.dram("XT", [8, 128, NT], F32)
        X1T = self.dram("X1T", [8, 128, NT], F32)
        QT = self.dram("QT", [4, 128, NT], BF16)
        KT = self.dram("KT", [4, 128, NT], BF16)
        VP = self.dram("VP", [NT, 1024], BF16)
        PDN = self.dram("PDN", [12, 128, 2 * PADW], F32)
        ZS = self.dram("ZS", [NT, 512], F32)
        GB = self.dram("GB", [NT, 16], F32)
        MIXT = self.dram("MIXT", [8, 128, NT], BF16)
        DNQT = self.dram("DNQT", [4, 128, NT], F32)
        DNKT = self.dram("DNKT", [4, 128, NT], F32)
        DNK = self.dram("DNK", [NT, 512], F32)
        DNV = self.dram("DNV", [NT, 512], F32)
        OFB = self.dram("OFB", [2, NT, 512], F32)
        WS = {}
        for l in range(L):
            for f in (1, 2):
                for m in range(NFC):
                    WS[(l, "gu", f, m)] = self.dram("wgu%d_%d_%d" % (l, f, m), [128, 2048], BF16)
                for dc in range(8):
                    WS[(l, "dn", f, dc)] = self.dram("wdn%d_%d_%d" % (l, f, dc), [128, NFC * 128], BF16)
            for j in range(5):
                WS[(l, "inF", j)] = self.dram("winF%d_%d" % (l, j), [128, 4096], BF16)
            for j in range(2):
                WS[(l, "inP", j)] = self.dram("winP%d_%d" % (l, j), [128, 4096], BF16)
            for j in range(2):
                WS[(l, "inT", j)] = self.dram("winT%d_%d" % (l, j), [128, 4096], BF16)
            WS[(l, "inAB")] = self.dram("winAB%d" % l, [128, 128], BF16)
            for j in range(2):
                WS[(l, "out", j)] = self.dram("wout%d_%d" % (l, j), [128, 4096], BF16)

        es0 = ExitStack()
        self.es0 = es0
        P = Prog(nc, es0)
        self.P = P

        cst = self.sb(es0, "cst", [128, 16 * 128], F32)
        cstb = self.sb(es0, "cstb", [128, 16 * 128], BF16)
        flag = self.sb(es0, "flag", [128, 1], F32)
        modT = self.sb(es0, "modT", [128, L, 72, 2], F32)
        coef = self.sb(es0, "coef", [128, L, 9, 8, 2], F32)
        nfin = self.sb(es0, "nfin", [128, 8], F32)
        convw = self.sb(es0, "convw", [128, L, 5, 12], F32)
        gconst = self.sb(es0, "gconst", [128, L, 2, 8], F32)
        dnw = self.sb(es0, "dnw", [128, L, 512], F32)
        ones_b = self.sb(es0, "ones_b", [128, 128], BF16)
        epsD = self.sb(es0, "epsD", [128, 1], F32)

        def C(i):
            return cst[:, i * 128:(i + 1) * 128]

        def CB(i):
            return cstb[:, i * 128:(i + 1) * 128]
        self.C, self.CB = C, CB

        P.add("sp", lambda e: [e.dma_start(out=cst[:], in_=cst_in[:, :])], w=["cst"], slot="c0")
        P.add("sp", lambda e: [e.dma_start(out=flag[:], in_=flag_in[:, :])], w=["flag"], slot="c1")
        P.add("dve", lambda e: e.tensor_copy(cstb[:], cst[:]), r=["cst"], w=["cstb"])
        P.add("dve", lambda e: e.memset(ones_b[:], 1.0), w=["ones_b"])
        self.epsc = self.sb(es0, "epsc", [128, 4], F32)
        P.add("dve", lambda e: e.memset(self.epsc[:, 0:1], float(D * EPS)), w=["epsc0"])
        P.add("dve", lambda e: e.memset(self.epsc[:, 1:2], float(EPS)), w=["epsc1"])
        P.add("dve", lambda e: e.memset(self.epsc[:, 2:3], 1.0), w=["epsc2"])
        P.add("dve", lambda e: e.memset(self.epsc[:, 3:4], 0.0), w=["epsc3"])

        self.prologue_params(inp, modT, coef, nfin, convw, gconst, dnw, flag)
        P.flush()
        self.prologue_weights(inp, WS)
        P.flush()

        st = dict(XT=XT, X1T=X1T, QT=QT, KT=KT, VP=VP, PDN=PDN, ZS=ZS, GB=GB, MIXT=MIXT, DNQT=DNQT,
                  DNKT=DNKT, DNK=DNK, DNV=DNV, OFB=OFB, WS=WS, coef=coef, nfin=nfin, convw=convw,
                  gconst=gconst, dnw=dnw, ones_b=ones_b, flag=flag, x_in=x_in, y_out=y_out,
                  cos_in=cos_in, sin_in=sin_in, PADW=PADW)
        self.st = st
        for l in range(L):
            if self.debug == "P":
                break
            self.phase_A(l)
            P.flush()
            if self.debug == "A":
                break
            self.phase_attn(l)
            P.flush()
            if self.debug == "attn":
                break
            self.phase_dn(l)
            P.flush()
            if self.debug and self.debug.startswith("dn"):
                break
            self.phase_C(l)
            P.flush()
        if self.debug:
            self.dumps()
            P.flush()
        P.add("sp", lambda e: None, w=["Y"], slot=None)
        P.flush()
        es0.close()
        return nc

    def prologue_params(self, inp, modT, coef, nfin, convw, gconst, dnw, flag):
        nc, P, L = self.nc, self.P, self.depth
        es = ExitStack()
        cT = self.sb(es, "cT", [128, 8, 2], F32)
        scT = self.sb(es, "scT", [128, 8, 2], F32)
        adab = self.sb(es, "adab", [128, L, 72], F32)
        nw = self.sb(es, "nw", [128, L, 3, 8], F32)
        aw = [self.sb(es, "aw%d" % i, [128, 8, 512], F32) for i in range(2)]
        mps = self.ps(es, "mps", [128, 72, 2])
        tmpc = self.sb(es, "tmpc", [128, 8, 2], F32)

        c_in = inp["c2"]
        C = self.C
        prm = [self.sb(es, "prm%d" % i, [128, 128], F32) for i in range(3 + L)]
        tps = self.ps(es, "tps", [128, 512])

        def rows(ap2, i):
            return ap2[i:i + 1, :].rearrange("o (c p) -> (o c) p", p=128)

        def load_T(k, tile, srcs, nrows, col0, readers):
            P.add("sp", lambda e: [e.dma_start(out=tile[r0:r0 + n, :], in_=src) for (r0, n, src) in srcs],
                  w=[("prm", k)], slot=("prm", k), ndma=len(srcs))
            P.add("pe", lambda e: e.transpose(tps[:, col0:col0 + nrows], tile[0:nrows, :], C(0)[0:nrows, 0:nrows]),
                  r=[("prm", k), "cst"], w=["tps"])
        load_T(0, prm[0], [(s_ * 8, 8, rows(c_in, s_)) for s_ in range(2)], 16, 0, None)
        P.add("dve", lambda e: e.tensor_copy(cT[:].rearrange("p c s -> p s c"), tps[:, 0:16].rearrange("p (s c) -> p s c", s=2)),
              r=["tps"], w=["cT"])
        P.add("act", lambda e: e.activation(out=scT[:], in_=cT[:], func=AF.Silu), r=["cT"], w=["scT"])
        names = ["norm_ffn1", "norm_mix", "norm_ffn2"]
        srcs = [((i * L + l) * 8, 8, rows(inp[names[i]], l)) for i in range(3) for l in range(L)]
        srcs.append((3 * L * 8, 8, rows(inp["norm_final"], 0)))
        nr = 3 * L * 8 + 8
        load_T(1, prm[1], srcs, nr, 16, None)
        P.add("dve", lambda e: e.tensor_copy(nw[:].rearrange("p l i c -> p i l c"),
                                             tps[:, 16:16 + 3 * L * 8].rearrange("p (i l c) -> p i l c", i=3, l=L)),
              r=["tps"], w=["nw"])
        P.add("dve", lambda e: e.tensor_copy(nfin[:], tps[:, 16 + 3 * L * 8:16 + nr]), r=["tps"], w=["nfin"])
        for l in range(L):
            col0 = 16 + nr + l * 60
            load_T(2 + l, prm[2 + l], [(j * 12, 12, rows(inp["conv_w"][l], j)) for j in range(5)], 60, col0, None)
            P.add("dve", lambda e, l=l, col0=col0: e.tensor_copy(convw[:, l, :, :].rearrange("p j c -> p (j c)"), tps[:, col0:col0 + 60]),
                  r=["tps"], w=[("convw", l)])
        P.add("dve", lambda e: e.memset(tmpc[:], 0.0), r=[("convw", l) for l in range(L)], w=["convw", "tmpc"])
        tps2 = self.ps(es, "tps2", [128, 512])
        prmb = [self.sb(es, "prmb%d" % l, [128, 128], F32) for l in range(L)]
        for l in range(L):
            P.add("sp", lambda e, l=l: [e.dma_start(out=prmb[l][0:72, :], in_=rows(inp["ada_b"], l))], w=[("prmb", l)], slot=("prmb", l))
            P.add("pe", lambda e, l=l: e.transpose(tps2[:, l * 72:(l + 1) * 72], prmb[l][0:72, :], C(0)[0:72, 0:72]),
                  r=[("prmb", l), "cst"], w=["tps2"])
            P.add("dve", lambda e, l=l: e.tensor_copy(adab[:, l, :], tps2[:, l * 72:(l + 1) * 72]), r=["tps2"], w=[("adab", l)])
        P.add("dve", lambda e: e.memset(tmpc[:], 0.0), r=[("adab", l) for l in range(L)] + ["tmpc"], w=["adab", "tmpc"])
        P.add("sp", lambda e: [e.dma_start(out=gconst[:, l, i, :], in_=inp[nm][l:l + 1, :].partition_broadcast(128))
                               for l in range(L) for i, nm in enumerate(("a_log", "dt_bias"))],
              w=["gconst0"], slot="p5", ndma=2 * L)
        P.add("dve", lambda e: e.tensor_scalar_mul(nfin[:], nfin[:], float(np.sqrt(D))), r=["nfin"], w=["nfin"])
        P.add("act", lambda e: e.activation(out=gconst[:, :, 0, :], in_=gconst[:, :, 0, :], func=AF.Exp),
              r=["gconst0"], w=["gconst1"])
        P.add("dve", lambda e: e.tensor_scalar_mul(gconst[:, :, 0, :], gconst[:, :, 0, :], -1.0),
              r=["gconst1"], w=["gconst"])
        for l in range(L):
            P.add("sp", lambda e, l=l: [e.dma_start(out=dnw[:, l, h * 128:(h + 1) * 128],
                                                   in_=inp["dn_norm"][l:l + 1, :].partition_broadcast(128))
                                        for h in range(4)], w=["dnw"], slot="p6", ndma=4)
        for l in range(L):
            for pc in range(18):
                buf = aw[pc % 2]
                bk = ("aw", pc % 2)
                src = inp["ada_w"][l].rearrange("(kc p) n -> p kc n", p=128)[:, :, pc * 512:(pc + 1) * 512]
                P.add("sp", lambda e, buf=buf, src=src: [e.dma_start(out=buf[:], in_=src)], w=[bk], slot=bk)
                for jj in range(4):
                    j = pc * 4 + jj

                    def mm(e, buf=buf, jj=jj, j=j):
                        ins = None
                        for kc in range(8):
                            ins = e.matmul(mps[:, j, :], lhsT=buf[:, kc, jj * 128:(jj + 1) * 128], rhs=scT[:, kc, :],
                                           start=(kc == 0), stop=(kc == 7))
                        return ins
                    P.add("pe", mm, r=[bk, "scT"], w=[("mps", j)])
            P.add("dve", lambda e, l=l: e.tensor_tensor(out=modT[:, l, :, :], in0=mps[:],
                                                        in1=adab[:, l, :].unsqueeze(2).broadcast_to([128, 72, 2]),
                                                        op=ALU.add),
                  r=[("mps", j) for j in range(72)] + ["adab"], w=[("mps", j) for j in range(72)] + [("modT", l)])
        sqD = float(np.sqrt(D))
        for l in range(L):
            for i in range(3):
                sh = modT[:, l, (3 * i) * 8:(3 * i + 1) * 8, :]
                sc = modT[:, l, (3 * i + 1) * 8:(3 * i + 2) * 8, :]
                gt = modT[:, l, (3 * i + 2) * 8:(3 * i + 3) * 8, :]
                gs = 1.0 if i == 1 else 0.5
                wv = nw[:, l, i, :].unsqueeze(2).broadcast_to([128, 8, 2])
                P.add("dve", lambda e, sc=sc: e.tensor_scalar(tmpc[:], sc, 1.0, sqD, op0=ALU.add, op1=ALU.mult),
                      r=[("modT", l)], w=["tmpc"])
                P.add("dve", lambda e, l=l, i=i, wv=wv: e.tensor_tensor(out=coef[:, l, 3 * i, :, :], in0=tmpc[:], in1=wv,
                                                                       op=ALU.mult),
                      r=["tmpc", "nw"], w=[("coefa", l, i)])
                P.add("dve", lambda e, l=l, i=i, sh=sh: e.tensor_copy(coef[:, l, 3 * i + 1, :, :], sh),
                      r=[("modT", l)], w=[("coefb", l, i)])
                P.add("dve", lambda e, l=l, i=i, gt=gt, gs=gs: e.tensor_scalar_mul(coef[:, l, 3 * i + 2, :, :], gt, gs),
                      r=[("modT", l)], w=[("coefg", l, i)])
        P.add("dve", lambda e: e.memset(tmpc[:], 0.0),
              r=[("coefa", l, i) for l in range(L) for i in range(3)] + [("coefb", l, i) for l in range(L) for i in range(3)]
              + [("coefg", l, i) for l in range(L) for i in range(3)] + ["nfin", "convw", "gconst", "dnw", "tmpc"],
              w=["coef", "tmpc"])
        P.flush()
        es.close()

    def prologue_weights(self, inp, WS):
        nc, P, L = self.nc, self.P, self.depth
        es = ExitStack()
        NB = 3
        s32 = [self.sb(es, "s32_%d" % i, [128, 4096], F32) for i in range(NB)]
        s16 = [self.sb(es, "s16_%d" % i, [128, 4096], BF16) for i in range(NB)]
        s16p = [self.sb(es, "s16p_%d" % i, [128, 4096], BF16) for i in range(2)]
        state = dict(i=0, ip=0)
        cast_eng = ["dve", "act", "pool"]

        def piece(dkey, n, srcs, perm_key=None):
            dst = WS[dkey]
            perm_dst = WS[perm_key] if perm_key is not None else None
            i = state["i"]
            state["i"] += 1
            b = i % NB
            k32, k16 = ("s32", b), ("s16", b)

            def ld(e):
                return [e.dma_start(out=dv(s32[b]), in_=sv) for dv, sv in srcs]
            P.add("sp", ld, w=[k32], slot=k32, ndma=len(srcs))
            ce = cast_eng[i % 2]
            if ce == "act":
                P.add("act", lambda e: e.copy(out=s16[b][:, :n], in_=s32[b][:, :n]), r=[k32], w=[k16])
            else:
                P.add(ce, lambda e: e.tensor_copy(s16[b][:, :n], s32[b][:, :n]), r=[k32], w=[k16])
            P.add("pool", lambda e: [e.dma_start(out=dst[:, :n], in_=s16[b][:, :n])], r=[k16], w=[("W", dkey)], slot=k16)
            if perm_dst is not None:
                ip = state["ip"] % 2
                state["ip"] += 1
                kp = ("s16p", ip)
                src4 = s16[b][:].rearrange("p (j k e d) -> p j k e d", j=4, k=8, e=2)
                dst4 = s16p[ip][:].rearrange("p (j k e d) -> p j k e d", j=4, k=8, e=2)

                def pm(e):
                    for j in range(4):
                        e.tensor_copy(dst4[:, j, :, :, 16:64], src4[:, j, :, :, 16:64])
                        e.tensor_copy(dst4[:, j, :, :, 0:8], src4[:, j, :, :, 8:16])
                        ins = e.tensor_copy(dst4[:, j, :, :, 8:16], src4[:, j, :, :, 0:8])
                    return ins
                P.add("dve", pm, r=[k16], w=[kp])
                P.add("pool", lambda e: [e.dma_start(out=perm_dst[:, :], in_=s16p[ip][:])], r=[kp], w=[("W", perm_key)], slot=kp)

        def view(shape_str, lo, hi, **kw):
            return lambda t: t[:, lo:hi].rearrange(shape_str, **kw)

        for l in range(L):
            for f, (wg, wu, wd) in ((1, ("ffn1_w_gate", "ffn1_w_up", "ffn1_w_down")), (2, ("ffn2_w_gate", "ffn2_w_up", "ffn2_w_down"))):
                g3 = inp[wg][l].rearrange("(kc p) n -> p kc n", p=128)
                u3 = inp[wu][l].rearrange("(kc p) n -> p kc n", p=128)
                d3 = inp[wd][l].rearrange("(fc p) n -> p fc n", p=128)
                for m in range(NFC):
                    piece((l, "gu", f, m), 2048,
                          [(view("p (k c) -> p k c", 0, 1024, k=8), g3[:, :, m * 128:(m + 1) * 128]),
                           (view("p (k c) -> p k c", 1024, 2048, k=8), u3[:, :, m * 128:(m + 1) * 128])])
                for dc in range(8):
                    piece((l, "dn", f, dc), NFC * 128,
                          [(view("p (k c) -> p k c", f0 * 128, f1 * 128, k=f1 - f0), d3[:, f0:f1, dc * 128:(dc + 1) * 128])
                           for (f0, f1) in ((0, 8), (8, 16), (16, NFC))])
            i3 = inp["w_in"][l].rearrange("(kc p) n -> p kc n", p=128)
            fcols = [0, 512, 1536, 2048, 2560]
            for j in range(5):
                c0 = fcols[j]
                srcs = [(view("p (k c) -> p k c", jj * 1024, (jj + 1) * 1024, k=8), i3[:, :, c0 + jj * 128:c0 + (jj + 1) * 128])
                        for jj in range(4)]
                piece((l, "inF", j), 4096, srcs, perm_key=((l, "inP", j) if j < 2 else None))
            for j, c0 in enumerate((1024, 3072)):
                piece((l, "inT", j), 4096, [(view("p (k c) -> p k c", 0, 4096, k=8), i3[:, :, c0:c0 + 512])])
            piece((l, "inAB"), 128, [(view("p (k c) -> p k c", 0, 128, k=8), i3[:, :, 3584:3600])])
            o3 = inp["w_out"][l].rearrange("(kc p) n -> p kc n", p=128)
            for j in range(2):
                srcs = [(view("p (k c) -> p k c", jj * 1024, (jj + 1) * 1024, k=8),
                         o3[:, :, (j * 4 + jj) * 128:(j * 4 + jj + 1) * 128]) for jj in range(4)]
                piece((l, "out", j), 4096, srcs)
        P.flush()
        es.close()


    def dumps(self):
        st = self.st
        dbg = self.debug
        if dbg == "P":
            self.dump("coef", st["coef"][:].rearrange("p a b c d -> p (a b c d)"), [])
            self.dump("w_gu0", st["WS"][(0, "gu", 1, 0)], [])
            self.dump("w_inP0", st["WS"][(0, "inP", 0)], [])
            self.dump("w_dn3", st["WS"][(0, "dn", 2, 3)], [])
        if dbg == "A":
            for nm in ("X1T", "QT", "KT", "VP", "PDN", "ZS", "GB"):
                self.dump(nm, st[nm], [])
        if dbg == "attn":
            self.dump("MIXT", st["MIXT"][0:4], [])
        if dbg and dbg.startswith("dn"):
            names = {"dn1": ("DNQT", "DNKT", "DNK", "DNV"), "dn2": ("DNQT", "DNKT", "DNK", "DNV", "OFB")}.get(
                dbg, ("DNQT",) if dbg.startswith("dn2:") else ("DNQT", "DNKT", "DNK", "DNV", "OFB", "MIXT"))
            for nm in names:
                self.dump(nm, st[nm], [])

    def dump(self, name, src, rkeys):
        out = self.dram("dbg_" + name, list(src.shape), src.dtype, kind="ExternalOutput")
        self.dbg_out[name] = out
        self.P.add("sp", lambda e: [e.dma_start(out=out, in_=src)], r=list(rkeys) + ["Y"], slot="dbg")

    class WStream:
        def __init__(self, B, slots, pieces):
            self.B, self.slots, self.pieces = B, slots, pieces
            self.i = 0
            self.j = 0
            for _ in range(len(slots)):
                self._load()

        def _load(self):
            if self.j >= len(self.pieces):
                return
            key, n = self.pieces[self.j]
            s = self.j % len(self.slots)
            src = self.B.st["WS"][key]
            tile = self.slots[s]
            self.B.P.add("sp", lambda e: [e.dma_start(out=tile[:, :n], in_=src[:, :n])],
                         r=[("W", key)], w=[("wsl", s)], slot=("wsl", s))
            self.j += 1

        def get(self, key):
            k, n = self.pieces[self.i]
            assert k == key, (k, key)
            s = self.i % len(self.slots)
            self.i += 1
            return self.slots[s], ("wsl", s)

        def done(self):
            self._load()

    def norm(self, xt_t, kx, hout, kh, a_fn, b_fn, bf):
        P = self.P
        sq, pss, kpss, rstd, tmpn, ones_b = bf["sq"], bf["pss"], bf["kpss"], bf["rstd"], bf["tmpn"], self.st["ones_b"]
        P.add("act", lambda e: e.activation(out=sq[:, 0:4, :], in_=xt_t[:, 0:4, :], func=AF.Square), r=[kx], w=[("sq", 0)])
        P.add("dve", lambda e: e.tensor_tensor(out=sq[:, 4:8, :], in0=xt_t[:, 4:8, :], in1=xt_t[:, 4:8, :], op=ALU.mult), r=[kx], w=[("sq", 1)])
        for half in range(2):
            def mm(e, half=half):
                for c in range(half * 4, half * 4 + 4):
                    ins = e.matmul(pss[:], lhsT=ones_b[:], rhs=sq[:, c, :], start=(c == 0), stop=(c == 7))
                return ins
            P.add("pe", mm, r=[("sq", half), "ones_b"], w=[kpss])
        P.add("act", lambda e: e.activation(out=bf["tmpn"][0][:], in_=pss[:], func=AF.Sqrt, bias=self.epsc[:, 0:1], scale=1.0),
              r=[kpss, ("tmpn", 0)], w=[("tmpn", 0)])
        P.add("dve", lambda e: e.reciprocal(rstd[:], bf["tmpn"][0][:]), r=[("tmpn", 0)], w=["rstd"])
        for c in range(8):
            tb = tmpn[c % 2]
            P.add("dve", lambda e, c=c, tb=tb: e.tensor_tensor(out=tb[:], in0=xt_t[:, c, :], in1=rstd[:], op=ALU.mult),
                  r=[kx, "rstd"], w=[("tmpn", c % 2)])
            bias = b_fn(c) if b_fn is not None else 0.0
            P.add("act", lambda e, c=c, tb=tb, bias=bias: e.activation(out=hout[:, c, :], in_=tb[:], func=AF.Identity,
                                                                      bias=bias, scale=a_fn(c)),
                  r=[("tmpn", c % 2), "coef"], w=[(kh, c)])

    def ffn(self, l, f, s, xt_t, kx, bf, ws):
        P, coef = self.P, self.st["coef"]
        i = 0 if f == 1 else 2
        h, act, sg, banks = bf["h"], bf["act"], bf["sg"], bf["banks"]
        self.norm(xt_t, kx, h, "h", lambda c: coef[:, l, 3 * i, c, s:s + 1], lambda c: coef[:, l, 3 * i + 1, c, s:s + 1], bf)
        hk = [("h", c) for c in range(8)]
        for m in range(NFC):
            wt, wk = ws.get((l, "gu", f, m))
            w4 = wt[:, :2048].rearrange("p (g k c) -> p g k c", g=2, k=8)
            pg, kpg = banks[(m % 2) * 2], ("bank", (m % 2) * 2)
            pu, kpu = banks[(m % 2) * 2 + 1], ("bank", (m % 2) * 2 + 1)

            def mm(e, w4=w4, pg=pg, pu=pu):
                for g, pp in ((0, pg), (1, pu)):
                    for kc in range(8):
                        ins = e.matmul(pp[:], lhsT=w4[:, g, kc, :], rhs=h[:, kc, :], start=(kc == 0), stop=(kc == 7))
                return ins
            if m == 0:
                for kc in range(8):
                    P.add("pe", lambda e, w4=w4, pg=pg, kc=kc: e.matmul(pg[:], lhsT=w4[:, 0, kc, :], rhs=h[:, kc, :], start=(kc == 0), stop=(kc == 7)),
                          r=[wk, ("h", kc)], w=[kpg])

                def mmu(e, w4=w4, pu=pu):
                    for kc in range(8):
                        ins = e.matmul(pu[:], lhsT=w4[:, 1, kc, :], rhs=h[:, kc, :], start=(kc == 0), stop=(kc == 7))
                    return ins
                P.add("pe", mmu, r=[wk] + hk, w=[kpu])
            else:
                P.add("pe", mm, r=[wk] + hk, w=[kpg, kpu])
            ws.done()
            sgb = sg[m % 2]
            P.add("act", lambda e, pg=pg, sgb=sgb: e.activation(out=sgb[:], in_=pg[:], func=AF.Silu),
                  r=[kpg], w=[("sg", m % 2)])
            P.add("dve", lambda e, pu=pu, sgb=sgb, m=m: e.tensor_tensor(out=act[:, m, :], in0=pu[:], in1=sgb[:], op=ALU.mult),
                  r=[kpu, ("sg", m % 2)], w=[("act", m)])
        ak = [("act", m) for m in range(NFC)]
        for dc in range(8):
            wt, wk = ws.get((l, "dn", f, dc))
            w3 = wt[:, :NFC * 128].rearrange("p (k c) -> p k c", k=NFC)
            py, kpy = banks[4 + dc % 2], ("bank", 4 + dc % 2)

            def mm2(e, w3=w3, py=py):
                for fc in range(NFC):
                    ins = e.matmul(py[:], lhsT=w3[:, fc, :], rhs=act[:, fc, :], start=(fc == 0), stop=(fc == NFC - 1))
                return ins
            P.add("pe", mm2, r=[wk] + ak, w=[kpy])
            ws.done()
            P.add("dve", lambda e, dc=dc, py=py: e.scalar_tensor_tensor(out=xt_t[:, dc, :], in0=py[:],
                                                                       scalar=coef[:, l, 3 * i + 2, dc, s:s + 1],
                                                                       in1=xt_t[:, dc, :], op0=ALU.mult, op1=ALU.add),
                  r=[kpy, kx, "coef"], w=[kx])

    def gemm_bufs(self, es):
        bf = {}
        bf["sq"] = self.sb(es, "sq", [128, 8, 512], BF16)
        bf["rstd"] = self.sb(es, "rstd", [128, 512], F32)
        bf["tmpn"] = [self.sb(es, "tmpn", [128, 512], F32) for _ in range(2)]
        bf["h"] = self.sb(es, "h", [128, 8, 512], BF16)
        bf["act"] = self.sb(es, "act", [128, NFC, 512], BF16)
        bf["sg"] = [self.sb(es, "sg", [128, 512], F32) for _ in range(2)]
        bf["wsl"] = [self.sb(es, "wsl", [128, 4096], BF16) for _ in range(4)]
        bf["banks"] = [self.ps(es, "bank", [128, 512]) for _ in range(8)]
        bf["pss"], bf["kpss"] = bf["banks"][6], ("bank", 6)
        return bf

    def phase_A(self, l):
        nc, P, st = self.nc, self.P, self.st
        NT, SEG = self.NT, self.SEG
        C, coef = self.C, st["coef"]
        es = ExitStack()
        bf = self.gemm_bufs(es)
        banks = bf["banks"]
        xt = [self.sb(es, "xt", [128, 8, 512], F32) for _ in range(2)]
        xtok = [self.sb(es, "xtok", [128, 1024], F32) for _ in range(2)]
        cosb = self.sb(es, "cosb", [128, 512], F32)
        sinb = self.sb(es, "sinb", [128, 512], F32)
        t1 = self.sb(es, "t1", [128, 512], F32)
        t2 = self.sb(es, "t2", [128, 512], F32)
        stq = [self.sb(es, "stq", [128, 512], BF16) for _ in range(2)]
        stg = [self.sb(es, "stg", [128, 512], F32) for _ in range(3)]
        vst = [self.sb(es, "vst", [128, 1024], BF16) for _ in range(2)]
        zst = [self.sb(es, "zst", [128, 512], F32) for _ in range(2)]
        gab = [self.sb(es, "gab", [128, 16], F32) for _ in range(2)]
        gt = [self.sb(es, "gt", [128, 8], F32) for _ in range(4)]
        gout = [self.sb(es, "gout", [128, 16], F32) for _ in range(2)]
        zpad = self.sb(es, "zpad", [128, 2], F32)
        h = bf["h"]
        NTILE = NT // TT
        PADW = st["PADW"]
        for i in range(2):
            P.add("dve", lambda e, i=i: e.memset(vst[i][:], 0.0), w=[("vst", i)])
        P.add("dve", lambda e: e.memset(zpad[:], 0.0), w=["zpad"])
        P.add("pool", lambda e: [e.dma_start(out=st["PDN"][c, :, 0:2], in_=zpad[:]) for c in range(12)]
              + [e.dma_start(out=st["PDN"][c, :, 2 * PADW - 2:2 * PADW], in_=zpad[:]) for c in range(12)],
              r=["zpad"], w=[("PDNpad", l)], slot="zpad", ndma=24)
        pieces = []
        for t in range(NTILE):
            pieces += [((l, "gu", 1, m), 2048) for m in range(NFC)] + [((l, "dn", 1, dc), NFC * 128) for dc in range(8)]
            pieces += [((l, "inF", 0), 4096), ((l, "inP", 0), 4096), ((l, "inF", 1), 4096), ((l, "inP", 1), 4096),
                       ((l, "inF", 2), 4096), ((l, "inF", 3), 4096), ((l, "inF", 4), 4096),
                       ((l, "inT", 0), 4096), ((l, "inT", 1), 4096), ((l, "inAB"), 128)]
        ws = Builder.WStream(self, bf["wsl"], pieces)
        ident = C(0)
        def _tile(t):
            s = (t * TT) // SEG
            tok0 = t * TT
            xt_t, kx = xt[t % 2], ("xt", t % 2)
            if l == 0:
                for sb_ in range(4):
                    xk = xtok[sb_ % 2]
                    kxk = ("xtok", sb_ % 2)
                    P.add("sp", lambda e, xk=xk, sb_=sb_: [e.dma_start(out=xk[:], in_=st["x_in"][tok0 + sb_ * 128:tok0 + (sb_ + 1) * 128, :])],
                          w=[kxk], slot=kxk)
                    for hf in range(2):
                        bk, kbk = banks[hf], ("bank", hf)

                        def tr(e, xk=xk, hf=hf, bk=bk):
                            for cc in range(4):
                                c = hf * 4 + cc
                                ins = e.transpose(bk[:, cc * 128:(cc + 1) * 128], xk[:, c * 128:(c + 1) * 128], ident)
                            return ins
                        P.add("pe", tr, r=[kxk, "cst"], w=[kbk])
                        P.add("act", lambda e, hf=hf, bk=bk, sb_=sb_: e.copy(out=xt_t[:, hf * 4:(hf + 1) * 4, sb_ * 128:(sb_ + 1) * 128],
                                                                            in_=bk[:].rearrange("p (c t) -> p c t", c=4)),
                              r=[kbk], w=[kx])
            else:
                P.add("pool", lambda e: [e.dma_start(out=xt_t[:], in_=st["XT"][:, :, tok0:tok0 + TT].rearrange("c p t -> p c t"))],
                      r=[("XT", t)], w=[kx], slot=kx)
            self.ffn(l, 1, s, xt_t, kx, bf, ws)
            P.add("pool", lambda e: [e.dma_start(out=st["X1T"][:, :, tok0:tok0 + TT].rearrange("c p t -> p c t"), in_=xt_t[:])],
                  r=[kx], w=[("X1T", t)], slot=("x1st", t % 2))
            self.norm(xt_t, kx, h, "h", lambda c: coef[:, l, 3, c, s:s + 1], lambda c: coef[:, l, 4, c, s:s + 1], bf)
            hk = [("h", c) for c in range(8)]
            P.add("sp", lambda e: [e.dma_start(out=cosb[:], in_=st["cos_in"][:, tok0:tok0 + TT]),
                                   e.dma_start(out=sinb[:], in_=st["sin_in"][:, tok0:tok0 + TT])],
                  w=["cs"], slot="cs", ndma=2)
            for qk in range(2):
                wt, wk = ws.get((l, "inF", qk))
                wp, wpk = ws.get((l, "inP", qk))
                w4 = wt[:].rearrange("p (j k c) -> p j k c", j=4, k=8)
                p4 = wp[:].rearrange("p (j k c) -> p j k c", j=4, k=8)
                dst = st["QT"] if qk == 0 else st["KT"]
                for j in range(4):
                    pa, kpa = banks[0 + (j % 2) * 2], ("bank", (j % 2) * 2)
                    pb, kpb = banks[1 + (j % 2) * 2], ("bank", 1 + (j % 2) * 2)

                    def mm(e, w4=w4, p4=p4, j=j, pa=pa, pb=pb):
                        for ww, pp in ((w4, pa), (p4, pb)):
                            for kc in range(8):
                                ins = e.matmul(pp[:], lhsT=ww[:, j, kc, :], rhs=h[:, kc, :], start=(kc == 0), stop=(kc == 7))
                        return ins
                    P.add("pe", mm, r=[wk, wpk] + hk, w=[kpa, kpb])
                    P.add("dve", lambda e, pa=pa: e.tensor_tensor(out=t1[:], in0=pa[:], in1=cosb[:], op=ALU.mult),
                          r=[kpa, "cs"], w=["t1"])
                    P.add("dve", lambda e, pb=pb: e.tensor_tensor(out=t2[:], in0=pb[:], in1=sinb[:], op=ALU.mult),
                          r=[kpb, "cs"], w=["t2"])
                    sq_, ksq = stq[j % 2], ("stq", j % 2)
                    P.add("dve", lambda e, sq_=sq_: e.tensor_tensor(out=sq_[:], in0=t1[:], in1=t2[:], op=ALU.add),
                          r=["t1", "t2"], w=[ksq])
                    P.add("pool", lambda e, sq_=sq_, j=j, dst=dst: [e.dma_start(out=dst[j, :, tok0:tok0 + TT], in_=sq_[:])],
                          r=[ksq], w=[("QK", qk, j, t)], slot=ksq)
                ws.done()
                ws.done()
            for g in range(3):
                wt, wk = ws.get((l, "inF", 2 + g))
                w4 = wt[:].rearrange("p (j k c) -> p j k c", j=4, k=8)
                for j in range(4):
                    cidx = g * 4 + j
                    pa, kpa = banks[cidx % 4], ("bank", cidx % 4)

                    def mm(e, w4=w4, j=j, pa=pa):
                        for kc in range(8):
                            ins = e.matmul(pa[:], lhsT=w4[:, j, kc, :], rhs=h[:, kc, :], start=(kc == 0), stop=(kc == 7))
                        return ins
                    P.add("pe", mm, r=[wk] + hk, w=[kpa])
                    sg_, ksg = stg[cidx % 3], ("stg", cidx % 3)
                    P.add("act", lambda e, pa=pa, sg_=sg_: e.copy(out=sg_[:], in_=pa[:]), r=[kpa], w=[ksg])
                    col0 = s * PADW + 2 + (tok0 - s * SEG)
                    P.add("pool", lambda e, sg_=sg_, cidx=cidx, col0=col0: [e.dma_start(out=st["PDN"][cidx, :, col0:col0 + TT], in_=sg_[:])],
                          r=[ksg], w=[("PDN", cidx, t)], slot=ksg)
                    if tok0 + TT == SEG or tok0 == SEG:
                        left = (tok0 + TT == SEG)
                        srcv = sg_[:, TT - 2:TT] if left else sg_[:, 0:2]
                        dcol = (PADW + 0) if left else (PADW - 2)
                        P.add("dve", lambda e, srcv=srcv: e.tensor_scalar(zpad[:], srcv, st["flag"][:, 0:1], None, op0=ALU.mult),
                              r=[ksg, "flag", "zpad"], w=["zpad"])
                        P.add("pool", lambda e, cidx=cidx, dcol=dcol: [e.dma_start(out=st["PDN"][cidx, :, dcol:dcol + 2], in_=zpad[:])],
                              r=["zpad"], w=[("PDNh", cidx, left)], slot="zpad")
                ws.done()
            wv, wvk = ws.get((l, "inT", 0))
            wz, wzk = ws.get((l, "inT", 1))
            wab, wabk = ws.get((l, "inAB"))
            wv3 = wv[:].rearrange("p (k c) -> p k c", k=8)
            wz3 = wz[:].rearrange("p (k c) -> p k c", k=8)
            wab3 = wab[:, :128].rearrange("p (k c) -> p k c", k=8)
            for sb_ in range(4):
                tk0 = tok0 + sb_ * 128
                pv, kpv = banks[0 + (sb_ % 2) * 3], ("bank", (sb_ % 2) * 3)
                pz, kpz = banks[1 + (sb_ % 2) * 3], ("bank", 1 + (sb_ % 2) * 3)
                pab, kpab = banks[2 + (sb_ % 2) * 3], ("bank", 2 + (sb_ % 2) * 3)

                def mm(e, sb_=sb_, pv=pv, pz=pz, pab=pab):
                    for kc in range(8):
                        e.matmul(pv[:], lhsT=h[:, kc, sb_ * 128:(sb_ + 1) * 128], rhs=wv3[:, kc, :], start=(kc == 0), stop=(kc == 7))
                    for kc in range(8):
                        e.matmul(pz[:], lhsT=h[:, kc, sb_ * 128:(sb_ + 1) * 128], rhs=wz3[:, kc, :], start=(kc == 0), stop=(kc == 7))
                    for kc in range(8):
                        ins = e.matmul(pab[:, 0:16], lhsT=h[:, kc, sb_ * 128:(sb_ + 1) * 128], rhs=wab3[:, kc, :], start=(kc == 0), stop=(kc == 7))
                    return ins
                P.add("pe", mm, r=[wvk, wzk, wabk] + hk, w=[kpv, kpz, kpab])
                vs, kvs = vst[sb_ % 2], ("vst", sb_ % 2)

                def cpv(e, vs=vs, pv=pv):
                    v4 = vs[:].rearrange("p (j e d) -> p j e d", j=4, e=2)
                    p4 = pv[:].rearrange("p (j e d) -> p j e d", j=4, e=2)
                    e.tensor_copy(v4[:, :, 0, 0:64], p4[:, :, 0, :])
                    return e.tensor_copy(v4[:, :, 1, 64:128], p4[:, :, 1, :])
                P.add("dve", cpv, r=[kpv], w=[kvs])
                P.add("pool", lambda e, vs=vs, tk0=tk0: [e.dma_start(out=st["VP"][tk0:tk0 + 128, :], in_=vs[:])],
                      r=[kvs], w=[("VP", tk0)], slot=kvs)
                zs_, kzs = zst[sb_ % 2], ("zst", sb_ % 2)
                P.add("act", lambda e, zs_=zs_, pz=pz: e.activation(out=zs_[:], in_=pz[:], func=AF.Silu), r=[kpz], w=[kzs])
                P.add("pool", lambda e, zs_=zs_, tk0=tk0: [e.dma_start(out=st["ZS"][tk0:tk0 + 128, :], in_=zs_[:])],
                      r=[kzs], w=[("ZS", tk0)], slot=kzs)
                ga, kga = gab[sb_ % 2], ("gab", sb_ % 2)
                go, kgo = gout[sb_ % 2], ("gout", sb_ % 2)
                gc_ = st["gconst"]
                P.add("dve", lambda e, ga=ga, pab=pab: e.tensor_copy(ga[:], pab[:, 0:16]), r=[kpab], w=[kga])
                P.add("dve", lambda e, ga=ga: e.tensor_tensor(out=gt[0][:], in0=ga[:, 0:8], in1=gc_[:, l, 1, :], op=ALU.add),
                      r=[kga, "coef"], w=["gt0"])
                P.add("dve", lambda e: e.scalar_tensor_tensor(out=gt[1][:], in0=gt[0][:], scalar=-1.0, in1=gt[0][:],
                                                             op0=ALU.mult, op1=ALU.max), r=["gt0"], w=["gt1"])
                P.add("act", lambda e: e.activation(out=gt[2][:], in_=gt[1][:], func=AF.Exp, scale=-1.0), r=["gt1"], w=["gt2"])
                P.add("act", lambda e: e.activation(out=gt[3][:], in_=gt[2][:], func=AF.Ln, bias=self.epsc[:, 2:3], scale=1.0), r=["gt2"], w=["gt3"])
                P.add("dve", lambda e: e.scalar_tensor_tensor(out=gt[1][:], in0=gt[0][:], scalar=0.0, in1=gt[3][:],
                                                             op0=ALU.max, op1=ALU.add), r=["gt0", "gt3", "gt1"], w=["gt1"])
                P.add("dve", lambda e, go=go: e.tensor_tensor(out=go[:, 0:8], in0=gt[1][:], in1=gc_[:, l, 0, :], op=ALU.mult),
                      r=["gt1", "coef"], w=[(kgo, 0)])
                P.add("act", lambda e, go=go, ga=ga: e.activation(out=go[:, 8:16], in_=ga[:, 8:16], func=AF.Sigmoid),
                      r=[kga], w=[(kgo, 1)])
                P.add("pool", lambda e, go=go, tk0=tk0: [e.dma_start(out=st["GB"][tk0:tk0 + 128, :], in_=go[:])],
                      r=[(kgo, 0), (kgo, 1)], w=[("GB", tk0), (kgo, 0), (kgo, 1)], slot=kgo)
            ws.done()
            ws.done()
            ws.done()
        for t in range(NTILE):
            _tile(t)
        P.flush()
        es.close()

    def phase_C(self, l):
        nc, P, st = self.nc, self.P, self.st
        NT, SEG = self.NT, self.SEG
        C, coef = self.C, st["coef"]
        last = (l == self.depth - 1)
        es = ExitStack()
        bf = self.gemm_bufs(es)
        banks = bf["banks"]
        xt = [self.sb(es, "xt", [128, 8, 512], F32) for _ in range(2)]
        mx = [self.sb(es, "mx", [128, 8, 512], BF16) for _ in range(2)]
        yt = self.sb(es, "yt", [128, 8, 512], F32)
        ytok = [self.sb(es, "ytok", [128, 1024], F32) for _ in range(2)]
        NTILE = NT // TT
        pieces = []
        for t in range(NTILE):
            pieces += [((l, "out", 0), 4096), ((l, "out", 1), 4096)]
            pieces += [((l, "gu", 2, m), 2048) for m in range(NFC)] + [((l, "dn", 2, dc), NFC * 128) for dc in range(8)]
        ws = Builder.WStream(self, bf["wsl"], pieces)
        ident = C(0)
        def _tile(t):
            s = (t * TT) // SEG
            tok0 = t * TT
            xt_t, kx = xt[t % 2], ("xt", t % 2)
            mx_t, kmx = mx[t % 2], ("mx", t % 2)
            P.add("pool", lambda e: [e.dma_start(out=xt_t[:], in_=st["X1T"][:, :, tok0:tok0 + TT].rearrange("c p t -> p c t"))],
                  r=[("X1T", t)], w=[kx], slot=kx)
            P.add("pool", lambda e: [e.dma_start(out=mx_t[:], in_=st["MIXT"][:, :, tok0:tok0 + TT].rearrange("c p t -> p c t"))],
                  r=["MIXT"], w=[kmx], slot=kmx)
            for j in range(2):
                wt, wk = ws.get((l, "out", j))
                w4 = wt[:].rearrange("p (j k c) -> p j k c", j=4, k=8)
                for jj in range(4):
                    dc = j * 4 + jj
                    py, kpy = banks[dc % 2], ("bank", dc % 2)

                    def mm(e, w4=w4, jj=jj, py=py):
                        for kc in range(8):
                            ins = e.matmul(py[:], lhsT=w4[:, jj, kc, :], rhs=mx_t[:, kc, :], start=(kc == 0), stop=(kc == 7))
                        return ins
                    P.add("pe", mm, r=[wk, kmx], w=[kpy])
                    P.add("dve", lambda e, dc=dc, py=py: e.scalar_tensor_tensor(out=xt_t[:, dc, :], in0=py[:],
                                                                               scalar=coef[:, l, 5, dc, s:s + 1],
                                                                               in1=xt_t[:, dc, :], op0=ALU.mult, op1=ALU.add),
                          r=[kpy, kx, "coef"], w=[kx])
                ws.done()
            self.ffn(l, 2, s, xt_t, kx, bf, ws)
            if not last:
                P.add("pool", lambda e: [e.dma_start(out=st["XT"][:, :, tok0:tok0 + TT].rearrange("c p t -> p c t"), in_=xt_t[:])],
                      r=[kx], w=[("XT", t)], slot=("x1st", t % 2))
            else:
                nf = st["nfin"]
                self.norm(xt_t, kx, yt, "yt", lambda c: nf[:, c:c + 1], None, bf)
                yk = [("yt", c) for c in range(8)]
                for sb_ in range(4):
                    yk_, kyk = ytok[sb_ % 2], ("ytok", sb_ % 2)
                    for hf in range(2):
                        bk, kbk = banks[hf], ("bank", hf)

                        def tr(e, hf=hf, bk=bk, sb_=sb_):
                            for cc in range(4):
                                c = hf * 4 + cc
                                ins = e.transpose(bk[:, cc * 128:(cc + 1) * 128], yt[:, c, sb_ * 128:(sb_ + 1) * 128], ident)
                            return ins
                        P.add("pe", tr, r=yk + ["cst"], w=[kbk])
                        P.add("act", lambda e, hf=hf, bk=bk, yk_=yk_: e.copy(out=yk_[:, hf * 512:(hf + 1) * 512], in_=bk[:]),
                              r=[kbk], w=[(kyk, hf)])
                    P.add("sp", lambda e, yk_=yk_, sb_=sb_: [e.dma_start(out=st["y_out"][tok0 + sb_ * 128:tok0 + (sb_ + 1) * 128, :], in_=yk_[:])],
                          r=[(kyk, 0), (kyk, 1), "Y"], w=[(kyk, 0), (kyk, 1)], slot=kyk)
        for t in range(NTILE):
            _tile(t)
        P.flush()
        es.close()

    def phase_attn(self, l):
        nc, P, st = self.nc, self.P, self.st
        NT, SEG = self.NT, self.SEG
        CB = self.CB
        es = ExitStack()
        NBLK = NT // 128
        qt = self.sb(es, "qt", [128, NT], BF16)
        kt = self.sb(es, "kt", [128, NT], BF16)
        vpad = self.sb(es, "vpad", [128, NBLK, 256], BF16)
        acc = self.sb(es, "acc", [128, 2, NT], F32)
        pe_ = [self.sb(es, "pe", [128, 2, 384], BF16) for _ in range(2)]
        pm_ = [self.sb(es, "pm", [128, 2, 384], BF16) for _ in range(2)]
        mkv = {v: self.sb(es, "mk" + v, [128, 2, 384], BF16) for v in ("n", "pf", "nf")}
        rden = self.sb(es, "rden", [128, 512], F32)
        ost = [self.sb(es, "ost", [128, 512], BF16) for _ in range(2)]
        sps = [self.ps(es, "sps", [128, 2, 512]) for _ in range(2)]
        nd = [self.ps(es, "nd", [128, 512]) for _ in range(2)]
        flag = st["flag"]

        def mkbuild(e):
            for v in ("n", "pf", "nf"):
                for hh in range(2):
                    for j in range(3):
                        dst = mkv[v][:, hh, j * 128:(j + 1) * 128]
                        if (v == "pf" and j == 0) or (v == "nf" and j == 2):
                            ins = e.tensor_scalar(dst, CB(7 + [1, 0, 2][j]), flag[:, 0:1], None, op0=ALU.mult)
                        else:
                            ins = e.tensor_copy(dst, CB(7 + [1, 0, 2][j]))
            return ins
        P.add("dve", mkbuild, r=["cstb", "flag"], w=["mkv"])
        it = 0
        for hp in range(4):
            P.add("sp", lambda e, hp=hp: [e.dma_start(out=qt[:], in_=st["QT"][hp]), e.dma_start(out=kt[:], in_=st["KT"][hp])],
                  w=["qk"], slot="qk", ndma=2)
            for pi, dil in enumerate((1, 4, 16)):
                nb = NT // dil // 128
                vsrc = st["VP"][:, hp * 256:(hp + 1) * 256].rearrange("(b p r) c -> r p b c", p=128, r=dil)
                vchunks = [(r, b0, min(b0 + 8, nb)) for r in range(dil) for b0 in range(0, nb, 8)]
                P.add("sp", lambda e, vsrc=vsrc, nb=nb, vchunks=vchunks: [e.dma_start(out=vpad[:, r * nb + b0:r * nb + b1, :], in_=vsrc[r][:, b0:b1, :])
                                                                         for (r, b0, b1) in vchunks],
                      w=["vpad"], slot="vpad", ndma=len(vchunks))
                def block_iter(r, b, par, pi=pi, dil=dil, nb=nb):

                        def tsl(blk, r=r, dil=dil):
                            s0 = r + dil * 128 * blk
                            return slice(s0, s0 + 127 * dil + 1, dil) if dil > 1 else slice(s0, s0 + 128)
                        kbs = [(j, b + j - 1) for j in range(3) if 0 <= b + j - 1 < nb]
                        j0, j1 = kbs[0][0], kbs[-1][0] + 1
                        v = "pf" if b == nb // 2 else ("nf" if b == nb // 2 - 1 else "n")
                        sp_, ksp = sps[par], ("sps", par)
                        nd_, knd = nd[par], ("nd", par)
                        pe__, kpe = pe_[par], ("pe", par)
                        pm__, kpm = pm_[par], ("pm", par)
                        qs = tsl(b)

                        def mm(e, kbs=kbs, sp_=sp_, qs=qs, tsl=tsl):
                            for hh in range(2):
                                for j, kb in kbs:
                                    ins = e.matmul(sp_[:, hh, j * 128:(j + 1) * 128], lhsT=kt[hh * 64:(hh + 1) * 64, tsl(kb)],
                                                   rhs=qt[hh * 64:(hh + 1) * 64, qs], start=True, stop=True)
                            return ins
                        P.add("pe", mm, r=["qk"], w=[ksp])
                        P.add("act", lambda e, sp_=sp_, pe__=pe__, j0=j0, j1=j1: e.activation(
                            out=pe__[:, :, j0 * 128:j1 * 128], in_=sp_[:, :, j0 * 128:j1 * 128], func=AF.Exp, scale=0.125),
                            r=[ksp], w=[kpe])
                        P.add("dve", lambda e, pe__=pe__, pm__=pm__, j0=j0, j1=j1, v=v: e.tensor_tensor(
                            out=pm__[:, :, j0 * 128:j1 * 128], in0=pe__[:, :, j0 * 128:j1 * 128],
                            in1=mkv[v][:, :, j0 * 128:j1 * 128], op=ALU.mult), r=[kpe, "mkv"], w=[kpm])

                        def mm2(e, kbs=kbs, nd_=nd_, pm__=pm__, r=r, nb=nb):
                            n = 2 * len(kbs)
                            for which in range(2):
                                i = 0
                                for hh in range(2):
                                    for j, kb in kbs:
                                        lhsT = vpad[:, r * nb + kb, hh * 128:(hh + 1) * 128] if which == 0 else CB(10 + hh)
                                        ins = e.matmul(nd_[:, which * 128:(which + 1) * 128], lhsT=lhsT,
                                                       rhs=pm__[:, hh, j * 128:(j + 1) * 128], start=(i == 0), stop=(i == n - 1))
                                        i += 1
                            return ins
                        yield
                        P.add("pe", mm2, r=[kpm, "vpad", "cstb"], w=[knd])
                        ndv = nd_[:, 0:256].rearrange("p (a q) -> p a q", a=2)
                        if pi == 0:
                            P.add("act", lambda e, ndv=ndv, qs=qs: e.copy(out=acc[:, :, qs], in_=ndv), r=[knd], w=["acc"])
                        else:
                            P.add("dve", lambda e, ndv=ndv, qs=qs: e.tensor_tensor(out=acc[:, :, qs], in0=acc[:, :, qs], in1=ndv,
                                                                                  op=ALU.add), r=[knd, "acc"], w=["acc"])
                prev = None
                for r in range(dil):
                    for b in range(nb):
                        it += 1
                        g = block_iter(r, b, it % 2)
                        next(g)
                        if prev is not None:
                            for _ in prev:
                                pass
                        prev = g
                for _ in prev:
                    pass
            for tq in range(NT // 512):
                sl = slice(tq * 512, (tq + 1) * 512)
                o_, ko = ost[tq % 2], ("ost", tq % 2)
                P.add("dve", lambda e, sl=sl: e.reciprocal(rden[:], acc[:, 1, sl]), r=["acc"], w=["rden"])
                P.add("dve", lambda e, sl=sl, o_=o_: e.tensor_tensor(out=o_[:], in0=acc[:, 0, sl], in1=rden[:], op=ALU.mult),
                      r=["acc", "rden"], w=[ko])
                P.add("pool", lambda e, sl=sl, o_=o_, hp=hp: [e.dma_start(out=st["MIXT"][hp, :, sl], in_=o_[:])],
                      r=[ko], w=[ko + ("d",)], slot=ko)
        P.flush()
        es.close()

    def dn_stage1(self, l):
        nc, P, st = self.nc, self.P, self.st
        NT, SEG, PADW = self.NT, self.SEG, self.st["PADW"]
        C = self.C
        ones_b = st["ones_b"]
        es = ExitStack()
        NG = 4
        xin = [self.sb(es, "xin", [128, 516], F32) for _ in range(2 * NG)]
        ca = [self.sb(es, "ca", [128, 512], F32) for _ in range(NG)]
        yv = [self.sb(es, "yv", [128, 512], F32) for _ in range(NG)]
        sqb = [self.sb(es, "sqb", [128, 512], BF16) for _ in range(NG)]
        rn0 = [self.sb(es, "rn0", [128, 512], F32) for _ in range(NG)]
        rn = [self.sb(es, "rn", [128, 512], F32) for _ in range(NG)]
        yn = [self.sb(es, "yn", [128, 512], F32) for _ in range(NG)]
        tst = [self.sb(es, "tst", [128, 4, 128], F32) for _ in range(NG)]
        bss = [self.ps(es, "bss", [128, 512]) for _ in range(NG)]
        btr = [self.ps(es, "btr", [128, 512]) for _ in range(NG)]
        cw = st["convw"]
        groups = [(t, kind) for t in range(NT // TT) for kind in range(3)]

        def loads(gi):
            t, kind = groups[gi]
            tok0 = t * TT
            s_ = tok0 // SEG
            col0 = s_ * PADW + (tok0 - s_ * SEG)
            for i in range(NG):
                cidx = kind * 4 + i
                xi, kxi = xin[(gi % 2) * NG + i], ("xin", (gi % 2) * NG + i)
                P.add("sp", lambda e, xi=xi, cidx=cidx, col0=col0: [e.dma_start(out=xi[:], in_=st["PDN"][cidx, :, col0:col0 + 516])],
                      w=[kxi], slot=kxi)
        loads(0)
        for gi, (t, kind) in enumerate(groups):
            tok0 = t * TT
            if gi + 1 < len(groups):
                loads(gi + 1)
            X = [(xin[(gi % 2) * NG + i], ("xin", (gi % 2) * NG + i)) for i in range(NG)]
            cids = [kind * 4 + i for i in range(NG)]
            for j in range(5):
                for i in range(NG):
                    xi, kxi = X[i]
                    a_, ka, cidx = ca[i], ("ca", i), cids[i]
                    if j == 0:
                        P.add("dve", lambda e, a_=a_, xi=xi, cidx=cidx: e.tensor_scalar(a_[:], xi[:, 0:512], cw[:, l, 0, cidx:cidx + 1], None, op0=ALU.mult),
                              r=[kxi, "coef"], w=[ka])
                    else:
                        P.add("dve", lambda e, a_=a_, xi=xi, cidx=cidx, j=j: e.scalar_tensor_tensor(
                            out=a_[:], in0=xi[:, j:j + 512], scalar=cw[:, l, j, cidx:cidx + 1], in1=a_[:], op0=ALU.mult, op1=ALU.add),
                            r=[kxi, ka, "coef"], w=[ka])
            for i in range(NG):
                P.add("act", lambda e, i=i: e.activation(out=yv[i][:], in_=ca[i][:], func=AF.Silu), r=[("ca", i)], w=[("yv", i)])
            src = [(yv[i], ("yv", i)) for i in range(NG)]
            if kind < 2:
                for i in range(NG):
                    P.add("act", lambda e, i=i: e.activation(out=sqb[i][:], in_=yv[i][:], func=AF.Square), r=[("yv", i)], w=[("sqb", i)])
                for i in range(NG):
                    P.add("pe", lambda e, i=i: e.matmul(bss[i][:], lhsT=ones_b[:], rhs=sqb[i][:], start=True, stop=True),
                          r=[("sqb", i), "ones_b"], w=[("bss", i)])
                for i in range(NG):
                    P.add("act", lambda e, i=i: e.activation(out=rn0[i][:], in_=bss[i][:], func=AF.Sqrt, bias=self.epsc[:, 1:2], scale=1.0),
                          r=[("bss", i)], w=[("rn0", i)])
                for i in range(NG):
                    P.add("dve", lambda e, i=i: e.reciprocal(rn[i][:], rn0[i][:]), r=[("rn0", i)], w=[("rn", i)])
                scl = float(DK ** -0.5) if kind == 0 else 1.0
                for i in range(NG):
                    P.add("dve", lambda e, i=i, scl=scl: e.scalar_tensor_tensor(out=yn[i][:], in0=yv[i][:], scalar=scl, in1=rn[i][:],
                                                                               op0=ALU.mult, op1=ALU.mult),
                          r=[("yv", i), ("rn", i)], w=[("yn", i)])
                dstT = st["DNQT"] if kind == 0 else st["DNKT"]
                for i in range(NG):
                    P.add("pool", lambda e, i=i, dstT=dstT, tok0=tok0: [e.dma_start(out=dstT[i, :, tok0:tok0 + TT], in_=yn[i][:])],
                          r=[("yn", i)], w=[("yn", i, "d")], slot=("yn", i))
                src = [(yn[i], ("yn", i)) for i in range(NG)]
            if kind >= 1:
                for i in range(NG):
                    sr, ks = src[i]

                    def tr(e, i=i, sr=sr):
                        for sb_ in range(4):
                            ins = e.transpose(btr[i][:, sb_ * 128:(sb_ + 1) * 128], sr[:, sb_ * 128:(sb_ + 1) * 128], C(0))
                        return ins
                    P.add("pe", tr, r=[ks, "cst"], w=[("btr", i)])
                for i in range(NG):
                    P.add("act", lambda e, i=i: e.copy(out=tst[i][:], in_=btr[i][:].rearrange("p (a d) -> p a d", a=4)),
                          r=[("btr", i)], w=[("tst", i)])
                dstM = st["DNK"] if kind == 1 else st["DNV"]
                for i in range(NG):
                    P.add("pool", lambda e, i=i, dstM=dstM, tok0=tok0: [e.dma_start(
                        out=dstM[tok0:tok0 + TT, i * 128:(i + 1) * 128].rearrange("(a p) d -> p a d", p=128), in_=tst[i][:])],
                        r=[("tst", i)], w=[("tst", i, "d")], slot=("tst", i))
        P.flush()
        es.close()

    def phase_dn(self, l):
        nc, P, st = self.nc, self.P, self.st
        NT, SEG, PADW = self.NT, self.SEG, self.st["PADW"]
        C, CB = self.C, self.CB
        flag = st["flag"]
        ones_b = st["ones_b"]
        self.dn_stage1(l)
        if self.debug == "dn1":
            return
        es = ExitStack()
        NCH = NT // 128

        def T2(name, shape=(128, 512), dt=F32, n=2):
            return [self.sb(es, name, list(shape), dt) for _ in range(n)]
        kT4, qT4, k4, v4 = T2("kT4"), T2("qT4"), T2("k4"), T2("v4")
        gb = T2("gb", (128, 16))
        eg = T2("eg", (128, 12))
        bege = T2("bege", (128, 4))
        G4, nabs, E, EA, tA, R, Pm, X = T2("G4"), T2("nabs"), T2("E"), T2("EA"), T2("tA"), T2("R", n=4), T2("Pm", n=4), T2("X")
        Vb4, Kbg4, kdec4, u4, nw4, EQ, qk4, vn4, o1s, o4 = (T2("Vb4"), T2("Kbg4"), T2("kdec4"), T2("u4"), T2("nw4"), T2("EQ"),
                                                           T2("qk4"), T2("vn4"), T2("o1s"), T2("o4"))
        S = T2("S")
        m4 = {nm: self.sb(es, "m4" + nm, [128, 512], F32) for nm in ("Ui", "Li", "Us", "Ls", "I", "bd", "o16", "o32", "o64")}
        Ad, Aoff = T2("Ad"), [T2("Ao16"), T2("Ao32"), T2("Ao64")]
        Wt, M1 = T2("Wt"), T2("M1")
        banks = [self.ps(es, "bank", [128, 512]) for _ in range(8)]
        bstate = dict(i=0)

        def nbank():
            i = bstate["i"] % 8
            bstate["i"] += 1
            return banks[i], ("bank", i)

        def v3(t_):
            return t_[:].rearrange("p (h d) -> p h d", h=4)

        def bc(ap4):
            return ap4.unsqueeze(2).broadcast_to([128, 4, 128])

        def m4build(e):
            for nm, ci in (("Ui", 2), ("Li", 3), ("Us", 4), ("Ls", 5), ("I", 0), ("bd", 12), ("o16", 13), ("o32", 14), ("o64", 15)):
                for hh in range(4):
                    ins = e.tensor_copy(m4[nm][:, hh * 128:(hh + 1) * 128], C(ci))
            return ins
        P.add("dve", m4build, r=["cst"], w=["m4"])
        for d in range(2):
            if DN_R != "none":
                P.add("dve", lambda e, d=d: e.tensor_scalar(S[d][:].bitcast(F32R), m4["I"][:], 0.0, None, op0=ALU.mult), r=["m4"], w=[("S", d)])
            else:
                P.add("dve", lambda e, d=d: e.memset(S[d][:], 0.0), w=[("S", d)])

        cut = float(self.debug.split(":")[1]) if (self.debug and self.debug.startswith("dn2:")) else None

        def step(c, d, par):
            rO = (lambda a: a.bitcast(F32R)) if DN_R in ("outer", "all") else (lambda a: a)
            rI = (lambda a: a.bitcast(F32R)) if DN_R == "all" else (lambda a: a)
            wO = rO
            wI = rI
            wX = rO
            Mincl = C(2) if d == 0 else C(3)
            Mrem = C(5) if d == 0 else C(4)
            MA4 = m4["Ls"] if d == 0 else m4["Us"]
            MQ4 = m4["Ui"] if d == 0 else m4["Li"]
            tk = slice(c * 128, (c + 1) * 128)
            K = lambda nm: (nm, par)
            P.add("sp", lambda e: [e.dma_start(out=v3(kT4[par]), in_=st["DNKT"][:, :, tk].rearrange("h p t -> p h t")),
                                   e.dma_start(out=v3(qT4[par]), in_=st["DNQT"][:, :, tk].rearrange("h p t -> p h t")),
                                   e.dma_start(out=k4[par][:], in_=st["DNK"][tk, :]),
                                   e.dma_start(out=v4[par][:], in_=st["DNV"][tk, :]),
                                   e.dma_start(out=gb[par][:], in_=st["GB"][tk, :])],
                  w=[K("ld")], slot=K("ld"), ndma=5)
            g4 = gb[par][:, d * 4:(d + 1) * 4]
            beta4 = gb[par][:, 8 + d * 4:8 + (d + 1) * 4]
            gp, kgp = nbank()

            def mm1(e):
                e.matmul(gp[:, 0:4], lhsT=Mincl, rhs=g4, start=True, stop=True)
                e.matmul(gp[:, 4:8], lhsT=Mrem, rhs=g4, start=True, stop=True)
                return e.matmul(gp[:, 8:12], lhsT=C(1), rhs=g4, start=True, stop=True)
            P.add("pe", mm1, r=[K("ld"), "cst"], w=[kgp])
            P.add("act", lambda e: e.activation(out=eg[par][:], in_=gp[:, 0:12], func=AF.Exp), r=[kgp], w=[K("eg")])
            egc, erem, etot = eg[par][:, 0:4], eg[par][:, 4:8], eg[par][:, 8:12]
            P.add("dve", lambda e: e.tensor_tensor(out=bege[par][:], in0=beta4, in1=egc, op=ALU.mult), r=[K("ld"), K("eg")], w=[K("bege")])
            yield
            if cut is not None and cut <= 1:
                return
            P.add("dve", lambda e: e.tensor_tensor(out=v3(G4[par]), in0=v3(m4["Ui" if d == 0 else "Li"]), in1=bc(g4), op=ALU.mult),
                  r=[K("ld"), "m4"], w=[K("G4")])
            Dp, kDp = nbank()

            def mm2(e):
                for hh in range(4):
                    sl = slice(hh * 128, (hh + 1) * 128)
                    e.matmul(Dp[:, sl], lhsT=G4[par][:, sl], rhs=C(1), start=True, stop=False)
                    ins = e.matmul(Dp[:, sl], lhsT=C(6), rhs=G4[par][:, sl], start=False, stop=True)
                return ins
            P.add("pe", mm2, r=[K("G4"), "cst"], w=[kDp])
            P.add("dve", lambda e: e.tensor_scalar(tA[par][:], Dp[:], 0.0, None, op0=ALU.min), r=[kDp, K("tA")], w=[K("tA")])
            P.add("dve", lambda e: e.scalar_tensor_tensor(out=nabs[par][:], in0=Dp[:], scalar=0.0, in1=tA[par][:], op0=ALU.max, op1=ALU.subtract),
                  r=[kDp, K("tA")], w=[K("nabs")])
            P.add("act", lambda e: e.activation(out=E[par][:], in_=nabs[par][:], func=AF.Exp, scale=-1.0), r=[K("nabs")], w=[K("E")])
            yield
            if cut is not None and cut <= 2:
                return
            kk, kkk = nbank()

            def mm3(e):
                for hh in range(4):
                    sl = slice(hh * 128, (hh + 1) * 128)
                    ins = e.matmul(kk[:, sl], lhsT=rO(kT4[par][:, sl]), rhs=rO(kT4[par][:, sl]), start=True, stop=True)
                return ins
            P.add("pe", mm3, r=[K("ld")], w=[kkk])
            P.add("dve", lambda e: e.tensor_tensor(out=EA[par][:], in0=E[par][:], in1=MA4[:], op=ALU.mult), r=[K("E"), "m4"], w=[K("EA")])
            P.add("dve", lambda e: e.tensor_tensor(out=tA[par][:], in0=kk[:], in1=EA[par][:], op=ALU.mult), r=[kkk, K("EA")], w=[K("tA")])
            r0 = R[par * 2]
            P.add("dve", lambda e: e.tensor_tensor(out=wI(v3(r0)), in0=v3(tA[par]), in1=bc(beta4), op=ALU.mult),
                  r=[K("tA"), K("ld")], w=[("R", par * 2)])
            yield
            if cut is not None and cut <= 3:
                return
            P.add("dve", lambda e: e.tensor_tensor(out=wI(Ad[par][:]), in0=r0[:], in1=m4["bd"][:], op=ALU.mult),
                  r=[("R", par * 2), "m4"], w=[K("Ad")])
            for li, nm in enumerate(("o16", "o32", "o64")):
                P.add("dve", lambda e, li=li, nm=nm: e.tensor_tensor(out=wI(Aoff[li][par][:]), in0=r0[:], in1=m4[nm][:], op=ALU.mult),
                      r=[("R", par * 2), "m4"], w=[K("Ao%d" % li)])
            Bp, kBp = nbank()

            def mm4(e):
                for hh in range(4):
                    sl = slice(hh * 128, (hh + 1) * 128)
                    ins = e.transpose(Bp[:, sl], Ad[par][:, sl], C(0))
                return ins
            P.add("pe", mm4, r=[K("Ad"), "cst"], w=[kBp])
            p0 = Pm[par * 2]
            P.add("act", lambda e: e.copy(out=wI(p0[:]), in_=Bp[:]), r=[kBp], w=[("Pm", par * 2)])
            P.add("dve", lambda e: e.scalar_tensor_tensor(out=wX(X[par][:]), in0=p0[:], scalar=-1.0, in1=m4["I"][:], op0=ALU.mult, op1=ALU.add),
                  r=[("Pm", par * 2), "m4"], w=[K("X")])
            yield
            if cut is not None and cut <= 3.2:
                return
            Rb = [(Ad[par], K("Ad")), (R[par * 2 + 1], ("R", par * 2 + 1)), (R[par * 2], ("R", par * 2)), (R[par * 2 + 1], ("R", par * 2 + 1))]
            Pb = [(Pm[par * 2], ("Pm", par * 2)), (Pm[par * 2 + 1], ("Pm", par * 2 + 1)), (Pm[par * 2], ("Pm", par * 2))]
            NLEV = 3
            for lev in range(1, NLEV + 1):
                (Rc, kRc), (Pc, kPc) = Rb[lev - 1], Pb[lev - 1]
                Rn, kRn = Rb[lev]
                Pp, kPp = nbank()
                Rp, kRp = nbank()

                def mm5(e, Rc=Rc, Pc=Pc, Pp=Pp, Rp=Rp, lev=lev):
                    for hh in range(4):
                        sl = slice(hh * 128, (hh + 1) * 128)
                        if lev < NLEV:
                            e.matmul(Pp[:, sl], lhsT=rI(Rc[:, sl]), rhs=rI(Pc[:, sl]), start=True, stop=True)
                        ins = e.matmul(Rp[:, sl], lhsT=rI(Pc[:, sl]), rhs=rI(Rc[:, sl]), start=True, stop=True)
                    return ins
                P.add("pe", mm5, r=[kRc, kPc], w=([kPp, kRp] if lev < NLEV else [kRp]))
                if lev < NLEV:
                    Pn, kPn = Pb[lev]
                    P.add("act", lambda e, Pn=Pn, Pp=Pp: e.copy(out=wI(Pn[:]), in_=Pp[:]), r=[kPp], w=[kPn])
                P.add("dve", lambda e, Rn=Rn, Rp=Rp: e.tensor_copy(wI(Rn[:]), Rp[:]), r=[kRp], w=[kRn])
                Xp, kXp = nbank()

                def mm6(e, Rn=Rn, Xp=Xp):
                    for hh in range(4):
                        sl = slice(hh * 128, (hh + 1) * 128)
                        ins = e.matmul(Xp[:, sl], lhsT=rI(Rn[:, sl]), rhs=rI(X[par][:, sl]), start=True, stop=True)
                    return ins
                P.add("pe", mm6, r=[kRn, K("X")], w=[kXp])
                P.add("dve", lambda e, Xp=Xp: e.tensor_tensor(out=wX(X[par][:]), in0=X[par][:], in1=Xp[:], op=ALU.add),
                      r=[kXp, K("X")], w=[K("X")])
                yield
                if cut is not None and cut <= 3.2 + 0.2 * lev:
                    return
            yield
            if cut is not None and cut <= 4:
                return
            for li in range(3):
                Wp, kWp = nbank()

                def mmw(e, Wp=Wp):
                    for hh in range(4):
                        sl = slice(hh * 128, (hh + 1) * 128)
                        ins = e.transpose(Wp[:, sl], X[par][:, sl], C(0))
                    return ins
                P.add("pe", mmw, r=[K("X"), "cst"], w=[kWp])
                P.add("act", lambda e, Wp=Wp: e.copy(out=wI(Wt[par][:]), in_=Wp[:]), r=[kWp], w=[K("Wt")])
                M1p, kM1p = nbank()

                def mmm(e, M1p=M1p, li=li):
                    for hh in range(4):
                        sl = slice(hh * 128, (hh + 1) * 128)
                        ins = e.matmul(M1p[:, sl], lhsT=rI(Aoff[li][par][:, sl]), rhs=rI(X[par][:, sl]), start=True, stop=True)
                    return ins
                P.add("pe", mmm, r=[K("Ao%d" % li), K("X")], w=[kM1p])
                P.add("act", lambda e, M1p=M1p: e.copy(out=wI(M1[par][:]), in_=M1p[:]), r=[kM1p], w=[K("M1")])
                X2p, kX2p = nbank()

                def mmx(e, X2p=X2p):
                    for hh in range(4):
                        sl = slice(hh * 128, (hh + 1) * 128)
                        ins = e.matmul(X2p[:, sl], lhsT=rI(Wt[par][:, sl]), rhs=rI(M1[par][:, sl]), start=True, stop=True)
                    return ins
                P.add("pe", mmx, r=[K("Wt"), K("M1")], w=[kX2p])
                P.add("dve", lambda e, X2p=X2p: e.tensor_tensor(out=wX(X[par][:]), in0=X[par][:], in1=X2p[:], op=ALU.subtract),
                      r=[kX2p, K("X")], w=[K("X")])
                yield
            yield
            if cut is not None and cut <= 5:
                return
            P.add("dve", lambda e: e.tensor_tensor(out=wO(v3(Vb4[par])), in0=v3(v4[par]), in1=bc(beta4), op=ALU.mult), r=[K("ld")], w=[K("Vb4")])
            P.add("dve", lambda e: e.tensor_tensor(out=wO(v3(Kbg4[par])), in0=v3(k4[par]), in1=bc(bege[par][:]), op=ALU.mult),
                  r=[K("ld"), K("bege")], w=[K("Kbg4")])
            P.add("dve", lambda e: e.tensor_tensor(out=wO(v3(kdec4[par])), in0=v3(k4[par]), in1=bc(erem), op=ALU.mult),
                  r=[K("ld"), K("eg")], w=[K("kdec4")])
            up, kup = nbank()
            wp, kwp = nbank()

            def mm7(e):
                for hh in range(4):
                    sl = slice(hh * 128, (hh + 1) * 128)
                    e.matmul(up[:, sl], lhsT=rO(X[par][:, sl]), rhs=rO(Vb4[par][:, sl]), start=True, stop=True)
                    ins = e.matmul(wp[:, sl], lhsT=rO(Kbg4[par][:, sl]), rhs=rO(X[par][:, sl]), start=True, stop=True)
                return ins
            P.add("pe", mm7, r=[K("X"), K("Vb4"), K("Kbg4")], w=[kup, kwp])
            P.add("act", lambda e: e.copy(out=u4[par][:], in_=up[:]), r=[kup], w=[K("u4")])
            P.add("act", lambda e: e.mul(out=wO(nw4[par][:]), in_=wp[:], mul=-1.0), r=[kwp], w=[K("nw4")])
            yield
            if cut is not None and cut <= 6:
                return
            qkp, kqkp = nbank()

            def mm8(e):
                for hh in range(4):
                    sl = slice(hh * 128, (hh + 1) * 128)
                    ins = e.matmul(qkp[:, sl], lhsT=rO(kT4[par][:, sl]), rhs=rO(qT4[par][:, sl]), start=True, stop=True)
                return ins
            P.add("pe", mm8, r=[K("ld")], w=[kqkp])
            P.add("dve", lambda e: e.tensor_tensor(out=EQ[par][:], in0=E[par][:], in1=MQ4[:], op=ALU.mult), r=[K("E"), "m4"], w=[K("EQ")])
            P.add("dve", lambda e: e.tensor_tensor(out=wO(qk4[par][:]), in0=qkp[:], in1=EQ[par][:], op=ALU.mult), r=[kqkp, K("EQ")], w=[K("qk4")])
            yield
            if cut is not None and cut <= 7:
                return
            Sd, kS = S[d], ("S", d)
            link = (c == NCH // 2) if d == 0 else (c == NCH // 2 - 1)
            if link:
                P.add("dve", lambda e: e.tensor_scalar(wO(Sd[:]), Sd[:], flag[:, 0:1], None, op0=ALU.mult), r=[kS, "flag"], w=[kS])
            vnp, kvnp = nbank()
            O1p, kO1p = nbank()

            def mm9(e):
                for hh in range(4):
                    sl = slice(hh * 128, (hh + 1) * 128)
                    e.matmul(vnp[:, sl], lhsT=rO(nw4[par][:, sl]), rhs=rO(Sd[:, sl]), start=True, stop=True)
                    ins = e.matmul(O1p[:, sl], lhsT=rO(qT4[par][:, sl]), rhs=rO(Sd[:, sl]), start=True, stop=True)
                return ins
            P.add("pe", mm9, r=[K("nw4"), K("ld"), kS], w=[kvnp, kO1p])
            P.add("dve", lambda e: e.tensor_tensor(out=wO(vn4[par][:]), in0=vnp[:], in1=u4[par][:], op=ALU.add), r=[kvnp, K("u4")], w=[K("vn4")])
            P.add("dve", lambda e: e.tensor_tensor(out=v3(o1s[par]), in0=O1p[:].rearrange("p (h d) -> p h d", h=4), in1=bc(egc), op=ALU.mult),
                  r=[kO1p, K("eg")], w=[K("o1s")])
            yield
            O2p, kO2p = nbank()
            dSp, kdSp = nbank()

            def mm10(e):
                for hh in range(4):
                    sl = slice(hh * 128, (hh + 1) * 128)
                    e.matmul(O2p[:, sl], lhsT=rO(qk4[par][:, sl]), rhs=rO(vn4[par][:, sl]), start=True, stop=True)
                    ins = e.matmul(dSp[:, sl], lhsT=rO(kdec4[par][:, sl]), rhs=rO(vn4[par][:, sl]), start=True, stop=True)
                return ins
            P.add("pe", mm10, r=[K("qk4"), K("vn4"), K("kdec4")], w=[kO2p, kdSp])
            P.add("dve", lambda e: e.tensor_tensor(out=o4[par][:], in0=O2p[:], in1=o1s[par][:], op=ALU.add), r=[kO2p, K("o1s")], w=[K("o4")])
            P.add("pool", lambda e: [e.dma_start(out=st["OFB"][d, tk, :], in_=o4[par][:])], r=[K("o4")], w=[K("o4d")], slot=K("o4"))
            P.add("dve", lambda e: e.tensor_tensor(out=wO(v3(Sd)), in0=v3(Sd), in1=bc(etot), op=ALU.mult), r=[kS, K("eg")], w=[kS])
            P.add("dve", lambda e: e.tensor_tensor(out=wO(Sd[:]), in0=Sd[:], in1=dSp[:], op=ALU.add), r=[kS, kdSp], w=[kS])

        for sidx in range(NCH if cut is None else 1):
            gens = [step(sidx, 0, 0)] + ([step(NCH - 1 - sidx, 1, 1)] if cut is None else [])
            while gens:
                for g in list(gens):
                    try:
                        next(g)
                    except StopIteration:
                        gens.remove(g)
        P.flush()
        es.close()

        if self.debug and self.debug.startswith("dn2"):
            return
        es = ExitStack()
        of_ = [self.sb(es, "of", [128, 512], F32) for _ in range(2)]
        ob_ = [self.sb(es, "ob", [128, 512], F32) for _ in range(2)]
        zs_ = [self.sb(es, "zs", [128, 512], F32) for _ in range(2)]
        osum = self.sb(es, "osum", [128, 512], F32)
        osq = self.sb(es, "osq", [128, 512], F32)
        ss = self.sb(es, "ss", [128, 4], F32)
        rs = self.sb(es, "rs", [128, 4], F32)
        y1 = self.sb(es, "y1", [128, 512], F32)
        y2 = self.sb(es, "y2", [128, 512], F32)
        y3 = self.sb(es, "y3", [128, 512], F32)
        mst = [self.sb(es, "mst", [128, 4, 128], BF16) for _ in range(2)]
        banks = [self.ps(es, "bank", [128, 512]) for _ in range(2)]
        dnw = st["dnw"]

        def v3b(t_):
            return t_[:].rearrange("p (h d) -> p h d", h=4)
        for t in range(NT // 128):
            par = t % 2
            tk = slice(t * 128, (t + 1) * 128)
            kl = ("s3ld", par)
            P.add("sp", lambda e, par=par, tk=tk: [e.dma_start(out=of_[par][:], in_=st["OFB"][0, tk, :]),
                                                  e.dma_start(out=ob_[par][:], in_=st["OFB"][1, tk, :]),
                                                  e.dma_start(out=zs_[par][:], in_=st["ZS"][tk, :])],
                  w=[kl], slot=kl, ndma=3)
            P.add("dve", lambda e, par=par: e.tensor_tensor(out=osum[:], in0=of_[par][:], in1=ob_[par][:], op=ALU.add), r=[kl], w=["osum"])
            P.add("dve", lambda e: e.tensor_tensor(out=osq[:], in0=osum[:], in1=osum[:], op=ALU.mult), r=["osum"], w=["osq"])
            P.add("dve", lambda e: e.reduce_sum(out=ss[:], in_=v3b(osq), axis=AX.X), r=["osq"], w=["ss"])
            P.add("dve", lambda e: e.tensor_scalar(rs[:], ss[:], 1.0 / 128.0, float(EPS), op0=ALU.mult, op1=ALU.add), r=["ss"], w=["rs0"])
            P.add("act", lambda e: e.activation(out=ss[:], in_=rs[:], func=AF.Sqrt), r=["rs0", "ss"], w=["ss"])
            P.add("dve", lambda e: e.reciprocal(rs[:], ss[:]), r=["ss", "rs0"], w=["rs"])
            P.add("dve", lambda e: e.tensor_tensor(out=v3b(y1), in0=v3b(osum), in1=rs[:].unsqueeze(2).broadcast_to([128, 4, 128]), op=ALU.mult),
                  r=["osum", "rs"], w=["y1"])
            P.add("dve", lambda e: e.tensor_tensor(out=y2[:], in0=y1[:], in1=dnw[:, l, :], op=ALU.mult), r=["y1", "coef"], w=["y2"])
            P.add("dve", lambda e, par=par: e.tensor_tensor(out=y3[:], in0=y2[:], in1=zs_[par][:], op=ALU.mult), r=["y2", kl], w=["y3"])
            bk, kbk = banks[par], ("bank", par)

            def tr(e, bk=bk):
                for hh in range(4):
                    ins = e.transpose(bk[:, hh * 128:(hh + 1) * 128], y3[:, hh * 128:(hh + 1) * 128], C(0))
                return ins
            P.add("pe", tr, r=["y3", "cst"], w=[kbk])
            km = ("mst", par)
            P.add("act", lambda e, bk=bk, par=par: e.copy(out=mst[par][:], in_=bk[:].rearrange("p (h d) -> p h d", h=4)), r=[kbk], w=[km])
            P.add("pool", lambda e, par=par, tk=tk: [e.dma_start(out=st["MIXT"][4:8, :, tk].rearrange("h p t -> p h t"), in_=mst[par][:])],
                  r=[km], w=[km + ("d",)], slot=km)
        P.flush()
        es.close()


def make_consts():
    k = np.arange(128)[:, None]
    m = np.arange(128)[None, :]
    t = np.zeros((16, 128, 128), np.float32)
    t[0] = (k == m)
    t[1] = 1.0
    t[2] = (k <= m)
    t[3] = (k >= m)
    t[4] = (k < m)
    t[5] = (k > m)
    t[6] = -1.0
    t[7] = (np.abs(k - m) <= 64)
    t[8] = (k >= m + 64)
    t[9] = (k <= m - 64)
    t[10] = (m < 64) * np.ones((128, 1))
    t[11] = (m >= 64) * np.ones((128, 1))
    t[12] = (k // 16 == m // 16)
    t[13] = (k // 32 == m // 32) & (k // 16 != m // 16)
    t[14] = (k // 64 == m // 64) & (k // 32 != m // 32)
    t[15] = (k // 64 != m // 64)
    return np.ascontiguousarray(t.transpose(1, 0, 2).reshape(128, 16 * 128)).astype(np.float32)


def make_rope(pos):
    half = 8
    inv = (np.float32(ROPE_THETA) ** (-(np.arange(half, dtype=np.float32) / np.float32(half)))).astype(np.float32)
    ang = pos.astype(np.float32)[None, :] * inv[:, None]
    cos = np.cos(ang).astype(np.float32)
    sin = np.sin(ang).astype(np.float32)
    NT = pos.shape[0]
    cosT = np.ones((128, NT), np.float32)
    sinT = np.zeros((128, NT), np.float32)
    for e in range(2):
        cosT[e * 64:e * 64 + 8] = cos
        cosT[e * 64 + 8:e * 64 + 16] = cos
        sinT[e * 64:e * 64 + 8] = -sin
        sinT[e * 64 + 8:e * 64 + 16] = sin
    return cosT, sinT


_WNAMES = ["ada_w", "ada_b", "norm_ffn1", "ffn1_w_gate", "ffn1_w_up", "ffn1_w_down", "norm_mix", "w_in", "conv_w",
           "a_log", "dt_bias", "dn_norm", "w_out", "norm_ffn2", "ffn2_w_gate", "ffn2_w_up", "ffn2_w_down", "norm_final"]


def core_inputs(xc, c2, cont, weights, depth):
    NT = xc.shape[0]
    seg = NT // 2
    pos = np.arange(NT) if cont else (np.arange(NT) % seg)
    cosT, sinT = make_rope(pos)
    m = {"x": np.ascontiguousarray(xc, dtype=np.float32), "c2": np.ascontiguousarray(c2, dtype=np.float32),
         "flag": np.full((128, 1), 1.0 if cont else 0.0, np.float32), "cosT": cosT, "sinT": sinT,
         "consts": make_consts()}
    for n in _WNAMES:
        w = np.asarray(weights[n], dtype=np.float32)
        if n in ("a_log", "dt_bias"):
            w = w.reshape(depth, 8)
        if n == "norm_final":
            w = w.reshape(1, D)
        m[n] = np.ascontiguousarray(w)
    return m


_NC_CACHE = {}


def kernel(**inputs):
    xp = np.asarray(inputs["x_prompt"], dtype=np.float32)
    xs = np.asarray(inputs["x_sample"], dtype=np.float32)
    cp = np.asarray(inputs["c_prompt"], dtype=np.float32)
    cs = np.asarray(inputs["c_sample"], dtype=np.float32)
    depth = np.asarray(inputs["ada_w"]).shape[0]
    Bp, Sp, _ = xp.shape
    Bs, Ss, _ = xs.shape
    seg = Sp
    assert Ss == 2 * Sp and Bp == 2 * Bs and Bp + Bs * 2 == 16 or True
    in_maps = []
    npc = Bp // 2
    for i in range(npc):
        xc = xp[2 * i:2 * i + 2].reshape(2 * Sp, D)
        in_maps.append(core_inputs(xc, cp[2 * i:2 * i + 2], False, inputs, depth))
    for i in range(Bs):
        xc = xs[i]
        in_maps.append(core_inputs(xc, np.stack([cs[i], cs[i]]), True, inputs, depth))
    key = (seg, depth)
    if key not in _NC_CACHE:
        _NC_CACHE[key] = Builder(seg, depth).build()
    nc = _NC_CACHE[key]
    res = run_bass_kernel_spmd(nc, in_maps, core_ids=list(range(len(in_maps))))
    ys = [r["y"] for r in res.results]
    y_prompt = np.stack(ys[:npc]).reshape(Bp, Sp, D).astype(np.float32)
    y_sample = np.stack(ys[npc:]).reshape(Bs, Ss, D).astype(np.float32)
    return (y_prompt, y_sample)
```
